# Optimizing a Trainium2 kernel written in Bass

```python
import math
import jax, jax.numpy as jnp
from jax import lax
import numpy as np

D_MODEL = 2048
BATCH = 16
SEQ = 256
DEPTH = 2
DEC_BATCH = 2
DEC_SEQ = 4096
PAST_LEN = 512

GRID_W = 64
GDN_HEADS = 4
GDN_DK = 128
GDN_DV = 128
GDN_W = GDN_HEADS * GDN_DV
CONV_K = 4
CHUNK = 64
DIFF_HEADS = 8
DIFF_DQK = 64
DIFF_DV = 2 * DIFF_DQK
DIFF_QK_W = DIFF_HEADS * 2 * DIFF_DQK
DIFF_W = DIFF_HEADS * DIFF_DV
Q_BLOCK = 128
ROPE_BASE = 10000.0
POOL_WINDOWS = (2, 4, 8, 16)
POOL_GROUPS = len(POOL_WINDOWS)
POOL_GW = 128
POOL_W = POOL_GROUPS * POOL_GW
MIX_W = GDN_W + DIFF_W + POOL_W
D_FF = 5632
N_MOD = 9
EPS = 1e-6
IN_SPLIT_SIZES = (3 * GDN_W, GDN_W, 2 * GDN_HEADS, 2 * GDN_HEADS, DIFF_QK_W, DIFF_QK_W, DIFF_W, POOL_W)
IN_COLS = sum(IN_SPLIT_SIZES)
IN_SPLITS = tuple(int(s) for s in np.cumsum(IN_SPLIT_SIZES)[:-1])

kernel_name = 'hybrid_gdn_diffattn_pool_macaron_prefix_step'


def lambda_init(l):
    return 0.8 - 0.6 * math.exp(-0.3 * l)


def rmsnorm(x, g):
    xf = x.astype(jnp.float32)
    y = xf * lax.rsqrt(jnp.mean(xf * xf, axis=-1, keepdims=True) + EPS)
    return (y * g.astype(jnp.float32)).astype(x.dtype)


def l2norm(x):
    xf = x.astype(jnp.float32)
    return (xf * lax.rsqrt(jnp.sum(xf * xf, axis=-1, keepdims=True) + EPS)).astype(x.dtype)


def modulate(x, shift, scale):
    return x * (1.0 + scale) + shift


def swiglu(h, w_in, w_out):
    gate, up = jnp.split(h @ w_in, 2, axis=-1)
    return (jax.nn.silu(gate) * up) @ w_out


def short_conv(x, w):
    L = x.shape[1]
    left = CONV_K // 2
    xp = jnp.pad(x, ((0, 0), (left, CONV_K - 1 - left), (0, 0)))
    return sum(xp[:, j:j + L] * w[j] for j in range(CONV_K))


def axial_rope(x):
    L = x.shape[1]
    rows = L // GRID_W
    row = jnp.broadcast_to(jnp.arange(rows)[:, None], (rows, GRID_W)).reshape(L).astype(jnp.float32)
    col = jnp.broadcast_to(jnp.arange(GRID_W)[None, :], (rows, GRID_W)).reshape(L).astype(jnp.float32)
    half = DIFF_DQK // 2
    inv_freq = ROPE_BASE ** (-jnp.arange(0, half, 2, dtype=jnp.float32) / half)

    def rot(xa, pos):
        ang = pos[:, None] * inv_freq[None, :]
        cos = jnp.cos(ang)[None, :, None, None, :]
        sin = jnp.sin(ang)[None, :, None, None, :]
        x1, x2 = jnp.split(xa.astype(jnp.float32), 2, axis=-1)
        return jnp.concatenate([x1 * cos - x2 * sin, x1 * sin + x2 * cos], axis=-1)

    out = jnp.concatenate([rot(x[..., :half], row), rot(x[..., half:], col)], axis=-1)
    return out.astype(x.dtype)


def gdn_chunked(q, k, v, g, beta, s0):
    B, L, H, _ = q.shape
    n = L // CHUNK
    f32 = jnp.float32

    def to_chunks(t):
        t = t.astype(f32).reshape((B, n, CHUNK, H) + t.shape[3:])
        return jnp.moveaxis(t, 3, 2)

    qc, kc, vc = to_chunks(q), to_chunks(k), to_chunks(v)
    gc = jnp.cumsum(to_chunks(g), axis=-1)
    bc = to_chunks(beta)
    kb = kc * bc[..., None]
    vb = vc * bc[..., None]
    tri = jnp.tril(jnp.ones((CHUNK, CHUNK), bool))
    strict = jnp.tril(jnp.ones((CHUNK, CHUNK), bool), k=-1)
    decay = jnp.exp(jnp.where(tri, gc[..., :, None] - gc[..., None, :], -jnp.inf))
    lmat = jnp.where(strict, jnp.einsum('bnhid,bnhjd->bnhij', kb, kc) * decay, 0.0)
    eye = jnp.eye(CHUNK, dtype=f32)
    tmat = lax.linalg.triangular_solve(eye + lmat, jnp.broadcast_to(eye, lmat.shape),
                                       left_side=True, lower=True, unit_diagonal=True)
    u = jnp.einsum('bnhij,bnhjv->bnhiv', tmat, vb)
    w = jnp.einsum('bnhij,bnhjk->bnhik', tmat, kb * jnp.exp(gc)[..., None])
    a_intra = jnp.einsum('bnhid,bnhjd->bnhij', qc, kc) * decay
    qg = qc * jnp.exp(gc)[..., None]
    kd = kc * jnp.exp(gc[..., -1:] - gc)[..., None]
    glast = jnp.exp(gc[..., -1])

    def step(S, xs):
        u_i, w_i, qg_i, a_i, kd_i, gl_i = xs
        v_new = u_i - jnp.einsum('bhck,bhkv->bhcv', w_i, S)
        o = jnp.einsum('bhck,bhkv->bhcv', qg_i, S) + jnp.einsum('bhij,bhjv->bhiv', a_i, v_new)
        S = S * gl_i[..., None, None] + jnp.einsum('bhck,bhcv->bhkv', kd_i, v_new)
        return S, o

    xs = tuple(jnp.moveaxis(t, 1, 0) for t in (u, w, qg, a_intra, kd, glast))
    s_fin, o = lax.scan(step, s0.astype(f32), xs)
    o = o.transpose(1, 0, 3, 2, 4).reshape(B, L, H, v.shape[-1])
    return o.astype(v.dtype), s_fin


def diff_attention(q, k, v, lam):
    B, Lq, H = q.shape[:3]
    nb = Lq // Q_BLOCK
    qb = jnp.moveaxis(q.reshape(B, nb, Q_BLOCK, H, 2, DIFF_DQK), 1, 0)
    scale = DIFF_DQK ** -0.5

    def block(qi):
        s = jnp.einsum('bqhmd,bkhmd->bmhqk', qi, k, preferred_element_type=jnp.float32) * scale
        p = jax.nn.softmax(s, axis=-1)
        pd = p[:, 0] - lam * p[:, 1]
        return jnp.einsum('bhqk,bkhv->bqhv', pd.astype(v.dtype), v)

    o = lax.map(block, qb)
    return jnp.moveaxis(o, 0, 1).reshape(B, Lq, H, v.shape[-1])


def centred_mean(x, w):
    B, L, C = x.shape
    a = w // 2
    b = w - a - 1
    cs = jnp.concatenate([jnp.zeros((B, 1, C), jnp.float32), jnp.cumsum(x.astype(jnp.float32), axis=1)], axis=1)
    t = jnp.arange(L)
    lo = jnp.clip(t - a, 0, L)
    hi = jnp.clip(t + b + 1, 0, L)
    cnt = (hi - lo).astype(jnp.float32)
    return ((cs[:, hi] - cs[:, lo]) / cnt[None, :, None]).astype(x.dtype)


def token_mixing(h, lp, l, ctx):
    B, L, _ = h.shape
    proj = h @ lp['w_in']
    qkv, z, a, b, dq, dk, dv, pin = jnp.split(proj, IN_SPLITS, axis=-1)

    qkv = jax.nn.silu(short_conv(qkv, lp['gdn_conv']))
    gq, gk, gv = jnp.split(qkv, 3, axis=-1)
    gq = l2norm(gq.reshape(B, L, GDN_HEADS, GDN_DK)) * (GDN_DK ** -0.5)
    gk = l2norm(gk.reshape(B, L, GDN_HEADS, GDN_DK))
    gv = gv.reshape(B, L, GDN_HEADS, GDN_DV)
    a = a.reshape(B, L, 2, GDN_HEADS).astype(jnp.float32)
    g = -jnp.exp(lp['gdn_a_log'].astype(jnp.float32)) * jax.nn.softplus(a + lp['gdn_dt_bias'].astype(jnp.float32))
    beta = jax.nn.sigmoid(b.reshape(B, L, 2, GDN_HEADS).astype(jnp.float32))
    if ctx is None:
        s0 = jnp.zeros((B, 2, GDN_HEADS, GDN_DK, GDN_DV), jnp.float32)
    else:
        s0 = ctx[2]
    flip = lambda t: jnp.flip(t, axis=1)
    o_f, s_f = gdn_chunked(gq, gk, gv, g[:, :, 0], beta[:, :, 0], s0[:, 0])
    o_b, s_b = gdn_chunked(flip(gq), flip(gk), flip(gv), flip(g[:, :, 1]), flip(beta[:, :, 1]), s0[:, 1])
    o_gdn = rmsnorm(o_f + flip(o_b), lp['gdn_norm']) * jax.nn.silu(z.reshape(B, L, GDN_HEADS, GDN_DV))

    dq = dq.reshape(B, L, DIFF_HEADS, 2, DIFF_DQK)
    dk = dk.reshape(B, L, DIFF_HEADS, 2, DIFF_DQK)
    dv = dv.reshape(B, L, DIFF_HEADS, DIFF_DV)
    if ctx is None:
        keys, vals, queries = dk, dv, dq
    else:
        ck = ctx[0].reshape(B, -1, DIFF_HEADS, 2, DIFF_DQK)
        keys = jnp.concatenate([ck, axial_rope(dk)], axis=1)
        vals = jnp.concatenate([ctx[1], dv], axis=1)
        queries = axial_rope(dq)
    lam_vec = lp['diff_lam'].astype(jnp.float32)
    lam_init = lambda_init(l)
    lam = jnp.exp(jnp.sum(lam_vec[0] * lam_vec[1])) - jnp.exp(jnp.sum(lam_vec[2] * lam_vec[3])) + lam_init
    o_diff = rmsnorm(diff_attention(queries, keys, vals, lam), lp['diff_norm']) * (1.0 - lam_init)

    pg = pin.reshape(B, L, POOL_GROUPS, POOL_GW)
    pooled = jnp.stack([centred_mean(pg[:, :, i], w) - pg[:, :, i] for i, w in enumerate(POOL_WINDOWS)], axis=2)
    o_pool = jnp.einsum('blgc,gcd->blgd', pooled, lp['pool_w']).reshape(B, L, POOL_W) * lp['pool_scale']

    cat = jnp.concatenate([o_gdn.reshape(B, L, GDN_W), o_diff.reshape(B, L, DIFF_W), o_pool], axis=-1)
    out = cat @ lp['w_out']
    if ctx is None:
        ctx_out = (dk.reshape(B, L, DIFF_HEADS, 2 * DIFF_DQK), dv, jnp.stack([s_f, s_b], axis=1).astype(h.dtype))
    else:
        ctx_out = None
    return out, ctx_out


def trunk_layer(x, cond, lp, l, ctx):
    ada = jax.nn.silu(cond) @ lp['w_ada'] + lp['b_ada']
    sh1, sc1, g1, shm, scm, gm, sh2, sc2, g2 = [t[:, None, :] for t in jnp.split(ada, N_MOD, axis=-1)]
    h = modulate(rmsnorm(x, lp['norm_ffn1']), sh1, sc1)
    x = x + 0.5 * g1 * swiglu(h, lp['ffn1_in'], lp['ffn1_out'])
    h = modulate(rmsnorm(x, lp['norm_mix']), shm, scm)
    mixed, ctx_out = token_mixing(h, lp, l, ctx)
    x = x + gm * mixed
    h = modulate(rmsnorm(x, lp['norm_ffn2']), sh2, sc2)
    x = x + 0.5 * g2 * swiglu(h, lp['ffn2_in'], lp['ffn2_out'])
    return x, ctx_out


def setup_inputs(seed: int = 0) -> dict:
    key = jax.random.key(seed)
    ks = jax.random.split(key, 32)
    f32 = jnp.float32
    nrm = lambda k, shape, s: jax.random.normal(k, shape, f32) * s
    gain = lambda k, shape: 1.0 + 0.02 * jax.random.normal(k, shape, f32)
    dt = jnp.exp(jax.random.uniform(ks[15], (DEPTH, 2, GDN_HEADS), f32, math.log(1e-3), math.log(1e-1)))
    return {
        'x_prompt': nrm(ks[0], (BATCH, SEQ, D_MODEL), 1.0),
        'x_sample': nrm(ks[1], (DEC_BATCH, DEC_SEQ, D_MODEL), 1.0),
        'c': nrm(ks[2], (DEC_BATCH, D_MODEL), 1.0),
        'cache_k': nrm(ks[3], (DEC_BATCH, DEPTH, PAST_LEN, DIFF_HEADS, 2 * DIFF_DQK), 1.0),
        'cache_v': nrm(ks[4], (DEC_BATCH, DEPTH, PAST_LEN, DIFF_HEADS, DIFF_DV), 1.0),
        'state_gdn': nrm(ks[5], (DEC_BATCH, DEPTH, 2, GDN_HEADS, GDN_DK, GDN_DV), 0.3),
        'c_ctx': nrm(ks[6], (D_MODEL,), 1.0),
        'w_ada': nrm(ks[7], (DEPTH, D_MODEL, N_MOD * D_MODEL), 0.3 * D_MODEL ** -0.5),
        'b_ada': nrm(ks[8], (DEPTH, N_MOD * D_MODEL), 0.02),
        'norm_ffn1': gain(ks[9], (DEPTH, D_MODEL)),
        'ffn1_in': nrm(ks[10], (DEPTH, D_MODEL, 2 * D_FF), D_MODEL ** -0.5),
        'ffn1_out': nrm(ks[11], (DEPTH, D_FF, D_MODEL), D_FF ** -0.5),
        'norm_mix': gain(ks[12], (DEPTH, D_MODEL)),
        'w_in': nrm(ks[13], (DEPTH, D_MODEL, IN_COLS), D_MODEL ** -0.5),
        'gdn_conv': nrm(ks[14], (DEPTH, CONV_K, 3 * GDN_W), CONV_K ** -0.5),
        'gdn_a_log': jnp.log(jax.random.uniform(ks[16], (DEPTH, 2, GDN_HEADS), f32, 1.0, 16.0)),
        'gdn_dt_bias': dt + jnp.log(-jnp.expm1(-dt)),
        'gdn_norm': gain(ks[17], (DEPTH, GDN_DV)),
        'diff_lam': nrm(ks[18], (DEPTH, 4, DIFF_DQK), 0.1),
        'diff_norm': gain(ks[19], (DEPTH, DIFF_DV)),
        'pool_w': nrm(ks[20], (DEPTH, POOL_GROUPS, POOL_GW, POOL_GW), POOL_GW ** -0.5),
        'pool_scale': gain(ks[21], (DEPTH, POOL_W)),
        'w_out': nrm(ks[22], (DEPTH, MIX_W, D_MODEL), MIX_W ** -0.5),
        'norm_ffn2': gain(ks[23], (DEPTH, D_MODEL)),
        'ffn2_in': nrm(ks[24], (DEPTH, D_MODEL, 2 * D_FF), D_MODEL ** -0.5),
        'ffn2_out': nrm(ks[25], (DEPTH, D_FF, D_MODEL), D_FF ** -0.5),
        'final_norm': gain(ks[26], (D_MODEL,)),
    }


def reference(x_prompt, x_sample, c, cache_k, cache_v, state_gdn, c_ctx, w_ada, b_ada, norm_ffn1, ffn1_in,
              ffn1_out, norm_mix, w_in, gdn_conv, gdn_a_log, gdn_dt_bias, gdn_norm, diff_lam, diff_norm,
              pool_w, pool_scale, w_out, norm_ffn2, ffn2_in, ffn2_out, final_norm):
    lps = [dict(w_ada=w_ada[l], b_ada=b_ada[l], norm_ffn1=norm_ffn1[l], ffn1_in=ffn1_in[l], ffn1_out=ffn1_out[l],
                norm_mix=norm_mix[l], w_in=w_in[l], gdn_conv=gdn_conv[l], gdn_a_log=gdn_a_log[l],
                gdn_dt_bias=gdn_dt_bias[l], gdn_norm=gdn_norm[l], diff_lam=diff_lam[l], diff_norm=diff_norm[l],
                pool_w=pool_w[l], pool_scale=pool_scale[l], w_out=w_out[l], norm_ffn2=norm_ffn2[l],
                ffn2_in=ffn2_in[l], ffn2_out=ffn2_out[l]) for l in range(DEPTH)]

    yp = x_prompt
    ks_, vs_, ss_ = [], [], []
    for l in range(DEPTH):
        yp, (k_l, v_l, s_l) = trunk_layer(yp, c_ctx[None, :], lps[l], l, None)
        ks_.append(k_l)
        vs_.append(v_l)
        ss_.append(s_l)
    y_prompt = rmsnorm(yp, final_norm)
    new_cache_k = jnp.stack(ks_, axis=1)
    new_cache_v = jnp.stack(vs_, axis=1)
    new_state_gdn = jnp.stack(ss_, axis=1)

    ys = x_sample
    for l in range(DEPTH):
        ys, _ = trunk_layer(ys, c, lps[l], l, (cache_k[:, l], cache_v[:, l], state_gdn[:, l]))
    y_sample = rmsnorm(ys, final_norm)
    return (y_prompt, y_sample, new_cache_k, new_cache_v, new_state_gdn)
```

```python
import numpy as np
from contextlib import ExitStack
import concourse.bass as bass
import concourse.mybir as mybir
from concourse.bass_utils import run_bass_kernel_spmd

F32 = mybir.dt.float32
BF16 = mybir.dt.bfloat16
AF = mybir.ActivationFunctionType
OP = mybir.AluOpType

D = 2048
DEPTH = 2
NPS = 2
LP = 256
LS = 4096
PAST = 512
NT = NPS * LP + LS
TT = 512
NTILE = NT // TT
DFF = 5632
NFC = DFF // 128
INC = 5648
INCP = 45 * 128
KC = 16
EPS = 1e-6
O_QKV, O_Z, O_DQ, O_DK, O_DV, O_PIN, O_A, O_B = 0, 1536, 2048, 3072, 4096, 5120, 5632, 5640
MASKV = 30000.0


class Slot:
    __slots__ = ("w", "r", "dsem", "dcnt", "name")

    def __init__(self, name):
        self.w = None
        self.r = {}
        self.dsem = None
        self.dcnt = 0
        self.name = name


class Eng:
    def __init__(self, name, h):
        self.name = name
        self.h = h
        self.sem = None
        self.n = 0
        self.seen = {}


class Ctx:
    def __init__(self, nc, stack):
        self.nc = nc
        self.stack = stack
        self.semstack = stack
        self.E = {
            "pe": Eng("pe", nc.tensor),
            "act": Eng("act", nc.scalar),
            "dve": Eng("dve", nc.vector),
            "pool": Eng("pool", nc.gpsimd),
            "sp": Eng("sp", nc.sync),
        }
        self.sems = []
        self.ninst = 0
        self.free_dsems = []
        self.scopes = []

    def newsem(self, name):
        h = self.semstack.enter_context(self.nc.semaphore(f"{name}_{len(self.sems)}"))
        self.sems.append(h)
        return len(self.sems) - 1

    def _wait(self, e, deps):
        for si, (val, eng) in deps.items():
            if e.name == "pe" and eng == "pe":
                continue
            if e.seen.get(si, 0) < val:
                e.h.wait_ge(self.sems[si], val)
                e.seen[si] = val

    @staticmethod
    def _add(deps, ev):
        si, val, eng = ev
        if si not in deps or deps[si][0] < val:
            deps[si] = (val, eng)

    def _deps(self, reads, writes):
        deps = {}
        for b in reads:
            if b.w is not None:
                self._add(deps, b.w)
        for b in writes:
            if b.w is not None:
                self._add(deps, b.w)
            for si, (val, eng) in b.r.items():
                self._add(deps, (si, val, eng))
        return deps

    def _record(self, ev, reads, writes):
        si, val, eng = ev
        for b in reads:
            if si not in b.r or b.r[si][0] < val:
                b.r[si] = (val, eng)
        for b in writes:
            b.w = ev
            b.r = {}
        self.ninst += 1

    def op(self, eng, fn, reads=(), writes=()):
        e = self.E[eng]
        if e.sem is None or e.n >= 15000:
            e.sem = self.newsem(eng)
            e.n = 0
        self._wait(e, self._deps(reads, writes))
        inst = fn(e.h)
        e.n += 1
        inst.then_inc(self.sems[e.sem], 1)
        ev = (e.sem, e.n, eng)
        self._record(ev, reads, writes)
        return ev

    def dma(self, q, out, in_, owner, reads=(), writes=(), **kw):
        e = self.E[q]
        if owner.dsem is None:
            if self.free_dsems:
                owner.dsem, owner.dcnt = self.free_dsems.pop()
            else:
                owner.dsem = self.newsem("d")
                owner.dcnt = 0
        deps = self._deps(reads, writes)
        if owner.dcnt > 0:
            self._add(deps, (owner.dsem, 16 * owner.dcnt, "dma"))
        self._wait(e, deps)
        if owner.dcnt >= 900:
            owner.dsem = self.newsem("d")
            owner.dcnt = 0
        inst = e.h.dma_start(out=out, in_=in_, **kw)
        owner.dcnt += 1
        inst.then_inc(self.sems[owner.dsem], 16)
        ev = (owner.dsem, 16 * owner.dcnt, "dma")
        self._record(ev, reads, writes)
        return ev

    def finish(self, slots):
        e = self.E["sp"]
        deps = {}
        for b in slots:
            if b.w is not None:
                self._add(deps, b.w)
            for si, (val, eng) in b.r.items():
                self._add(deps, (si, val, eng))
        self._wait(e, deps)


class T:
    cnt = 0

    def __init__(self, cx, name, shape, dtype, nslots=1, psum=False):
        alloc = cx.nc.psum_tensor if psum else cx.nc.sbuf_tensor
        T.cnt += 1
        self.t = cx.stack.enter_context(alloc(f"{name}_{T.cnt}", list(shape), dtype))
        self.s = [Slot(f"{name}{i}") for i in range(nslots)]
        self.name = name
        if cx.scopes:
            cx.scopes[-1].append(self)

    def __getitem__(self, idx):
        return self.t[idx]


class DR:
    def __init__(self, cx, name, shape, dtype, kind=None, nslots=1):
        if kind is None:
            self.ap = cx.nc.dram_tensor(name, list(shape), dtype).ap()
        else:
            self.ap = cx.nc.dram_tensor(name, list(shape), dtype, kind=kind).ap()
        self.s = [Slot(f"{name}{i}") for i in range(nslots)]


def build_program(dbg=False):
    nc = bass.Bass("TRN2", target_bir_lowering=False)
    stack = ExitStack()
    cx = Ctx(nc, stack)
    with stack:
        _emit(nc, cx, dbg)
    return nc


C_ID, C_ONE, C_U, C_UJ, C_J, C_EL, C_MS, C_MU, C_ML = range(9)
NCONST = 9
SHIFT = 10.0


def _emit(nc, cx, dbg):
    op, dma = cx.op, cx.dma
    IN = lambda name, shape, dt=F32: DR(cx, name, shape, dt, kind="ExternalInput")
    OUT = lambda name, shape, dt=F32: DR(cx, name, shape, dt, kind="ExternalOutput")
    mx = isinstance(dbg, str) and dbg.startswith("MX")
    big = (lambda sh: [1, 1, 1]) if mx else (lambda sh: sh)
    xT = IN("xT", [D, NT])
    cT = IN("cT", [128, KC, 2])
    w_ada = IN("w_ada", big([DEPTH, D, 9 * D]))
    b_ada = IN("b_ada", [DEPTH, 128, 144])
    norms = IN("norms", [128, DEPTH * 3 + 1, KC])
    ffn_in = [IN("ffn1_in", big([DEPTH, D, 2 * DFF])), IN("ffn2_in", big([DEPTH, D, 2 * DFF]))]
    ffn_out = [IN("ffn1_out", big([DEPTH, DFF, D])), IN("ffn2_out", big([DEPTH, DFF, D]))]
    w_in = IN("w_in", big([DEPTH, D, INC]))
    w_out = IN("w_out", big([DEPTH, D, D]))
    consts = IN("consts", [128, NCONST, 128])
    convw = IN("convw", [DEPTH, 128, 12, 4])
    gpar = IN("gpar", [DEPTH, 128, 2, 8])
    gnorm = IN("gnorm", [DEPTH, 128, 2, 128])
    dlam = IN("dlam", [DEPTH, 128, 4, 64])
    poolw = IN("poolw", [DEPTH, 128, 4, 128])
    pscale = IN("pscale", [DEPTH, 128, 4])
    pedge = IN("pedge", [128, 4, 16])
    ropeS = IN("ropeS", [64, 2, LS])
    ropeR = IN("ropeR", [64, 64])
    ckT = IN("ckT", [DEPTH, 8, 2, 64, PAST])
    cvT = IN("cvT", [DEPTH, PAST, 8, 128])
    st_i = IN("st_i", [DEPTH, 2, 4, 128, 128])
    yT = OUT("yT", [D, NT])
    ck_o = OUT("ck_o", [DEPTH, 1024, NPS * LP])
    cv_o = OUT("cv_o", [DEPTH, 1024, NPS * LP])
    st_o = OUT("st_o", [NPS, DEPTH, 2, 4, 128, 128])
    xs = DR(cx, "xs", [D, NT], F32, nslots=NTILE)
    proj = DR(cx, "proj", [INCP, NT], F32, nslots=NTILE, kind=("ExternalInput" if mx else None))
    cat = DR(cx, "cat", [D, NT], F32, nslots=NTILE, kind=("ExternalOutput" if mx else None))
    ofs = DR(cx, "ofs", [LS, 128], F32)
    obs = DR(cx, "obs", [LS, 128], F32)

    ones_b = T(cx, "ones_b", [128, 128], BF16)
    ada = T(cx, "ada", [128, 144, 2], F32)
    nrm = T(cx, "nrm", [128, DEPTH * 3 + 1, KC], F32)
    cond = T(cx, "cond", [128, KC, 2], F32)
    badat = T(cx, "badat", [128, 144], F32)
    modA = T(cx, "modA", [128, 3, KC, 2], F32)
    gate5 = T(cx, "gate5", [128, 3, KC, 2], F32)
    epsb = T(cx, "epsb", [128, 1], F32)
    oneb = T(cx, "oneb", [128, 1], F32)
    shb = T(cx, "shb", [128, 1], F32)
    CN = T(cx, "CN", [128, NCONST, 128], F32)
    PS = [T(cx, f"ps{i}", [128, 512], F32, psum=True) for i in range(8)]
    rot = {"lo": 0, "n": 8, "i": 0}

    def ps():
        rot["i"] = (rot["i"] + 1) % rot["n"]
        return PS[rot["lo"] + rot["i"]]

    def cst(i):
        return CN[:, i, :]

    op("pool", lambda h: h.memset(ones_b[:], 1.0), writes=ones_b.s)
    op("pool", lambda h: h.memset(epsb[:], EPS), writes=epsb.s)
    op("pool", lambda h: h.memset(oneb[:], 1.0), writes=oneb.s)
    op("pool", lambda h: h.memset(shb[:], -SHIFT), writes=shb.s)
    dma("sp", nrm[:], norms.ap[:, :, :], nrm.s[0], writes=nrm.s)
    dma("sp", cond[:], cT.ap[:, :, :], cond.s[0], writes=cond.s)
    dma("sp", CN[:], consts.ap[:, :, :], CN.s[0], writes=CN.s)
    op("act", lambda h: h.activation(out=cond[:], in_=cond[:], func=AF.Silu), reads=cond.s, writes=cond.s)

    class Scope:
        def __enter__(self):
            self.st = ExitStack()
            self.old = cx.stack
            cx.stack = self.st
            self.st.__enter__()
            cx.scopes.append([])
            return self

        def __exit__(self, *a):
            sc_ = cx.scopes.pop()
            deps_ = {}
            for t_ in sc_:
                for sl_ in t_.s:
                    if sl_.w is not None:
                        cx._add(deps_, sl_.w)
                    for si_, (v_, e_) in sl_.r.items():
                        cx._add(deps_, (si_, v_, e_))
            for en_ in cx.E.values():
                for si_, (v_, e_) in deps_.items():
                    if en_.seen.get(si_, 0) < v_:
                        en_.h.wait_ge(cx.sems[si_], v_)
                        en_.seen[si_] = v_
            for t_ in sc_:
                for sl_ in t_.s:
                    if sl_.dsem is not None:
                        cx.free_dsems.append((sl_.dsem, sl_.dcnt))
            cx.stack = self.old
            return self.st.__exit__(*a)

    def ada_layer(l):
        with Scope():
            wst = T(cx, f"wada{l}", [128, 2, KC, 512], F32, nslots=2)
            dma("sp", badat[:], b_ada.ap[l], badat.s[0], writes=badat.s)
            for blk in range(9 * D // 512):
                sl = blk % 2
                src = w_ada.ap[l, :, blk * 512:(blk + 1) * 512].rearrange("(k p) n -> p k n", p=128)
                dma("sp", wst[:, sl, :, :], src, wst.s[sl], writes=[wst.s[sl]])
                for j in range(4):
                    ch = blk * 4 + j
                    p = ps()
                    for k in range(KC):
                        op("pe", lambda h, k=k, j=j, p=p, sl=sl: h.matmul(
                            p[:, 0:2], lhsT=wst[:, sl, k, j * 128:(j + 1) * 128], rhs=cond[:, k, :],
                            start=(k == 0), stop=(k == KC - 1)),
                           reads=[wst.s[sl], cond.s[0]], writes=p.s)
                    op("dve", lambda h, p=p, ch=ch: h.tensor_scalar(
                        out=ada[:, ch, :], in0=p[:, 0:2], scalar1=badat[:, ch:ch + 1], scalar2=None, op0=OP.add),
                       reads=[p.s[0], badat.s[0]], writes=ada.s)
        for n in range(3):
            for c in range(2):
                op("dve", lambda h, n=n, c=c: h.scalar_tensor_tensor(
                    out=modA[:, n, :, c], in0=ada[:, (3 * n + 1) * KC:(3 * n + 2) * KC, c], scalar=1.0,
                    in1=nrm[:, l * 3 + n, :], op0=OP.add, op1=OP.mult), reads=[ada.s[0], nrm.s[0]], writes=modA.s)
            op("dve", lambda h, n=n: h.tensor_scalar(
                out=gate5[:, n, :, :], in0=ada[:, (3 * n + 2) * KC:(3 * n + 3) * KC, :],
                scalar1=(1.0 if n == 1 else 0.5), scalar2=None, op0=OP.mult),
               reads=[ada.s[0]], writes=gate5.s)

    def modnorm(x, h_, sq, rb, n, c, final=False):
        p = ps()
        for k in range(KC):
            op("act", lambda h, k=k: h.activation(out=sq[:, k % 2, :], in_=x[:, k, :], func=AF.Square),
               reads=x.s, writes=[sq.s[k % 2]])
            op("pe", lambda h, k=k, p=p: h.matmul(p[:, :], lhsT=ones_b[:], rhs=sq[:, k % 2, :],
                                                  start=(k == 0), stop=(k == KC - 1)),
               reads=[sq.s[k % 2], ones_b.s[0]], writes=p.s)
        op("act", lambda h, p=p: h.activation(out=rb[:], in_=p[:, :], func=AF.Sqrt, bias=epsb[:, 0:1], scale=1.0 / D),
           reads=[p.s[0], epsb.s[0]], writes=rb.s)
        op("dve", lambda h: h.reciprocal(out=rb[:], in_=rb[:]), reads=rb.s, writes=rb.s)
        for k in range(KC):
            if not final:
                op("dve", lambda h, k=k: h.scalar_tensor_tensor(
                    out=sq[:, 2 + k % 2, :], in0=x[:, k, :], scalar=modA[:, n, k, c:c + 1], in1=rb[:],
                    op0=OP.mult, op1=OP.mult), reads=[x.s[0], rb.s[0], modA.s[0]], writes=[sq.s[2 + k % 2]])
                op("act", lambda h, k=k: h.activation(
                    out=h_[:, k, :], in_=sq[:, 2 + k % 2, :], func=AF.Identity,
                    bias=ada[:, 3 * n * KC + k, c:c + 1], scale=1.0),
                   reads=[sq.s[2 + k % 2], ada.s[0]], writes=h_.s)
            else:
                op("dve", lambda h, k=k: h.scalar_tensor_tensor(
                    out=x[:, k, :], in0=x[:, k, :], scalar=nrm[:, DEPTH * 3, k:k + 1], in1=rb[:],
                    op0=OP.mult, op1=OP.mult), reads=[x.s[0], rb.s[0], nrm.s[0]], writes=x.s)

    def ffn(l, which, x, h_, act, wstage, wb, sg, c):
        wi = ffn_in[which].ap
        wo = ffn_out[which].ap
        gi = 0 if which == 0 else 2
        for j in range(NFC):
            sl = j % 2
            for half in range(2):
                col = half * DFF + j * 128
                src = wi[l, :, col:col + 128].rearrange("(k p) n -> p k n", p=128)
                dma("sp", wstage[:, sl, half * 2048:(half + 1) * 2048].rearrange("p (k n) -> p k n", n=128), src,
                    wstage.s[sl], writes=[wstage.s[sl]])
            ce = "pool" if j % 2 == 0 else "dve"
            op(ce, lambda h, sl=sl: h.tensor_copy(out=wb[:, sl, 0:4096], in_=wstage[:, sl, 0:4096]),
               reads=[wstage.s[sl]], writes=[wb.s[sl]])
            pg, pu = ps(), ps()
            for half, p in ((0, pg), (1, pu)):
                for k in range(KC):
                    o0 = half * 2048 + k * 128
                    op("pe", lambda h, p=p, k=k, o0=o0, sl=sl: h.matmul(
                        p[:, :], lhsT=wb[:, sl, o0:o0 + 128], rhs=h_[:, k, :], start=(k == 0), stop=(k == KC - 1)),
                       reads=[wb.s[sl], h_.s[0]], writes=p.s)
            op("act", lambda h, pg=pg, sl=sl: h.activation(out=sg[:, sl, :], in_=pg[:, :], func=AF.Silu),
               reads=pg.s, writes=[sg.s[sl]])
            op("dve", lambda h, pu=pu, j=j, sl=sl: h.tensor_tensor(out=act[:, j, :], in0=sg[:, sl, :], in1=pu[:, :], op=OP.mult),
               reads=[sg.s[sl], pu.s[0]], writes=act.s)
        for cc in range(KC):
            sl = cc % 2
            src = wo[l, :, cc * 128:(cc + 1) * 128].rearrange("(j p) n -> p j n", p=128)
            dma("sp", wstage[:, sl, 0:NFC * 128].rearrange("p (j n) -> p j n", n=128), src, wstage.s[sl],
                writes=[wstage.s[sl]])
            ce = "pool" if cc % 2 == 0 else "dve"
            op(ce, lambda h, sl=sl: h.tensor_copy(out=wb[:, sl, 0:NFC * 128], in_=wstage[:, sl, 0:NFC * 128]),
               reads=[wstage.s[sl]], writes=[wb.s[sl]])
            p = ps()
            for j in range(NFC):
                op("pe", lambda h, p=p, j=j, sl=sl: h.matmul(
                    p[:, :], lhsT=wb[:, sl, j * 128:(j + 1) * 128], rhs=act[:, j, :], start=(j == 0), stop=(j == NFC - 1)),
                   reads=[wb.s[sl], act.s[0]], writes=p.s)
            op("dve", lambda h, p=p, cc=cc: h.scalar_tensor_tensor(
                out=x[:, cc, :], in0=p[:, :], scalar=gate5[:, gi, cc, c:c + 1], in1=x[:, cc, :], op0=OP.mult, op1=OP.add),
               reads=[p.s[0], gate5.s[0], x.s[0]], writes=x.s)

    def token_phase(l, do_c, do_a, first, last):
        with Scope():
            x = T(cx, "xA", [128, KC, TT], F32)
            h_ = T(cx, "hA", [128, KC, TT], BF16)
            act = T(cx, "actA", [128, NFC, TT], BF16)
            sq = T(cx, "sqA", [128, 4, TT], BF16, nslots=4)
            sg = T(cx, "sgA", [128, 2, TT], F32, nslots=2)
            rb = T(cx, "rbA", [128, TT], F32)
            wstage = T(cx, "wstA", [128, 2, NFC * 128], F32, nslots=2)
            wb = T(cx, "wbA", [128, 2, NFC * 128], BF16, nslots=2)
            ev = T(cx, "evA", [128, 2, TT], F32, nslots=2)
            for t in range(NTILE):
                c = 0 if t == 0 else 1
                cols = slice(t * TT, (t + 1) * TT)
                if first:
                    dma("sp", x[:], xT.ap[:, cols].rearrange("(k p) n -> p k n", p=128), x.s[0], writes=x.s)
                else:
                    dma("sp", x[:], xs.ap[:, cols].rearrange("(k p) n -> p k n", p=128), x.s[0], reads=[xs.s[t]], writes=x.s)
                if do_c:
                    lc = l - 1
                    for k in range(KC):
                        sl = k % 2
                        dma("sp", ev[:, sl, :], cat.ap[k * 128:(k + 1) * 128, cols], ev.s[sl], reads=[cat.s[t]], writes=[ev.s[sl]])
                        op("pool", lambda h, k=k, sl=sl: h.tensor_copy(out=h_[:, k, :], in_=ev[:, sl, :]),
                           reads=[ev.s[sl]], writes=h_.s)
                    for cc in range(KC):
                        sl = cc % 2
                        src = w_out.ap[lc, :, cc * 128:(cc + 1) * 128].rearrange("(k p) n -> p k n", p=128)
                        dma("sp", wstage[:, sl, 0:KC * 128].rearrange("p (k n) -> p k n", n=128), src, wstage.s[sl],
                            writes=[wstage.s[sl]])
                        op("pool", lambda h, sl=sl: h.tensor_copy(out=wb[:, sl, 0:KC * 128], in_=wstage[:, sl, 0:KC * 128]),
                           reads=[wstage.s[sl]], writes=[wb.s[sl]])
                        p = ps()
                        for k in range(KC):
                            op("pe", lambda h, p=p, k=k, sl=sl: h.matmul(
                                p[:, :], lhsT=wb[:, sl, k * 128:(k + 1) * 128], rhs=h_[:, k, :],
                                start=(k == 0), stop=(k == KC - 1)), reads=[wb.s[sl], h_.s[0]], writes=p.s)
                        op("dve", lambda h, p=p, cc=cc: h.scalar_tensor_tensor(
                            out=x[:, cc, :], in0=p[:, :], scalar=gate5[:, 1, cc, c:c + 1], in1=x[:, cc, :],
                            op0=OP.mult, op1=OP.add), reads=[p.s[0], gate5.s[0], x.s[0]], writes=x.s)
                    modnorm(x, h_, sq, rb, 2, c)
                    ffn(lc, 1, x, h_, act, wstage, wb, sg, c)
                    if last:
                        modnorm(x, None, sq, rb, 0, c, final=True)
                        dma("sp", yT.ap[:, cols].rearrange("(k p) n -> p k n", p=128), x[:], x.s[0], reads=x.s, writes=yT.s)
                    else:
                        dma("sp", xs.ap[:, cols].rearrange("(k p) n -> p k n", p=128), x[:], x.s[0], reads=x.s, writes=[xs.s[t]])
                if do_a:
                    modnorm(x, h_, sq, rb, 0, c)
                    ffn(l, 0, x, h_, act, wstage, wb, sg, c)
                    modnorm(x, h_, sq, rb, 1, c)
                    dma("sp", xs.ap[:, cols].rearrange("(k p) n -> p k n", p=128), x[:], x.s[0], reads=x.s, writes=[xs.s[t]])
                    for oc in range(45):
                        ncol = 128 if oc < 44 else 16
                        sl = oc % 2
                        oc0 = oc * 128 if oc < 16 else (2064 + (oc - 16) * 128 if oc < 44 else 2048)
                        src = w_in.ap[l, :, oc0:oc0 + ncol].rearrange("(k p) n -> p k n", p=128)
                        dma("sp", wstage[:, sl, 0:KC * ncol].rearrange("p (k n) -> p k n", n=ncol), src, wstage.s[sl],
                            writes=[wstage.s[sl]])
                        op("pool", lambda h, sl=sl, ncol=ncol: h.tensor_copy(out=wb[:, sl, 0:KC * ncol], in_=wstage[:, sl, 0:KC * ncol]),
                           reads=[wstage.s[sl]], writes=[wb.s[sl]])
                        p = ps()
                        for k in range(KC):
                            op("pe", lambda h, p=p, k=k, sl=sl, ncol=ncol: h.matmul(
                                p[0:ncol, :], lhsT=wb[:, sl, k * ncol:(k + 1) * ncol], rhs=h_[:, k, :],
                                start=(k == 0), stop=(k == KC - 1)), reads=[wb.s[sl], h_.s[0]], writes=p.s)
                        op("act", lambda h, p=p, sl=sl, ncol=ncol: h.activation(out=ev[0:ncol, sl, :], in_=p[0:ncol, :], func=AF.Identity),
                           reads=p.s, writes=[ev.s[sl]])
                        dma("sp", proj.ap[oc * 128:oc * 128 + ncol, cols], ev[0:ncol, sl, :], ev.s[sl], reads=[ev.s[sl]],
                            writes=[proj.s[t]])
                        if t == 0:
                            r0 = oc * 128
                            if O_DK <= r0 < O_DK + 1024:
                                dma("sp", ck_o.ap[l, r0 - O_DK:r0 - O_DK + 128, :], ev[:, sl, :], ev.s[sl], reads=[ev.s[sl]],
                                    writes=ck_o.s)
                            if O_DV <= r0 < O_DV + 1024:
                                dma("sp", cv_o.ap[l, r0 - O_DV:r0 - O_DV + 128, :], ev[:, sl, :], ev.s[sl], reads=[ev.s[sl]],
                                    writes=cv_o.s)

    def tiles_of(t0, L):
        return [proj.s[t] for t in range(t0 // TT, (t0 + L + TT - 1) // TT)]

    def cat_slots(t0, L):
        return [cat.s[t] for t in range(t0 // TT, (t0 + L + TT - 1) // TT)]

    def gdn_seq(l, t0, L, sidx):
        nch = L // 128
        NG = nch * 8
        pr = tiles_of(t0, L)
        with Scope():
            abt = T(cx, "abt", [16, L], F32)
            gp = T(cx, "gp", [128, 2, 8], F32)
            gn = T(cx, "gn", [128, 128], F32)
            cw = T(cx, "cw", [128, 12, 4], F32)
            nA = T(cx, "nA", [128, 8], F32)
            tmp = T(cx, "gtmp", [128, nch, 8], F32)
            gcol = T(cx, "gcol", [128, nch, 8], F32)
            bcol = T(cx, "bcol", [128, nch, 8], F32)
            GC = T(cx, "GC", [128, nch, 8], F32)
            BE = T(cx, "BE", [128, nch, 8], F32)
            EG = T(cx, "EG", [128, nch, 8], F32)
            EKD = T(cx, "EKD", [128, nch, 8], F32)
            EGL = T(cx, "EGL", [128, nch, 8], F32)
            BG = T(cx, "BG", [128, nch, 8], F32)
            dma("sp", abt[:], proj.ap[O_A:O_A + 16, t0:t0 + L], abt.s[0], reads=pr, writes=abt.s)
            dma("sp", gp[:], gpar.ap[l], gp.s[0], writes=gp.s)
            dma("sp", gn[:], gnorm.ap[l, :, 0, :], gn.s[0], writes=gn.s)
            dma("sp", cw[:], convw.ap[l], cw.s[0], writes=cw.s)
            op("act", lambda h: h.activation(out=nA[:], in_=gp[:, 0, :], func=AF.Exp), reads=gp.s, writes=nA.s)
            op("dve", lambda h: h.tensor_scalar(out=nA[:], in0=nA[:], scalar1=-1.0, scalar2=None, op0=OP.mult),
               reads=nA.s, writes=nA.s)
            for cp in range(nch // 2):
                c0, c1, n = 2 * cp, 2 * cp + 2, 2
                pa = ps()
                for c in range(c0, c1):
                    op("pe", lambda h, c=c, pa=pa: h.transpose(out=pa[:, (c - c0) * 16:(c - c0 + 1) * 16],
                                                               in_=abt[0:16, c * 128:(c + 1) * 128], identity=CN[0:16, C_ID, 0:16]),
                       reads=[abt.s[0], CN.s[0]], writes=pa.s)
                pav = pa[:, 0:n * 16].rearrange("p (c k) -> p c k", k=16)
                op("dve", lambda h, pav=pav, c0=c0, c1=c1, n=n: h.tensor_tensor(
                    out=tmp[:, c0:c1, :], in0=pav[:, :, 0:8], in1=gp[:, 1:2, :].to_broadcast([128, n, 8]), op=OP.add),
                   reads=[pa.s[0], gp.s[0]], writes=tmp.s)
                op("act", lambda h, c0=c0, c1=c1: h.activation(out=tmp[:, c0:c1, :], in_=tmp[:, c0:c1, :], func=AF.Exp),
                   reads=tmp.s, writes=tmp.s)
                op("act", lambda h, c0=c0, c1=c1: h.activation(out=tmp[:, c0:c1, :], in_=tmp[:, c0:c1, :], func=AF.Ln,
                                                               bias=oneb[:, 0:1], scale=1.0), reads=tmp.s + oneb.s, writes=tmp.s)
                op("dve", lambda h, c0=c0, c1=c1, n=n: h.tensor_tensor(
                    out=gcol[:, c0:c1, :], in0=tmp[:, c0:c1, :], in1=nA[:, None, :].to_broadcast([128, n, 8]), op=OP.mult),
                   reads=tmp.s + nA.s, writes=gcol.s)
                op("act", lambda h, pav=pav, c0=c0, c1=c1: h.activation(out=bcol[:, c0:c1, :], in_=pav[:, :, 8:16], func=AF.Exp, scale=-1.0),
                   reads=pa.s, writes=bcol.s)
                op("dve", lambda h, c0=c0, c1=c1: h.tensor_scalar(out=bcol[:, c0:c1, :], in0=bcol[:, c0:c1, :], scalar1=1.0, scalar2=None, op0=OP.add),
                   reads=bcol.s, writes=bcol.s)
                op("dve", lambda h, c0=c0, c1=c1: h.reciprocal(out=bcol[:, c0:c1, :], in_=bcol[:, c0:c1, :]), reads=bcol.s, writes=bcol.s)
                fl2 = lambda t_: t_[:, c0:c1, :].rearrange("p c k -> p (c k)")
                pF, pB, pJ = ps(), ps(), ps()
                op("pe", lambda h: h.matmul(pF[:, 0:16], lhsT=cst(C_U), rhs=fl2(gcol), start=True, stop=True), reads=gcol.s + CN.s, writes=pF.s)
                op("pe", lambda h: h.matmul(pB[:, 0:16], lhsT=cst(C_UJ), rhs=fl2(gcol), start=True, stop=True), reads=gcol.s + CN.s, writes=pB.s)
                op("pe", lambda h: h.matmul(pJ[:, 0:16], lhsT=cst(C_J), rhs=fl2(bcol), start=True, stop=True), reads=bcol.s + CN.s, writes=pJ.s)
                v3 = lambda p_: p_[:, 0:16].rearrange("p (c k) -> p c k", k=8)
                op("dve", lambda h: h.tensor_copy(out=GC[:, c0:c1, 0:4], in_=v3(pF)[:, :, 0:4]), reads=pF.s, writes=GC.s)
                op("dve", lambda h: h.tensor_copy(out=GC[:, c0:c1, 4:8], in_=v3(pB)[:, :, 4:8]), reads=pB.s, writes=GC.s)
                op("dve", lambda h: h.tensor_copy(out=BE[:, c0:c1, 0:4], in_=bcol[:, c0:c1, 0:4]), reads=bcol.s, writes=BE.s)
                op("dve", lambda h: h.tensor_copy(out=BE[:, c0:c1, 4:8], in_=v3(pJ)[:, :, 4:8]), reads=pJ.s, writes=BE.s)
                pL = ps()
                op("pe", lambda h: h.matmul(pL[:, 0:16], lhsT=cst(C_EL), rhs=fl2(GC), start=True, stop=True), reads=GC.s + CN.s, writes=pL.s)
                op("act", lambda h: h.activation(out=EG[:, c0:c1, :], in_=GC[:, c0:c1, :], func=AF.Exp), reads=GC.s, writes=EG.s)
                op("act", lambda h: h.activation(out=fl2(EGL), in_=pL[:, 0:16], func=AF.Exp), reads=pL.s, writes=EGL.s)
                op("dve", lambda h: h.tensor_tensor(out=fl2(EKD), in0=pL[:, 0:16], in1=fl2(GC), op=OP.subtract), reads=pL.s + GC.s, writes=EKD.s)
                op("act", lambda h: h.activation(out=EKD[:, c0:c1, :], in_=EKD[:, c0:c1, :], func=AF.Exp), reads=EKD.s, writes=EKD.s)
                op("dve", lambda h: h.tensor_tensor(out=BG[:, c0:c1, :], in0=BE[:, c0:c1, :], in1=EG[:, c0:c1, :], op=OP.mult), reads=BE.s + EG.s, writes=BG.s)

            gfl = dbg.split(":")[2] if (isinstance(dbg, str) and dbg.count(":") >= 2) else ""
            if "1" in gfl:
                return
            for hh in range(1 if "3" in gfl or "2" in gfl else 4):
                with Scope():
                    pre = T(cx, "pre", [128, L], F32)
                    qkv = T(cx, "qkv", [128, 3, L], F32, nslots=3)
                    sqb = T(cx, "sqb", [128, 512], F32)
                    rsb = T(cx, "rsb", [128, 512], F32)
                    for i in range(3):
                        ci = i * 4 + hh
                        r0 = O_QKV + i * 512 + hh * 128
                        dma("sp", pre[:], proj.ap[r0:r0 + 128, t0:t0 + L], pre.s[0], reads=pr, writes=pre.s)
                        dst = qkv[:, i, :]
                        op("act", lambda h, dst=dst, ci=ci: h.activation(out=dst, in_=pre[:], func=AF.Identity, scale=cw[:, ci, 2:3]),
                           reads=pre.s + cw.s, writes=[qkv.s[i]])
                        for (j, so, do, n_) in ((0, 0, 2, L - 2), (1, 0, 1, L - 1), (3, 1, 0, L - 1)):
                            op("dve", lambda h, i=i, j=j, so=so, do=do, n_=n_, ci=ci: h.scalar_tensor_tensor(
                                out=qkv[:, i, do:do + n_], in0=pre[:, so:so + n_], scalar=cw[:, ci, j:j + 1],
                                in1=qkv[:, i, do:do + n_], op0=OP.mult, op1=OP.add),
                               reads=pre.s + cw.s + [qkv.s[i]], writes=[qkv.s[i]])
                        op("act", lambda h, dst=dst: h.activation(out=dst, in_=dst, func=AF.Silu), reads=[qkv.s[i]], writes=[qkv.s[i]])
                        if i < 2:
                            for b0 in range(0, L, 512):
                                bw = min(512, L - b0)
                                p = ps()
                                op("act", lambda h, i=i, b0=b0, bw=bw: h.activation(out=sqb[:, 0:bw], in_=qkv[:, i, b0:b0 + bw], func=AF.Square),
                                   reads=[qkv.s[i]], writes=sqb.s)
                                op("pe", lambda h, p=p, bw=bw: h.matmul(p[:, 0:bw], lhsT=cst(C_ONE), rhs=sqb[:, 0:bw], start=True, stop=True),
                                   reads=sqb.s + CN.s, writes=p.s)
                                op("act", lambda h, p=p, bw=bw: h.activation(out=rsb[:, 0:bw], in_=p[:, 0:bw], func=AF.Sqrt, bias=epsb[:, 0:1], scale=1.0),
                                   reads=p.s + epsb.s, writes=rsb.s)
                                op("dve", lambda h, bw=bw: h.reciprocal(out=rsb[:, 0:bw], in_=rsb[:, 0:bw]), reads=rsb.s, writes=rsb.s)
                                op("dve", lambda h, i=i, b0=b0, bw=bw: h.scalar_tensor_tensor(
                                    out=qkv[:, i, b0:b0 + bw], in0=qkv[:, i, b0:b0 + bw], scalar=(128 ** -0.5 if i == 0 else 1.0),
                                    in1=rsb[:, 0:bw], op0=OP.mult, op1=OP.mult), reads=[qkv.s[i]] + rsb.s, writes=[qkv.s[i]])
                    if "2" in gfl:
                        continue
                    gdn_head(l, t0, L, sidx, hh, qkv, GC, BE, EG, EKD, EGL, BG, gn, (2 if "3" in gfl else nch))

    def gdn_head(l, t0, L, sidx, hh, qkv, GC, BE, EG, EKD, EGL, BG, gn, nch):
        M = lambda name, w=128: T(cx, name, [128, w], F32)
        NAMES = ["S", "qT", "kT", "vT", "qFr", "kFr", "qTr", "kTr", "vTr", "dg", "X1", "dec", "decT", "Lm", "LT",
                 "Pa", "Pb", "PTa", "PTb", "wT", "AT", "qgT", "qgF", "kd", "vn", "osb"]
        WS = [{n_: M(n_ + str(d_)) for n_ in NAMES} for d_ in range(2)]
        for d_ in range(2):
            WS[d_]["B"] = M("B" + str(d_), 256)
        of_, ob_, zc, sz, on, ss, catc = M("of_"), M("ob_"), M("zc"), M("sz"), M("on"), M("ss", 1), M("catc")
        for dr in range(2):
            S = WS[dr]["S"]
            if sidx is None:
                dma("sp", S[:], st_i.ap[l, dr, hh], S.s[0], writes=S.s)
            else:
                op("pool", lambda h: h.memset(S[:], 0.0), writes=S.s)
        for ci in range(nch):
          for dr in range(2):
            W = WS[dr]
            S, qT, kT, vT, qFr, kFr, qTr, kTr, vTr = (W[k_] for k_ in ("S", "qT", "kT", "vT", "qFr", "kFr", "qTr", "kTr", "vTr"))
            dg, X1, dec, decT, Lm, LT, Pa, Pb, PTa, PTb = (W[k_] for k_ in ("dg", "X1", "dec", "decT", "Lm", "LT", "Pa", "Pb", "PTa", "PTb"))
            wT, AT, qgT, qgF, kd, vn, osb, B = (W[k_] for k_ in ("wT", "AT", "qgT", "qgF", "kd", "vn", "osb", "B"))
            gi = dr * 4 + hh
            if True:
                c = ci if dr == 0 else nch - 1 - ci
                cs = slice(c * 128, (c + 1) * 128)
                col = lambda A_: A_[:, c, gi:gi + 1]

                def tr(dst, src_ap, rd):
                    p = ps()
                    op("pe", lambda h: h.transpose(out=p[:, 0:128], in_=src_ap, identity=cst(C_ID)), reads=rd + CN.s, writes=p.s)
                    op("act", lambda h: h.activation(out=dst[:], in_=p[:, 0:128], func=AF.Identity), reads=p.s, writes=dst.s)

                def mm(dst, lhsT, rhs, rd, n=128, eng="act"):
                    p = ps()
                    op("pe", lambda h: h.matmul(p[:, 0:n], lhsT=lhsT, rhs=rhs, start=True, stop=True), reads=rd, writes=p.s)
                    if dst is not None:
                        if eng == "act":
                            op("act", lambda h: h.activation(out=dst[:, 0:n], in_=p[:, 0:n], func=AF.Identity), reads=p.s, writes=dst.s)
                        else:
                            op("dve", lambda h: h.tensor_copy(out=dst[:, 0:n], in_=p[:, 0:n]), reads=p.s, writes=dst.s)
                    return p

                tr(qT, qkv[:, 0, cs], [qkv.s[0]])
                tr(kT, qkv[:, 1, cs], [qkv.s[1]])
                tr(vT, qkv[:, 2, cs], [qkv.s[2]])
                if dr == 0:
                    qF_, kF_, qT_, kT_, vT_ = qkv[:, 0, cs], qkv[:, 1, cs], qT, kT, vT
                    rF = [qkv.s[0], qkv.s[1]]
                else:
                    mm(qFr, qT[:], cst(C_J), qT.s + CN.s)
                    mm(kFr, kT[:], cst(C_J), kT.s + CN.s)
                    mm(qTr, cst(C_J), qT[:], qT.s + CN.s)
                    mm(kTr, cst(C_J), kT[:], kT.s + CN.s)
                    mm(vTr, cst(C_J), vT[:], vT.s + CN.s)
                    qF_, kF_, qT_, kT_, vT_ = qFr[:], kFr[:], qTr, kTr, vTr
                    rF = qFr.s + kFr.s
                op("dve", lambda h: h.tensor_scalar(out=dg[:], in0=cst(C_ID), scalar1=col(GC), scalar2=None, op0=OP.mult),
                   reads=CN.s + GC.s, writes=dg.s)
                prow = ps()
                op("pe", lambda h: h.matmul(prow[:, 0:128], lhsT=cst(C_ONE), rhs=dg[:], start=True, stop=True), reads=dg.s + CN.s, writes=prow.s)
                op("dve", lambda h: h.scalar_tensor_tensor(out=X1[:], in0=prow[:, 0:128], scalar=col(GC), in1=cst(C_MU),
                                                           op0=OP.subtract, op1=OP.add), reads=prow.s + GC.s + CN.s, writes=X1.s)
                op("act", lambda h: h.activation(out=dec[:], in_=X1[:], func=AF.Exp, scale=-1.0), reads=X1.s, writes=dec.s)
                op("dve", lambda h: h.scalar_tensor_tensor(out=X1[:], in0=prow[:, 0:128], scalar=col(GC), in1=cst(C_ML),
                                                           op0=OP.subtract, op1=OP.subtract), reads=prow.s + GC.s + CN.s, writes=X1.s)
                op("act", lambda h: h.activation(out=decT[:], in_=X1[:], func=AF.Exp), reads=X1.s, writes=decT.s)
                pk = ps()
                op("pe", lambda h: h.matmul(pk[:, 0:128], lhsT=kF_, rhs=kF_, start=True, stop=True), reads=rF, writes=pk.s)
                op("dve", lambda h: h.scalar_tensor_tensor(out=Lm[:], in0=pk[:, 0:128], scalar=col(BE), in1=dec[:],
                                                           op0=OP.mult, op1=OP.mult), reads=pk.s + BE.s + dec.s, writes=Lm.s)
                op("pool", lambda h: h.tensor_tensor(out=Lm[:], in0=Lm[:], in1=cst(C_MS), op=OP.mult), reads=Lm.s + CN.s, writes=Lm.s)
                tr(LT, Lm[:], Lm.s)
                op("dve", lambda h: h.tensor_scalar(out=B[:, 0:128], in0=vT_[:], scalar1=col(BE), scalar2=None, op0=OP.mult),
                   reads=vT_.s + BE.s, writes=B.s)
                op("dve", lambda h: h.tensor_scalar(out=B[:, 128:256], in0=kT_[:], scalar1=col(BG), scalar2=None, op0=OP.mult),
                   reads=kT_.s + BG.s, writes=B.s)
                pbb = mm(None, LT[:], B[:], LT.s + B.s, n=256)
                op("dve", lambda h: h.tensor_tensor(out=B[:], in0=B[:], in1=pbb[:, 0:256], op=OP.subtract), reads=B.s + pbb.s, writes=B.s)
                Pc, PTc, Pn, PTn = Lm, LT, Pa, PTa
                for lev in range(6):
                    mm(PTn, Pc[:], PTc[:], Pc.s + PTc.s, eng="dve")
                    if lev < 5:
                        mm(Pn, PTc[:], Pc[:], Pc.s + PTc.s)
                    pbb = mm(None, PTn[:], B[:], PTn.s + B.s, n=256)
                    op("dve", lambda h, pbb=pbb: h.tensor_tensor(out=B[:], in0=B[:], in1=pbb[:, 0:256], op=OP.add), reads=B.s + pbb.s, writes=B.s)
                    Pc, PTc = Pn, PTn
                    Pn, PTn = (Pb, PTb) if Pn is Pa else (Pa, PTa)
                tr(wT, B[:, 128:256], B.s)
                pa_ = ps()
                op("pe", lambda h: h.matmul(pa_[:, 0:128], lhsT=kF_, rhs=qF_, start=True, stop=True), reads=rF, writes=pa_.s)
                op("dve", lambda h: h.tensor_tensor(out=AT[:], in0=pa_[:, 0:128], in1=decT[:], op=OP.mult), reads=pa_.s + decT.s, writes=AT.s)
                op("dve", lambda h: h.tensor_scalar(out=qgT[:], in0=qT_[:], scalar1=col(EG), scalar2=None, op0=OP.mult),
                   reads=qT_.s + EG.s, writes=qgT.s)
                tr(qgF, qgT[:], qgT.s)
                op("pool", lambda h: h.tensor_scalar(out=kd[:], in0=kT_[:], scalar1=col(EKD), scalar2=None, op0=OP.mult),
                   reads=kT_.s + EKD.s, writes=kd.s)
                pw = ps()
                op("pe", lambda h: h.matmul(pw[:, 0:128], lhsT=wT[:], rhs=S[:], start=True, stop=True), reads=wT.s + S.s, writes=pw.s)
                op("dve", lambda h: h.tensor_tensor(out=vn[:], in0=B[:, 0:128], in1=pw[:, 0:128], op=OP.subtract), reads=B.s + pw.s, writes=vn.s)
                po = ps()
                op("pe", lambda h: h.matmul(po[:, 0:128], lhsT=qgF[:], rhs=S[:], start=True, stop=False), reads=qgF.s + S.s, writes=po.s)
                op("pe", lambda h: h.matmul(po[:, 0:128], lhsT=AT[:], rhs=vn[:], start=False, stop=True), reads=AT.s + vn.s, writes=po.s)
                op("act", lambda h: h.activation(out=osb[:], in_=po[:, 0:128], func=AF.Identity), reads=po.s, writes=osb.s)
                pS = ps()
                op("pe", lambda h: h.matmul(pS[:, 0:128], lhsT=kd[:], rhs=vn[:], start=True, stop=True), reads=kd.s + vn.s, writes=pS.s)
                op("dve", lambda h: h.scalar_tensor_tensor(out=S[:], in0=S[:], scalar=col(EGL), in1=pS[:, 0:128],
                                                           op0=OP.mult, op1=OP.add), reads=S.s + EGL.s + pS.s, writes=S.s)
                dst_dr = ofs if dr == 0 else obs
                dma("sp", dst_dr.ap[c * 128:(c + 1) * 128, :], osb[:], osb.s[0], reads=osb.s, writes=dst_dr.s)
        if sidx is not None:
            for dr in range(2):
                S = WS[dr]["S"]
                dma("sp", st_o.ap[sidx, l, dr, hh], S[:], S.s[0], reads=S.s, writes=st_o.s)
        for c in range(nch):
            dma("sp", of_[:], ofs.ap[c * 128:(c + 1) * 128, :], of_.s[0], reads=ofs.s, writes=of_.s)
            dma("sp", ob_[:], obs.ap[c * 128:(c + 1) * 128, :], ob_.s[0], reads=obs.s, writes=ob_.s)
            r0 = O_Z + hh * 128
            dma("sp", zc[:], proj.ap[r0:r0 + 128, t0 + c * 128:t0 + (c + 1) * 128], zc.s[0], reads=tiles_of(t0, L), writes=zc.s)
            pj = ps()
            op("pe", lambda h: h.matmul(pj[:, 0:128], lhsT=cst(C_J), rhs=ob_[:], start=True, stop=True), reads=ob_.s + CN.s, writes=pj.s)
            op("dve", lambda h: h.tensor_tensor(out=of_[:], in0=of_[:], in1=pj[:, 0:128], op=OP.add), reads=of_.s + pj.s, writes=of_.s)
            op("act", lambda h: h.activation(out=on[:], in_=of_[:], func=AF.Square, accum_out=ss[:, 0:1]), reads=of_.s, writes=on.s + ss.s)
            op("act", lambda h: h.activation(out=ss[:], in_=ss[:], func=AF.Sqrt, bias=epsb[:, 0:1], scale=1.0 / 128), reads=ss.s + epsb.s, writes=ss.s)
            op("dve", lambda h: h.reciprocal(out=ss[:], in_=ss[:]), reads=ss.s, writes=ss.s)
            op("dve", lambda h: h.scalar_tensor_tensor(out=on[:], in0=of_[:], scalar=ss[:, 0:1], in1=gn[:], op0=OP.mult, op1=OP.mult),
               reads=of_.s + ss.s + gn.s, writes=on.s)
            op("act", lambda h: h.activation(out=sz[:], in_=zc[:], func=AF.Silu), reads=zc.s, writes=sz.s)
            pt = ps()
            op("pe", lambda h: h.transpose(out=pt[:, 0:128], in_=on[:], identity=cst(C_ID)), reads=on.s + CN.s, writes=pt.s)
            op("dve", lambda h: h.tensor_tensor(out=catc[:], in0=pt[:, 0:128], in1=sz[:], op=OP.mult), reads=pt.s + sz.s, writes=catc.s)
            dma("sp", cat.ap[hh * 128:(hh + 1) * 128, t0 + c * 128:t0 + (c + 1) * 128], catc[:], catc.s[0], reads=catc.s,
                writes=cat_slots(t0, L))

    def attn_seq(l, t0, L, sidx):
        sample = sidx is None
        Lk = L + (PAST if sample else 0)
        nkc = Lk // 128
        QB = min(512, L)
        nsub = QB // 128
        lam_init = 0.8 - 0.6 * float(np.exp(-0.3 * l))
        pr = tiles_of(t0, L)
        with Scope():
            dl = T(cx, "dl", [128, 4, 64], F32)
            dn = T(cx, "dn", [128, 128], F32)
            lam = T(cx, "lam", [128, 4], F32)
            ldump = T(cx, "ldump", [128, 64], F32)
            stg = T(cx, "stg", [128, L], F32)
            stg2 = T(cx, "stg2", [128, 512], F32)
            stg3 = T(cx, "stg3", [128, 512], F32)
            QF = T(cx, "QF", [64, 2, L], BF16, nslots=2)
            KF = T(cx, "KF", [64, 2, Lk], BF16, nslots=2)
            V1 = T(cx, "V1", [128, nkc, 132], BF16)
            PT = T(cx, "PT", [128, 2, 512], BF16, nslots=2)
            ob = T(cx, "ob", [128, 2, 128], F32, nslots=2)
            rs = T(cx, "rs", [128, 2], F32)
            on = T(cx, "onA", [128, 128], F32)
            ss = T(cx, "ssA", [128, 1], F32)
            catc = T(cx, "catcA", [128, 128], F32)
            rp = T(cx, "rp", [64, 2, LS if sample else 2], F32)
            rR = T(cx, "rR", [64, 64], F32)
            dma("sp", dl[:], dlam.ap[l], dl.s[0], writes=dl.s)
            dma("sp", dn[:], gnorm.ap[l, :, 1, :], dn.s[0], writes=dn.s)
            if sample:
                dma("sp", rp[:], ropeS.ap[:, :, :], rp.s[0], writes=rp.s)
                dma("sp", rR[:], ropeR.ap[:, :], rR.s[0], writes=rR.s)
            for i in range(2):
                op("dve", lambda h, i=i: h.tensor_tensor(out=ldump[:], in0=dl[:, 2 * i, :], in1=dl[:, 2 * i + 1, :], op=OP.mult),
                   reads=dl.s, writes=ldump.s)
                op("dve", lambda h, i=i: h.tensor_reduce(out=lam[:, i:i + 1], in_=ldump[:], axis=mybir.AxisListType.X, op=OP.add),
                   reads=ldump.s, writes=lam.s)
            op("act", lambda h: h.activation(out=lam[:, 0:2], in_=lam[:, 0:2], func=AF.Exp), reads=lam.s, writes=lam.s)
            op("dve", lambda h: h.tensor_tensor(out=lam[:, 2:3], in0=lam[:, 1:2], in1=lam[:, 0:1], op=OP.subtract), reads=lam.s, writes=lam.s)
            op("dve", lambda h: h.tensor_scalar(out=lam[:, 2:3], in0=lam[:, 2:3], scalar1=-lam_init, scalar2=None, op0=OP.add), reads=lam.s, writes=lam.s)
            rot["lo"], rot["n"], rot["i"] = 4, 4, 0
            ACC = PS[0:4]
            for hd in range(8):
                for m in range(2):
                    for (which, O_, dst, off) in (("q", O_DQ, QF, 0), ("k", O_DK, KF, Lk - L)):
                        r0 = O_ + hd * 128 + m * 64
                        dma("sp", stg[0:64, :], proj.ap[r0:r0 + 64, t0:t0 + L], stg.s[0], reads=pr, writes=stg.s)
                        if not sample:
                            op("dve", lambda h, dst=dst, m=m, off=off: h.tensor_copy(out=dst[:, m, off:off + L], in_=stg[0:64, :]),
                               reads=stg.s, writes=[dst.s[m]])
                        else:
                            for b0 in range(0, L, 512):
                                p = ps()
                                op("pe", lambda h, p=p, b0=b0: h.matmul(p[0:64, :], lhsT=rR[:], rhs=stg[0:64, b0:b0 + 512], start=True, stop=True),
                                   reads=stg.s + rR.s, writes=p.s)
                                op("pool", lambda h, b0=b0: h.tensor_tensor(out=stg2[0:64, :], in0=stg[0:64, b0:b0 + 512], in1=rp[:, 0, b0:b0 + 512], op=OP.mult),
                                   reads=stg.s + rp.s, writes=stg2.s)
                                op("dve", lambda h, p=p, b0=b0: h.tensor_tensor(out=stg3[0:64, :], in0=p[0:64, :], in1=rp[:, 1, b0:b0 + 512], op=OP.mult),
                                   reads=p.s + rp.s, writes=stg3.s)
                                op("dve", lambda h, dst=dst, m=m, off=off, b0=b0: h.tensor_tensor(
                                    out=dst[:, m, off + b0:off + b0 + 512], in0=stg2[0:64, :], in1=stg3[0:64, :], op=OP.add),
                                   reads=stg2.s + stg3.s, writes=[dst.s[m]])
                    if sample:
                        dma("sp", stg2[0:64, 0:PAST], ckT.ap[l, hd, m], stg2.s[0], writes=stg2.s)
                        op("dve", lambda h, m=m: h.tensor_copy(out=KF[:, m, 0:PAST], in_=stg2[0:64, 0:PAST]), reads=stg2.s, writes=[KF.s[m]])
                r0 = O_DV + hd * 128
                dma("sp", stg[:, :], proj.ap[r0:r0 + 128, t0:t0 + L], stg.s[0], reads=pr, writes=stg.s)
                op("pool", lambda h: h.memset(V1[:, :, 128:132], 1.0), writes=V1.s)
                koff = (PAST // 128) if sample else 0
                if sample:
                    dma("sp", stg2[:, 0:512].rearrange("p (c d) -> p c d", d=128),
                        cvT.ap[l, :, hd, :].rearrange("(c p) d -> p c d", p=128), stg2.s[0], writes=stg2.s)
                    op("dve", lambda h: h.tensor_copy(out=V1[:, 0:4, 0:128], in_=stg2[:, 0:512].rearrange("p (c d) -> p c d", d=128)),
                       reads=stg2.s, writes=V1.s)
                for c in range(L // 128):
                    p = ps()
                    op("pe", lambda h, p=p, c=c: h.transpose(out=p[:, 0:128], in_=stg[:, c * 128:(c + 1) * 128], identity=cst(C_ID)),
                       reads=stg.s + CN.s, writes=p.s)
                    op("act", lambda h, p=p, c=c: h.activation(out=V1[:, koff + c, 0:128], in_=p[:, 0:128], func=AF.Identity), reads=p.s, writes=V1.s)
                for qb in range(L // QB):
                    for s in range(nsub):
                        for m in range(2):
                            pass
                    for m in range(2):
                        for kc in range(nkc):
                            p = ps()
                            sl = kc % 2
                            op("pe", lambda h, p=p, kc=kc, m=m: h.matmul(p[:, 0:QB], lhsT=KF[:, m, kc * 128:(kc + 1) * 128],
                                                                         rhs=QF[:, m, qb * QB:(qb + 1) * QB], start=True, stop=True),
                               reads=[KF.s[m], QF.s[m]], writes=p.s)
                            op("act", lambda h, p=p, sl=sl: h.activation(out=PT[:, sl, 0:QB], in_=p[:, 0:QB], func=AF.Exp,
                                                                         bias=shb[:, 0:1], scale=0.125), reads=p.s + shb.s, writes=[PT.s[sl]])
                            for s in range(nsub):
                                op("pe", lambda h, s=s, sl=sl, kc=kc: h.matmul(ACC[s][:, 0:129], lhsT=PT[:, sl, s * 128:(s + 1) * 128],
                                                                              rhs=V1[:, kc, 0:129], start=(kc == 0), stop=(kc == nkc - 1)),
                                   reads=[PT.s[sl]] + V1.s, writes=ACC[s].s)
                        for s in range(nsub):
                            op("dve", lambda h, s=s, m=m: h.reciprocal(out=rs[:, m:m + 1], in_=ACC[s][:, 128:129]), reads=ACC[s].s, writes=rs.s)
                            if m == 0:
                                op("dve", lambda h, s=s: h.tensor_scalar(out=obq[:, s, :], in0=ACC[s][:, 0:128], scalar1=rs[:, 0:1], scalar2=None, op0=OP.mult),
                                   reads=ACC[s].s + rs.s, writes=[obq.s[s]])
                            else:
                                op("dve", lambda h, s=s: h.tensor_scalar(out=ob[:, 1, :], in0=ACC[s][:, 0:128], scalar1=rs[:, 1:2], scalar2=lam[:, 2:3],
                                                                         op0=OP.mult, op1=OP.mult), reads=ACC[s].s + rs.s + lam.s, writes=[ob.s[1]])
                                op("dve", lambda h, s=s: h.tensor_tensor(out=ob[:, 0, :], in0=obq[:, s, :], in1=ob[:, 1, :], op=OP.add),
                                   reads=[obq.s[s], ob.s[1]], writes=[ob.s[0]])
                                op("act", lambda h: h.activation(out=on[:], in_=ob[:, 0, :], func=AF.Square, accum_out=ss[:, 0:1]), reads=[ob.s[0]], writes=on.s + ss.s)
                                op("act", lambda h: h.activation(out=ss[:], in_=ss[:], func=AF.Sqrt, bias=epsb[:, 0:1], scale=1.0 / 128), reads=ss.s + epsb.s, writes=ss.s)
                                op("dve", lambda h: h.reciprocal(out=ss[:], in_=ss[:]), reads=ss.s, writes=ss.s)
                                op("dve", lambda h: h.scalar_tensor_tensor(out=on[:], in0=ob[:, 0, :], scalar=ss[:, 0:1], in1=dn[:], op0=OP.mult, op1=OP.mult),
                                   reads=[ob.s[0]] + ss.s + dn.s, writes=on.s)
                                pt = ps()
                                op("pe", lambda h, pt=pt: h.transpose(out=pt[:, 0:128], in_=on[:], identity=cst(C_ID)), reads=on.s + CN.s, writes=pt.s)
                                op("act", lambda h, pt=pt: h.activation(out=catc[:], in_=pt[:, 0:128], func=AF.Identity, scale=(1.0 - lam_init)),
                                   reads=pt.s, writes=catc.s)
                                tq = t0 + qb * QB + s * 128
                                dma("sp", cat.ap[512 + hd * 128:512 + (hd + 1) * 128, tq:tq + 128], catc[:], catc.s[0], reads=catc.s,
                                    writes=cat_slots(t0, L))
            rot["lo"], rot["n"], rot["i"] = 0, 8, 0

    obq = None

    def pool_seq(l, t0, L):
        pr = tiles_of(t0, L)
        W = L + 16
        with Scope():
            xp = T(cx, "xp", [128, W], F32)
            wa = T(cx, "wa", [128, W], F32)
            wbf = T(cx, "wbf", [128, W], F32)
            pw = T(cx, "pw", [128, 4, 128], F32)
            psc = T(cx, "psc", [128, 4], F32)
            ped = T(cx, "ped", [128, 4, 16], F32)
            oc_ = T(cx, "oc_", [128, 2, 512], F32, nslots=2)
            dma("sp", pw[:], poolw.ap[l], pw.s[0], writes=pw.s)
            dma("sp", psc[:], pscale.ap[l], psc.s[0], writes=psc.s)
            dma("sp", ped[:], pedge.ap[:, :, :], ped.s[0], writes=ped.s)
            for g in range(4):
                w = 2 << g
                r0 = O_PIN + g * 128
                op("pool", lambda h: h.memset(xp[:], 0.0), writes=xp.s)
                op("pool", lambda h: h.memset(wa[:], 0.0), writes=wa.s)
                op("pool", lambda h: h.memset(wbf[:], 0.0), writes=wbf.s)
                dma("sp", xp[:, 8:8 + L], proj.ap[r0:r0 + 128, t0:t0 + L], xp.s[0], reads=pr, writes=xp.s)
                op("dve", lambda h: h.tensor_tensor(out=wa[:, 1:W], in0=xp[:, 0:W - 1], in1=xp[:, 1:W], op=OP.add), reads=xp.s, writes=wa.s)
                cur, nxt = wa, wbf
                sh = 1
                for lev in range(g):
                    lo, hi = 2 * sh, W - 2 * sh
                    op("dve", lambda h, cur=cur, nxt=nxt, lo=lo, hi=hi, sh=sh: h.tensor_tensor(
                        out=nxt[:, lo:hi], in0=cur[:, lo - sh:hi - sh], in1=cur[:, lo + sh:hi + sh], op=OP.add),
                       reads=cur.s, writes=nxt.s)
                    cur, nxt = nxt, cur
                    sh *= 2
                op("dve", lambda h, cur=cur, g=g: h.tensor_tensor(out=cur[:, 8:16], in0=cur[:, 8:16], in1=ped[:, g, 0:8], op=OP.mult),
                   reads=cur.s + ped.s, writes=cur.s)
                op("dve", lambda h, cur=cur, g=g: h.tensor_tensor(out=cur[:, L:L + 8], in0=cur[:, L:L + 8], in1=ped[:, g, 8:16], op=OP.mult),
                   reads=cur.s + ped.s, writes=cur.s)
                op("dve", lambda h, cur=cur, w=w: h.scalar_tensor_tensor(out=cur[:, 8:8 + L], in0=cur[:, 8:8 + L], scalar=1.0 / w, in1=xp[:, 8:8 + L],
                                                                        op0=OP.mult, op1=OP.subtract), reads=cur.s + xp.s, writes=cur.s)
                for b0 in range(0, L, 512):
                    bw = min(512, L - b0)
                    sl = (b0 // 512) % 2
                    p = ps()
                    op("pe", lambda h, p=p, cur=cur, b0=b0, bw=bw, g=g: h.matmul(p[:, 0:bw], lhsT=pw[:, g, :], rhs=cur[:, 8 + b0:8 + b0 + bw], start=True, stop=True),
                       reads=cur.s + pw.s, writes=p.s)
                    op("act", lambda h, p=p, sl=sl, bw=bw, g=g: h.activation(out=oc_[:, sl, 0:bw], in_=p[:, 0:bw], func=AF.Identity, scale=psc[:, g:g + 1]),
                       reads=p.s + psc.s, writes=[oc_.s[sl]])
                    dma("sp", cat.ap[1536 + g * 128:1536 + (g + 1) * 128, t0 + b0:t0 + b0 + bw], oc_[:, sl, 0:bw], oc_.s[sl], reads=[oc_.s[sl]],
                        writes=cat_slots(t0, L))

    SEQS = [(0, LP, 0), (LP, LP, 1), (NPS * LP, LS, None)]

    def mixer(l):
        nonlocal obq
        fl = dbg.split(":")[1] if mx else "gapPS"
        for (t0, L, sidx) in SEQS:
            if (sidx is None and "S" not in fl) or (sidx is not None and "P" not in fl):
                continue
            if "g" in fl:
                gdn_seq(l, t0, L, sidx)
            if "a" in fl:
                with Scope():
                    obq = T(cx, "obq", [128, 4, 128], F32, nslots=4)
                    attn_seq(l, t0, L, sidx)
            if "p" in fl:
                pool_seq(l, t0, L)

    stage = dbg or "full"
    if mx:
        mixer(0)
        cx.finish(yT.s + ck_o.s + cv_o.s + st_o.s + xs.s + proj.s + cat.s + ofs.s + obs.s)
        return
    ada_layer(0)
    token_phase(0, False, True, True, False)
    if stage != "A0":
        mixer(0)
        if stage == "C0":
            token_phase(1, True, False, False, False)
        elif stage != "M0":
            for l in range(1, DEPTH):
                token_phase(l, True, False, False, False)
                ada_layer(l)
                token_phase(l, False, True, False, False)
                mixer(l)
            token_phase(DEPTH, True, False, False, True)
    cx.finish(yT.s + ck_o.s + cv_o.s + st_o.s + xs.s + proj.s + cat.s + ofs.s + obs.s)


def _consts():
    i = np.arange(128)
    C = np.zeros((NCONST, 128, 128), np.float32)
    C[C_ID] = np.eye(128)
    C[C_ONE] = 1.0
    C[C_U] = (i[:, None] <= i[None, :])
    C[C_UJ] = ((127 - i[:, None]) <= i[None, :])
    C[C_J] = (i[:, None] + i[None, :] == 127)
    C[C_EL] = (i[:, None] == 127)
    C[C_MS] = (i[None, :] < i[:, None])
    C[C_MU] = MASKV * (i[None, :] > i[:, None])
    C[C_ML] = MASKV * (i[None, :] < i[:, None])
    return np.ascontiguousarray(C.transpose(1, 0, 2))


def _rope():
    t = np.arange(LS)
    pos = np.stack([t // 64, t % 64]).astype(np.float32)
    half = 32
    inv = (10000.0 ** (-np.arange(0, half, 2, dtype=np.float32) / half)).astype(np.float32)
    tab = np.zeros((64, 2, LS), np.float32)
    for d in range(64):
        ang = pos[d // 32] * inv[(d % 32) % 16]
        tab[d, 0] = np.cos(ang)
        tab[d, 1] = np.sin(ang)
    R = np.zeros((64, 64), np.float32)
    for base in (0, 32):
        for k in range(16):
            R[base + k, base + k + 16] = -1.0
            R[base + k + 16, base + k] = 1.0
    return tab, np.ascontiguousarray(R.T)


def _pedge():
    E = np.ones((128, 4, 16), np.float32)
    for g, w in enumerate((2, 4, 8, 16)):
        a = w // 2
        b = w - a - 1
        for t in range(8):
            cnt = (t + b + 1) - max(t - a, 0)
            E[:, g, t] = w / cnt
        for r in range(8):
            d = 7 - r
            cnt = min(b, d) + 1 + a
            E[:, g, 8 + r] = w / cnt
    return E


def _fm(v):
    v = np.asarray(v, np.float32)
    n = v.shape[-1] // 128
    r = v.reshape(v.shape[:-1] + (n, 128))
    return np.ascontiguousarray(np.moveaxis(r, -1, 0))


_CACHE = {}
_DBG = {}


def kernel(x_prompt, x_sample, c, cache_k, cache_v, state_gdn, c_ctx, w_ada, b_ada, norm_ffn1, ffn1_in,
           ffn1_out, norm_mix, w_in, gdn_conv, gdn_a_log, gdn_dt_bias, gdn_norm, diff_lam, diff_norm,
           pool_w, pool_scale, w_out, norm_ffn2, ffn2_in, ffn2_out, final_norm, _dbg=None):
    f = lambda a: np.ascontiguousarray(np.asarray(a, np.float32))
    key = _dbg or "full"
    if key not in _CACHE:
        _CACHE[key] = build_program(_dbg)
    nc = _CACHE[key]
    x_prompt, x_sample = f(x_prompt), f(x_sample)
    rope_tab, rope_R = _rope()
    rep = lambda v: np.ascontiguousarray(np.broadcast_to(np.asarray(v, np.float32)[None], (128,) + tuple(np.shape(v))))
    norms = np.stack([_fm(np.stack([norm_ffn1[l], norm_mix[l], norm_ffn2[l]])) for l in range(DEPTH)], 1)
    norms = np.concatenate([norms.reshape(128, DEPTH * 3, KC), _fm(final_norm)[:, None, :]], 1)
    shared = {
        "w_ada": f(w_ada), "b_ada": np.ascontiguousarray(_fm(b_ada).transpose(1, 0, 2)),
        "norms": np.ascontiguousarray(norms),
        "ffn1_in": f(ffn1_in), "ffn2_in": f(ffn2_in), "ffn1_out": f(ffn1_out), "ffn2_out": f(ffn2_out),
        "w_in": f(w_in), "w_out": f(w_out), "consts": _consts(),
        "convw": np.ascontiguousarray(f(gdn_conv).reshape(DEPTH, 4, 12, 128).transpose(0, 3, 2, 1)),
        "gpar": np.ascontiguousarray(np.stack([rep(np.stack([f(gdn_a_log)[l].reshape(8), f(gdn_dt_bias)[l].reshape(8)])) for l in range(DEPTH)])),
        "gnorm": np.ascontiguousarray(np.stack([rep(np.stack([f(gdn_norm)[l], f(diff_norm)[l]])) for l in range(DEPTH)])),
        "dlam": np.ascontiguousarray(np.stack([rep(f(diff_lam)[l]) for l in range(DEPTH)])),
        "poolw": np.ascontiguousarray(f(pool_w).transpose(0, 2, 1, 3)),
        "pscale": np.ascontiguousarray(f(pool_scale).reshape(DEPTH, 4, 128).transpose(0, 2, 1)),
        "pedge": _pedge(), "ropeS": rope_tab, "ropeR": rope_R,
    }
    mx = isinstance(_dbg, str) and _dbg.startswith("MX")
    if mx:
        for k_ in ("w_ada", "ffn1_in", "ffn2_in", "ffn1_out", "ffn2_out", "w_in", "w_out"):
            shared[k_] = np.zeros((1, 1, 1), np.float32)
        shared["proj"] = _DBG["proj"]
    in_maps = []
    for r in range(8):
        b = r // 4
        xcat = np.concatenate([x_prompt[2 * r], x_prompt[2 * r + 1], x_sample[b]], 0)
        m = dict(shared)
        m["xT"] = np.ascontiguousarray(xcat.T)
        m["cT"] = np.ascontiguousarray(np.stack([_fm(c_ctx), _fm(f(c)[b])], -1))
        ck = f(cache_k)[b].reshape(DEPTH, PAST, 8, 2, 64)
        m["ckT"] = np.ascontiguousarray(ck.transpose(0, 2, 3, 4, 1))
        m["cvT"] = f(cache_v)[b]
        m["st_i"] = f(state_gdn)[b]
        in_maps.append(m)
    res = run_bass_kernel_spmd(nc, in_maps, core_ids=list(range(8))).results
    if mx:
        _DBG["cat"] = res[0]["cat"]
    y_prompt = np.zeros((16, LP, D), np.float32)
    y_sample = np.zeros((2, LS, D), np.float32)
    nck = np.zeros((16, DEPTH, LP, 8, 128), np.float32)
    ncv = np.zeros((16, DEPTH, LP, 8, 128), np.float32)
    nst = np.zeros((16, DEPTH, 2, 4, 128, 128), np.float32)
    for r in range(8):
        o = res[r]
        yt = o["yT"]
        for s in range(NPS):
            y_prompt[2 * r + s] = yt[:, s * LP:(s + 1) * LP].T
            nck[2 * r + s] = o["ck_o"][:, :, s * LP:(s + 1) * LP].reshape(DEPTH, 8, 128, LP).transpose(0, 3, 1, 2)
            ncv[2 * r + s] = o["cv_o"][:, :, s * LP:(s + 1) * LP].reshape(DEPTH, 8, 128, LP).transpose(0, 3, 1, 2)
            nst[2 * r + s] = o["st_o"][s]
        if r % 4 == 0:
            y_sample[r // 4] = yt[:, NPS * LP:].T
    return (y_prompt, y_sample, nck, ncv, nst)
```

```python
import numpy as np
from contextlib import ExitStack
import concourse.bass as bass
import concourse.mybir as mybir
from concourse.bass_utils import run_bass_kernel_spmd

F32 = mybir.dt.float32
BF16 = mybir.dt.bfloat16
AF = mybir.ActivationFunctionType
OP = mybir.AluOpType

D = 2048
DEPTH = 2
NPS = 2
LP = 256
LS = 4096
PAST = 512
NT = NPS * LP + LS
TT = 512
NTILE = NT // TT
DFF = 5632
NFC = DFF // 128
INC = 5648
INCP = 45 * 128
KC = 16
EPS = 1e-6
O_QKV, O_Z, O_DQ, O_DK, O_DV, O_PIN, O_A, O_B = 0, 1536, 2048, 3072, 4096, 5120, 5632, 5640
MASKV = 30000.0


class Slot:
    __slots__ = ("w", "r", "dsem", "dcnt", "name")

    def __init__(self, name):
        self.w = None
        self.r = {}
        self.dsem = None
        self.dcnt = 0
        self.name = name


class Eng:
    def __init__(self, name, h):
        self.name = name
        self.h = h
        self.sem = None
        self.n = 0
        self.seen = {}


class Ctx:
    def __init__(self, nc, stack):
        self.nc = nc
        self.stack = stack
        self.semstack = stack
        self.E = {
            "pe": Eng("pe", nc.tensor),
            "act": Eng("act", nc.scalar),
            "dve": Eng("dve", nc.vector),
            "pool": Eng("pool", nc.gpsimd),
            "sp": Eng("sp", nc.sync),
        }
        self.sems = []
        self.ninst = 0
        self.free_dsems = []
        self.scopes = []

    def newsem(self, name):
        h = self.semstack.enter_context(self.nc.semaphore(f"{name}_{len(self.sems)}"))
        self.sems.append(h)
        return len(self.sems) - 1

    def _wait(self, e, deps):
        for si, (val, eng) in deps.items():
            if e.name == "pe" and eng == "pe":
                continue
            if e.seen.get(si, 0) < val:
                e.h.wait_ge(self.sems[si], val)
                e.seen[si] = val

    @staticmethod
    def _add(deps, ev):
        si, val, eng = ev
        if si not in deps or deps[si][0] < val:
            deps[si] = (val, eng)

    def _deps(self, reads, writes):
        deps = {}
        for b in reads:
            if b.w is not None:
                self._add(deps, b.w)
        for b in writes:
            if b.w is not None:
                self._add(deps, b.w)
            for si, (val, eng) in b.r.items():
                self._add(deps, (si, val, eng))
        return deps

    def _record(self, ev, reads, writes):
        si, val, eng = ev
        for b in reads:
            if si not in b.r or b.r[si][0] < val:
                b.r[si] = (val, eng)
        for b in writes:
            b.w = ev
            b.r = {}
        self.ninst += 1

    def op(self, eng, fn, reads=(), writes=()):
        e = self.E[eng]
        if e.sem is None or e.n >= 15000:
            e.sem = self.newsem(eng)
            e.n = 0
        self._wait(e, self._deps(reads, writes))
        inst = fn(e.h)
        e.n += 1
        inst.then_inc(self.sems[e.sem], 1)
        ev = (e.sem, e.n, eng)
        self._record(ev, reads, writes)
        return ev

    def dma(self, q, out, in_, owner, reads=(), writes=(), **kw):
        e = self.E[q]
        if owner.dsem is None:
            if self.free_dsems:
                owner.dsem, owner.dcnt = self.free_dsems.pop()
            else:
                owner.dsem = self.newsem("d")
                owner.dcnt = 0
        deps = self._deps(reads, writes)
        if owner.dcnt > 0:
            self._add(deps, (owner.dsem, 16 * owner.dcnt, "dma"))
        self._wait(e, deps)
        if owner.dcnt >= 900:
            owner.dsem = self.newsem("d")
            owner.dcnt = 0
        inst = e.h.dma_start(out=out, in_=in_, **kw)
        owner.dcnt += 1
        inst.then_inc(self.sems[owner.dsem], 16)
        ev = (owner.dsem, 16 * owner.dcnt, "dma")
        self._record(ev, reads, writes)
        return ev

    def finish(self, slots):
        e = self.E["sp"]
        deps = {}
        for b in slots:
            if b.w is not None:
                self._add(deps, b.w)
            for si, (val, eng) in b.r.items():
                self._add(deps, (si, val, eng))
        self._wait(e, deps)


class T:
    cnt = 0

    def __init__(self, cx, name, shape, dtype, nslots=1, psum=False):
        alloc = cx.nc.psum_tensor if psum else cx.nc.sbuf_tensor
        T.cnt += 1
        self.t = cx.stack.enter_context(alloc(f"{name}_{T.cnt}", list(shape), dtype))
        self.s = [Slot(f"{name}{i}") for i in range(nslots)]
        self.name = name
        if cx.scopes:
            cx.scopes[-1].append(self)

    def __getitem__(self, idx):
        return self.t[idx]


class DR:
    def __init__(self, cx, name, shape, dtype, kind=None, nslots=1):
        if kind is None:
            self.ap = cx.nc.dram_tensor(name, list(shape), dtype).ap()
        else:
            self.ap = cx.nc.dram_tensor(name, list(shape), dtype, kind=kind).ap()
        self.s = [Slot(f"{name}{i}") for i in range(nslots)]


def build_program(dbg=False):
    nc = bass.Bass("TRN2", target_bir_lowering=False)
    stack = ExitStack()
    cx = Ctx(nc, stack)
    with stack:
        _emit(nc, cx, dbg)
    return nc


C_ID, C_ONE, C_U, C_UJ, C_J, C_EL, C_MS, C_MU, C_ML = range(9)
NCONST = 9
SHIFT = 10.0


def _emit(nc, cx, dbg):
    op, dma = cx.op, cx.dma
    IN = lambda name, shape, dt=F32: DR(cx, name, shape, dt, kind="ExternalInput")
    OUT = lambda name, shape, dt=F32: DR(cx, name, shape, dt, kind="ExternalOutput")
    mx = isinstance(dbg, str) and dbg.startswith("MX")
    big = (lambda sh: [1, 1, 1]) if mx else (lambda sh: sh)
    xT = IN("xT", [D, NT])
    cT = IN("cT", [128, KC, 2])
    w_ada = IN("w_ada", big([DEPTH, D, 9 * D]))
    b_ada = IN("b_ada", [DEPTH, 128, 144])
    norms = IN("norms", [128, DEPTH * 3 + 1, KC])
    ffn_in = [IN("ffn1_in", big([DEPTH, D, 2 * DFF])), IN("ffn2_in", big([DEPTH, D, 2 * DFF]))]
    ffn_out = [IN("ffn1_out", big([DEPTH, DFF, D])), IN("ffn2_out", big([DEPTH, DFF, D]))]
    w_in = IN("w_in", big([DEPTH, D, INC]))
    w_out = IN("w_out", big([DEPTH, D, D]))
    consts = IN("consts", [128, NCONST, 128])
    convw = IN("convw", [DEPTH, 128, 12, 4])
    gpar = IN("gpar", [DEPTH, 128, 2, 8])
    gnorm = IN("gnorm", [DEPTH, 128, 2, 128])
    dlam = IN("dlam", [DEPTH, 128, 4, 64])
    poolw = IN("poolw", [DEPTH, 128, 4, 128])
    pscale = IN("pscale", [DEPTH, 128, 4])
    pedge = IN("pedge", [128, 4, 16])
    ropeS = IN("ropeS", [64, 2, LS])
    ropeR = IN("ropeR", [64, 64])
    ckT = IN("ckT", [DEPTH, 8, 2, 64, PAST])
    cvT = IN("cvT", [DEPTH, PAST, 8, 128])
    st_i = IN("st_i", [DEPTH, 2, 4, 128, 128])
    yT = OUT("yT", [D, NT])
    ck_o = OUT("ck_o", [DEPTH, 1024, NPS * LP])
    cv_o = OUT("cv_o", [DEPTH, 1024, NPS * LP])
    st_o = OUT("st_o", [NPS, DEPTH, 2, 4, 128, 128])
    xs = DR(cx, "xs", [D, NT], F32, nslots=NTILE)
    proj = DR(cx, "proj", [INCP, NT], F32, nslots=NTILE, kind=("ExternalInput" if mx else None))
    cat = DR(cx, "cat", [D, NT], F32, nslots=NTILE, kind=("ExternalOutput" if mx else None))
    ofs = DR(cx, "ofs", [LS, 128], F32)
    obs = DR(cx, "obs", [LS, 128], F32)
    wsc = DR(cx, "wsc", [NFC + KC, 128, NFC * 128], BF16, nslots=NFC + KC)

    ones_b = T(cx, "ones_b", [128, 128], BF16)
    ada = T(cx, "ada", [128, 144, 2], F32)
    nrm = T(cx, "nrm", [128, DEPTH * 3 + 1, KC], F32)
    cond = T(cx, "cond", [128, KC, 2], F32)
    badat = T(cx, "badat", [128, 144], F32)
    modA = T(cx, "modA", [128, 3, KC, 2], F32)
    gate5 = T(cx, "gate5", [128, 3, KC, 2], F32)
    epsb = T(cx, "epsb", [128, 1], F32)
    oneb = T(cx, "oneb", [128, 1], F32)
    shb = T(cx, "shb", [128, 1], F32)
    CN = T(cx, "CN", [128, NCONST, 128], F32)
    PS = [T(cx, f"ps{i}", [128, 512], F32, psum=True) for i in range(8)]
    rot = {"lo": 0, "n": 8, "i": 0}

    def ps():
        rot["i"] = (rot["i"] + 1) % rot["n"]
        return PS[rot["lo"] + rot["i"]]

    def cst(i):
        return CN[:, i, :]

    op("pool", lambda h: h.memset(ones_b[:], 1.0), writes=ones_b.s)
    op("pool", lambda h: h.memset(epsb[:], EPS), writes=epsb.s)
    op("pool", lambda h: h.memset(oneb[:], 1.0), writes=oneb.s)
    op("pool", lambda h: h.memset(shb[:], -SHIFT), writes=shb.s)
    dma("sp", nrm[:], norms.ap[:, :, :], nrm.s[0], writes=nrm.s)
    dma("sp", cond[:], cT.ap[:, :, :], cond.s[0], writes=cond.s)
    dma("sp", CN[:], consts.ap[:, :, :], CN.s[0], writes=CN.s)
    op("act", lambda h: h.activation(out=cond[:], in_=cond[:], func=AF.Silu), reads=cond.s, writes=cond.s)

    class Scope:
        def __enter__(self):
            self.st = ExitStack()
            self.old = cx.stack
            cx.stack = self.st
            self.st.__enter__()
            cx.scopes.append([])
            return self

        def __exit__(self, *a):
            sc_ = cx.scopes.pop()
            deps_ = {}
            for t_ in sc_:
                for sl_ in t_.s:
                    if sl_.w is not None:
                        cx._add(deps_, sl_.w)
                    for si_, (v_, e_) in sl_.r.items():
                        cx._add(deps_, (si_, v_, e_))
            for en_ in cx.E.values():
                for si_, (v_, e_) in deps_.items():
                    if en_.seen.get(si_, 0) < v_:
                        en_.h.wait_ge(cx.sems[si_], v_)
                        en_.seen[si_] = v_
            for t_ in sc_:
                for sl_ in t_.s:
                    if sl_.dsem is not None:
                        cx.free_dsems.append((sl_.dsem, sl_.dcnt))
            cx.stack = self.old
            return self.st.__exit__(*a)

    def ada_layer(l):
        with Scope():
            wst = T(cx, f"wada{l}", [128, 2, KC, 512], F32, nslots=2)
            dma("sp", badat[:], b_ada.ap[l], badat.s[0], writes=badat.s)
            for blk in range(9 * D // 512):
                sl = blk % 2
                src = w_ada.ap[l, :, blk * 512:(blk + 1) * 512].rearrange("(k p) n -> p k n", p=128)
                dma("sp", wst[:, sl, :, :], src, wst.s[sl], writes=[wst.s[sl]])
                for j in range(4):
                    ch = blk * 4 + j
                    p = ps()
                    for k in range(KC):
                        op("pe", lambda h, k=k, j=j, p=p, sl=sl: h.matmul(
                            p[:, 0:2], lhsT=wst[:, sl, k, j * 128:(j + 1) * 128], rhs=cond[:, k, :],
                            start=(k == 0), stop=(k == KC - 1)),
                           reads=[wst.s[sl], cond.s[0]], writes=p.s)
                    op("dve", lambda h, p=p, ch=ch: h.tensor_scalar(
                        out=ada[:, ch, :], in0=p[:, 0:2], scalar1=badat[:, ch:ch + 1], scalar2=None, op0=OP.add),
                       reads=[p.s[0], badat.s[0]], writes=ada.s)
        for n in range(3):
            for c in range(2):
                op("dve", lambda h, n=n, c=c: h.scalar_tensor_tensor(
                    out=modA[:, n, :, c], in0=ada[:, (3 * n + 1) * KC:(3 * n + 2) * KC, c], scalar=1.0,
                    in1=nrm[:, l * 3 + n, :], op0=OP.add, op1=OP.mult), reads=[ada.s[0], nrm.s[0]], writes=modA.s)
            op("dve", lambda h, n=n: h.tensor_scalar(
                out=gate5[:, n, :, :], in0=ada[:, (3 * n + 2) * KC:(3 * n + 3) * KC, :],
                scalar1=(1.0 if n == 1 else 0.5), scalar2=None, op0=OP.mult),
               reads=[ada.s[0]], writes=gate5.s)

    def modnorm(x, h_, sq, rb, n, c, final=False):
        p = ps()
        for k in range(KC):
            op("act", lambda h, k=k: h.activation(out=sq[:, k % 2, :], in_=x[:, k, :], func=AF.Square),
               reads=x.s, writes=[sq.s[k % 2]])
            op("pe", lambda h, k=k, p=p: h.matmul(p[:, :], lhsT=ones_b[:], rhs=sq[:, k % 2, :],
                                                  start=(k == 0), stop=(k == KC - 1)),
               reads=[sq.s[k % 2], ones_b.s[0]], writes=p.s)
        op("act", lambda h, p=p: h.activation(out=rb[:], in_=p[:, :], func=AF.Sqrt, bias=epsb[:, 0:1], scale=1.0 / D),
           reads=[p.s[0], epsb.s[0]], writes=rb.s)
        op("dve", lambda h: h.reciprocal(out=rb[:], in_=rb[:]), reads=rb.s, writes=rb.s)
        for k in range(KC):
            if not final:
                op("dve", lambda h, k=k: h.scalar_tensor_tensor(
                    out=sq[:, 2 + k % 2, :], in0=x[:, k, :], scalar=modA[:, n, k, c:c + 1], in1=rb[:],
                    op0=OP.mult, op1=OP.mult), reads=[x.s[0], rb.s[0], modA.s[0]], writes=[sq.s[2 + k % 2]])
                op("act", lambda h, k=k: h.activation(
                    out=h_[:, k, :], in_=sq[:, 2 + k % 2, :], func=AF.Identity,
                    bias=ada[:, 3 * n * KC + k, c:c + 1], scale=1.0),
                   reads=[sq.s[2 + k % 2], ada.s[0]], writes=h_.s)
            else:
                op("dve", lambda h, k=k: h.scalar_tensor_tensor(
                    out=x[:, k, :], in0=x[:, k, :], scalar=nrm[:, DEPTH * 3, k:k + 1], in1=rb[:],
                    op0=OP.mult, op1=OP.mult), reads=[x.s[0], rb.s[0], nrm.s[0]], writes=x.s)

    def ffn(l, which, x, h_, act, wstage, wb, sg, c, t):
        wi = ffn_in[which].ap
        wo = ffn_out[which].ap
        gi = 0 if which == 0 else 2
        for j in range(NFC):
            sl = j % 2
            if t == 0:
                for half in range(2):
                    col = half * DFF + j * 128
                    src = wi[l, :, col:col + 128].rearrange("(k p) n -> p k n", p=128)
                    dma("sp", wstage[:, sl, half * 2048:(half + 1) * 2048].rearrange("p (k n) -> p k n", n=128), src,
                        wstage.s[sl], writes=[wstage.s[sl]])
                ce = "pool" if j % 2 == 0 else "dve"
                op(ce, lambda h, sl=sl: h.tensor_copy(out=wb[:, sl, 0:4096], in_=wstage[:, sl, 0:4096]),
                   reads=[wstage.s[sl]], writes=[wb.s[sl]])
                dma("sp", wsc.ap[j, :, 0:4096], wb[:, sl, 0:4096], wb.s[sl], reads=[wb.s[sl]], writes=[wsc.s[j]])
            else:
                dma("sp", wb[:, sl, 0:4096], wsc.ap[j, :, 0:4096], wb.s[sl], reads=[wsc.s[j]], writes=[wb.s[sl]])
            pg, pu = ps(), ps()
            for half, p in ((0, pg), (1, pu)):
                for k in range(KC):
                    o0 = half * 2048 + k * 128
                    op("pe", lambda h, p=p, k=k, o0=o0, sl=sl: h.matmul(
                        p[:, :], lhsT=wb[:, sl, o0:o0 + 128], rhs=h_[:, k, :], start=(k == 0), stop=(k == KC - 1)),
                       reads=[wb.s[sl], h_.s[0]], writes=p.s)
            op("act", lambda h, pg=pg, sl=sl: h.activation(out=sg[:, sl, :], in_=pg[:, :], func=AF.Silu),
               reads=pg.s, writes=[sg.s[sl]])
            op("dve", lambda h, pu=pu, j=j, sl=sl: h.tensor_tensor(out=act[:, j, :], in0=sg[:, sl, :], in1=pu[:, :], op=OP.mult),
               reads=[sg.s[sl], pu.s[0]], writes=act.s)
        for cc in range(KC):
            sl = cc % 2
            if t == 0:
                src = wo[l, :, cc * 128:(cc + 1) * 128].rearrange("(j p) n -> p j n", p=128)
                dma("sp", wstage[:, sl, 0:NFC * 128].rearrange("p (j n) -> p j n", n=128), src, wstage.s[sl],
                    writes=[wstage.s[sl]])
                ce = "pool" if cc % 2 == 0 else "dve"
                op(ce, lambda h, sl=sl: h.tensor_copy(out=wb[:, sl, 0:NFC * 128], in_=wstage[:, sl, 0:NFC * 128]),
                   reads=[wstage.s[sl]], writes=[wb.s[sl]])
                dma("sp", wsc.ap[NFC + cc, :, :], wb[:, sl, 0:NFC * 128], wb.s[sl], reads=[wb.s[sl]], writes=[wsc.s[NFC + cc]])
            else:
                dma("sp", wb[:, sl, 0:NFC * 128], wsc.ap[NFC + cc, :, :], wb.s[sl], reads=[wsc.s[NFC + cc]], writes=[wb.s[sl]])
            p = ps()
            for j in range(NFC):
                op("pe", lambda h, p=p, j=j, sl=sl: h.matmul(
                    p[:, :], lhsT=wb[:, sl, j * 128:(j + 1) * 128], rhs=act[:, j, :], start=(j == 0), stop=(j == NFC - 1)),
                   reads=[wb.s[sl], act.s[0]], writes=p.s)
            op("dve", lambda h, p=p, cc=cc: h.scalar_tensor_tensor(
                out=x[:, cc, :], in0=p[:, :], scalar=gate5[:, gi, cc, c:c + 1], in1=x[:, cc, :], op0=OP.mult, op1=OP.add),
               reads=[p.s[0], gate5.s[0], x.s[0]], writes=x.s)

    def token_phase(l, do_c, do_a, first, last):
        with Scope():
            x = T(cx, "xA", [128, KC, TT], F32)
            h_ = T(cx, "hA", [128, KC, TT], BF16)
            act = T(cx, "actA", [128, NFC, TT], BF16)
            sq = T(cx, "sqA", [128, 4, TT], BF16, nslots=4)
            sg = T(cx, "sgA", [128, 2, TT], F32, nslots=2)
            rb = T(cx, "rbA", [128, TT], F32)
            wstage = T(cx, "wstA", [128, 2, NFC * 128], F32, nslots=2)
            wb = T(cx, "wbA", [128, 2, NFC * 128], BF16, nslots=2)
            ev = T(cx, "evA", [128, 2, TT], F32, nslots=2)
            for t in range(NTILE):
                c = 0 if t == 0 else 1
                cols = slice(t * TT, (t + 1) * TT)
                if first:
                    dma("sp", x[:], xT.ap[:, cols].rearrange("(k p) n -> p k n", p=128), x.s[0], writes=x.s)
                else:
                    dma("sp", x[:], xs.ap[:, cols].rearrange("(k p) n -> p k n", p=128), x.s[0], reads=[xs.s[t]], writes=x.s)
                if do_c:
                    lc = l - 1
                    for k in range(KC):
                        sl = k % 2
                        dma("sp", ev[:, sl, :], cat.ap[k * 128:(k + 1) * 128, cols], ev.s[sl], reads=[cat.s[t]], writes=[ev.s[sl]])
                        op("pool", lambda h, k=k, sl=sl: h.tensor_copy(out=h_[:, k, :], in_=ev[:, sl, :]),
                           reads=[ev.s[sl]], writes=h_.s)
                    for cc in range(KC):
                        sl = cc % 2
                        src = w_out.ap[lc, :, cc * 128:(cc + 1) * 128].rearrange("(k p) n -> p k n", p=128)
                        dma("sp", wstage[:, sl, 0:KC * 128].rearrange("p (k n) -> p k n", n=128), src, wstage.s[sl],
                            writes=[wstage.s[sl]])
                        op("pool", lambda h, sl=sl: h.tensor_copy(out=wb[:, sl, 0:KC * 128], in_=wstage[:, sl, 0:KC * 128]),
                           reads=[wstage.s[sl]], writes=[wb.s[sl]])
                        p = ps()
                        for k in range(KC):
                            op("pe", lambda h, p=p, k=k, sl=sl: h.matmul(
                                p[:, :], lhsT=wb[:, sl, k * 128:(k + 1) * 128], rhs=h_[:, k, :],
                                start=(k == 0), stop=(k == KC - 1)), reads=[wb.s[sl], h_.s[0]], writes=p.s)
                        op("dve", lambda h, p=p, cc=cc: h.scalar_tensor_tensor(
                            out=x[:, cc, :], in0=p[:, :], scalar=gate5[:, 1, cc, c:c + 1], in1=x[:, cc, :],
                            op0=OP.mult, op1=OP.add), reads=[p.s[0], gate5.s[0], x.s[0]], writes=x.s)
                    modnorm(x, h_, sq, rb, 2, c)
                    ffn(lc, 1, x, h_, act, wstage, wb, sg, c, t)
                    if last:
                        modnorm(x, None, sq, rb, 0, c, final=True)
                        dma("sp", yT.ap[:, cols].rearrange("(k p) n -> p k n", p=128), x[:], x.s[0], reads=x.s, writes=yT.s)
                    else:
                        dma("sp", xs.ap[:, cols].rearrange("(k p) n -> p k n", p=128), x[:], x.s[0], reads=x.s, writes=[xs.s[t]])
                if do_a:
                    modnorm(x, h_, sq, rb, 0, c)
                    ffn(l, 0, x, h_, act, wstage, wb, sg, c, t)
                    modnorm(x, h_, sq, rb, 1, c)
                    dma("sp", xs.ap[:, cols].rearrange("(k p) n -> p k n", p=128), x[:], x.s[0], reads=x.s, writes=[xs.s[t]])
                    for oc in range(45):
                        ncol = 128 if oc < 44 else 16
                        sl = oc % 2
                        oc0 = oc * 128 if oc < 16 else (2064 + (oc - 16) * 128 if oc < 44 else 2048)
                        src = w_in.ap[l, :, oc0:oc0 + ncol].rearrange("(k p) n -> p k n", p=128)
                        dma("sp", wstage[:, sl, 0:KC * ncol].rearrange("p (k n) -> p k n", n=ncol), src, wstage.s[sl],
                            writes=[wstage.s[sl]])
                        op("pool", lambda h, sl=sl, ncol=ncol: h.tensor_copy(out=wb[:, sl, 0:KC * ncol], in_=wstage[:, sl, 0:KC * ncol]),
                           reads=[wstage.s[sl]], writes=[wb.s[sl]])
                        p = ps()
                        for k in range(KC):
                            op("pe", lambda h, p=p, k=k, sl=sl, ncol=ncol: h.matmul(
                                p[0:ncol, :], lhsT=wb[:, sl, k * ncol:(k + 1) * ncol], rhs=h_[:, k, :],
                                start=(k == 0), stop=(k == KC - 1)), reads=[wb.s[sl], h_.s[0]], writes=p.s)
                        op("act", lambda h, p=p, sl=sl, ncol=ncol: h.activation(out=ev[0:ncol, sl, :], in_=p[0:ncol, :], func=AF.Identity),
                           reads=p.s, writes=[ev.s[sl]])
                        dma("sp", proj.ap[oc * 128:oc * 128 + ncol, cols], ev[0:ncol, sl, :], ev.s[sl], reads=[ev.s[sl]],
                            writes=[proj.s[t]])
                        if t == 0:
                            r0 = oc * 128
                            if O_DK <= r0 < O_DK + 1024:
                                dma("sp", ck_o.ap[l, r0 - O_DK:r0 - O_DK + 128, :], ev[:, sl, :], ev.s[sl], reads=[ev.s[sl]],
                                    writes=ck_o.s)
                            if O_DV <= r0 < O_DV + 1024:
                                dma("sp", cv_o.ap[l, r0 - O_DV:r0 - O_DV + 128, :], ev[:, sl, :], ev.s[sl], reads=[ev.s[sl]],
                                    writes=cv_o.s)

    def tiles_of(t0, L):
        return [proj.s[t] for t in range(t0 // TT, (t0 + L + TT - 1) // TT)]

    def cat_slots(t0, L):
        return [cat.s[t] for t in range(t0 // TT, (t0 + L + TT - 1) // TT)]

    def gdn_seq(l, t0, L, sidx):
        nch = L // 128
        NG = nch * 8
        pr = tiles_of(t0, L)
        with Scope():
            abt = T(cx, "abt", [16, L], F32)
            gp = T(cx, "gp", [128, 2, 8], F32)
            gn = T(cx, "gn", [128, 128], F32)
            cw = T(cx, "cw", [128, 12, 4], F32)
            nA = T(cx, "nA", [128, 8], F32)
            tmp = T(cx, "gtmp", [128, nch, 8], F32)
            gcol = T(cx, "gcol", [128, nch, 8], F32)
            bcol = T(cx, "bcol", [128, nch, 8], F32)
            GC = T(cx, "GC", [128, nch, 8], F32)
            BE = T(cx, "BE", [128, nch, 8], F32)
            EG = T(cx, "EG", [128, nch, 8], F32)
            EKD = T(cx, "EKD", [128, nch, 8], F32)
            EGL = T(cx, "EGL", [128, nch, 8], F32)
            BG = T(cx, "BG", [128, nch, 8], F32)
            dma("sp", abt[:], proj.ap[O_A:O_A + 16, t0:t0 + L], abt.s[0], reads=pr, writes=abt.s)
            dma("sp", gp[:], gpar.ap[l], gp.s[0], writes=gp.s)
            dma("sp", gn[:], gnorm.ap[l, :, 0, :], gn.s[0], writes=gn.s)
            dma("sp", cw[:], convw.ap[l], cw.s[0], writes=cw.s)
            op("act", lambda h: h.activation(out=nA[:], in_=gp[:, 0, :], func=AF.Exp), reads=gp.s, writes=nA.s)
            op("dve", lambda h: h.tensor_scalar(out=nA[:], in0=nA[:], scalar1=-1.0, scalar2=None, op0=OP.mult),
               reads=nA.s, writes=nA.s)
            for cp in range(nch // 2):
                c0, c1, n = 2 * cp, 2 * cp + 2, 2
                pa = ps()
                for c in range(c0, c1):
                    op("pe", lambda h, c=c, pa=pa: h.transpose(out=pa[:, (c - c0) * 16:(c - c0 + 1) * 16],
                                                               in_=abt[0:16, c * 128:(c + 1) * 128], identity=CN[0:16, C_ID, 0:16]),
                       reads=[abt.s[0], CN.s[0]], writes=pa.s)
                pav = pa[:, 0:n * 16].rearrange("p (c k) -> p c k", k=16)
                op("dve", lambda h, pav=pav, c0=c0, c1=c1, n=n: h.tensor_tensor(
                    out=tmp[:, c0:c1, :], in0=pav[:, :, 0:8], in1=gp[:, 1:2, :].to_broadcast([128, n, 8]), op=OP.add),
                   reads=[pa.s[0], gp.s[0]], writes=tmp.s)
                op("act", lambda h, c0=c0, c1=c1: h.activation(out=tmp[:, c0:c1, :], in_=tmp[:, c0:c1, :], func=AF.Exp),
                   reads=tmp.s, writes=tmp.s)
                op("act", lambda h, c0=c0, c1=c1: h.activation(out=tmp[:, c0:c1, :], in_=tmp[:, c0:c1, :], func=AF.Ln,
                                                               bias=oneb[:, 0:1], scale=1.0), reads=tmp.s + oneb.s, writes=tmp.s)
                op("dve", lambda h, c0=c0, c1=c1, n=n: h.tensor_tensor(
                    out=gcol[:, c0:c1, :], in0=tmp[:, c0:c1, :], in1=nA[:, None, :].to_broadcast([128, n, 8]), op=OP.mult),
                   reads=tmp.s + nA.s, writes=gcol.s)
                op("act", lambda h, pav=pav, c0=c0, c1=c1: h.activation(out=bcol[:, c0:c1, :], in_=pav[:, :, 8:16], func=AF.Exp, scale=-1.0),
                   reads=pa.s, writes=bcol.s)
                op("dve", lambda h, c0=c0, c1=c1: h.tensor_scalar(out=bcol[:, c0:c1, :], in0=bcol[:, c0:c1, :], scalar1=1.0, scalar2=None, op0=OP.add),
                   reads=bcol.s, writes=bcol.s)
                op("dve", lambda h, c0=c0, c1=c1: h.reciprocal(out=bcol[:, c0:c1, :], in_=bcol[:, c0:c1, :]), reads=bcol.s, writes=bcol.s)
                fl2 = lambda t_: t_[:, c0:c1, :].rearrange("p c k -> p (c k)")
                pF, pB, pJ = ps(), ps(), ps()
                op("pe", lambda h: h.matmul(pF[:, 0:16], lhsT=cst(C_U), rhs=fl2(gcol), start=True, stop=True), reads=gcol.s + CN.s, writes=pF.s)
                op("pe", lambda h: h.matmul(pB[:, 0:16], lhsT=cst(C_UJ), rhs=fl2(gcol), start=True, stop=True), reads=gcol.s + CN.s, writes=pB.s)
                op("pe", lambda h: h.matmul(pJ[:, 0:16], lhsT=cst(C_J), rhs=fl2(bcol), start=True, stop=True), reads=bcol.s + CN.s, writes=pJ.s)
                v3 = lambda p_: p_[:, 0:16].rearrange("p (c k) -> p c k", k=8)
                op("dve", lambda h: h.tensor_copy(out=GC[:, c0:c1, 0:4], in_=v3(pF)[:, :, 0:4]), reads=pF.s, writes=GC.s)
                op("dve", lambda h: h.tensor_copy(out=GC[:, c0:c1, 4:8], in_=v3(pB)[:, :, 4:8]), reads=pB.s, writes=GC.s)
                op("dve", lambda h: h.tensor_copy(out=BE[:, c0:c1, 0:4], in_=bcol[:, c0:c1, 0:4]), reads=bcol.s, writes=BE.s)
                op("dve", lambda h: h.tensor_copy(out=BE[:, c0:c1, 4:8], in_=v3(pJ)[:, :, 4:8]), reads=pJ.s, writes=BE.s)
                pL = ps()
                op("pe", lambda h: h.matmul(pL[:, 0:16], lhsT=cst(C_EL), rhs=fl2(GC), start=True, stop=True), reads=GC.s + CN.s, writes=pL.s)
                op("act", lambda h: h.activation(out=EG[:, c0:c1, :], in_=GC[:, c0:c1, :], func=AF.Exp), reads=GC.s, writes=EG.s)
                op("act", lambda h: h.activation(out=fl2(EGL), in_=pL[:, 0:16], func=AF.Exp), reads=pL.s, writes=EGL.s)
                op("dve", lambda h: h.tensor_tensor(out=fl2(EKD), in0=pL[:, 0:16], in1=fl2(GC), op=OP.subtract), reads=pL.s + GC.s, writes=EKD.s)
                op("act", lambda h: h.activation(out=EKD[:, c0:c1, :], in_=EKD[:, c0:c1, :], func=AF.Exp), reads=EKD.s, writes=EKD.s)
                op("dve", lambda h: h.tensor_tensor(out=BG[:, c0:c1, :], in0=BE[:, c0:c1, :], in1=EG[:, c0:c1, :], op=OP.mult), reads=BE.s + EG.s, writes=BG.s)

            gfl = dbg.split(":")[2] if (isinstance(dbg, str) and dbg.count(":") >= 2) else ""
            if "1" in gfl:
                return
            for hh in range(1 if "3" in gfl or "2" in gfl else 4):
                with Scope():
                    pre = T(cx, "pre", [128, L], F32)
                    qkv = T(cx, "qkv", [128, 3, L], F32, nslots=3)
                    sqb = T(cx, "sqb", [128, 512], F32)
                    rsb = T(cx, "rsb", [128, 512], F32)
                    for i in range(3):
                        ci = i * 4 + hh
                        r0 = O_QKV + i * 512 + hh * 128
                        dma("sp", pre[:], proj.ap[r0:r0 + 128, t0:t0 + L], pre.s[0], reads=pr, writes=pre.s)
                        dst = qkv[:, i, :]
                        op("act", lambda h, dst=dst, ci=ci: h.activation(out=dst, in_=pre[:], func=AF.Identity, scale=cw[:, ci, 2:3]),
                           reads=pre.s + cw.s, writes=[qkv.s[i]])
                        for (j, so, do, n_) in ((0, 0, 2, L - 2), (1, 0, 1, L - 1), (3, 1, 0, L - 1)):
                            op("dve", lambda h, i=i, j=j, so=so, do=do, n_=n_, ci=ci: h.scalar_tensor_tensor(
                                out=qkv[:, i, do:do + n_], in0=pre[:, so:so + n_], scalar=cw[:, ci, j:j + 1],
                                in1=qkv[:, i, do:do + n_], op0=OP.mult, op1=OP.add),
                               reads=pre.s + cw.s + [qkv.s[i]], writes=[qkv.s[i]])
                        op("act", lambda h, dst=dst: h.activation(out=dst, in_=dst, func=AF.Silu), reads=[qkv.s[i]], writes=[qkv.s[i]])
                        if i < 2:
                            for b0 in range(0, L, 512):
                                bw = min(512, L - b0)
                                p = ps()
                                op("act", lambda h, i=i, b0=b0, bw=bw: h.activation(out=sqb[:, 0:bw], in_=qkv[:, i, b0:b0 + bw], func=AF.Square),
                                   reads=[qkv.s[i]], writes=sqb.s)
                                op("pe", lambda h, p=p, bw=bw: h.matmul(p[:, 0:bw], lhsT=cst(C_ONE), rhs=sqb[:, 0:bw], start=True, stop=True),
                                   reads=sqb.s + CN.s, writes=p.s)
                                op("act", lambda h, p=p, bw=bw: h.activation(out=rsb[:, 0:bw], in_=p[:, 0:bw], func=AF.Sqrt, bias=epsb[:, 0:1], scale=1.0),
                                   reads=p.s + epsb.s, writes=rsb.s)
                                op("dve", lambda h, bw=bw: h.reciprocal(out=rsb[:, 0:bw], in_=rsb[:, 0:bw]), reads=rsb.s, writes=rsb.s)
                                op("dve", lambda h, i=i, b0=b0, bw=bw: h.scalar_tensor_tensor(
                                    out=qkv[:, i, b0:b0 + bw], in0=qkv[:, i, b0:b0 + bw], scalar=(128 ** -0.5 if i == 0 else 1.0),
                                    in1=rsb[:, 0:bw], op0=OP.mult, op1=OP.mult), reads=[qkv.s[i]] + rsb.s, writes=[qkv.s[i]])
                    if "2" in gfl:
                        continue
                    gdn_head(l, t0, L, sidx, hh, qkv, GC, BE, EG, EKD, EGL, BG, gn, (2 if "3" in gfl else nch))

    def gdn_head(l, t0, L, sidx, hh, qkv, GC, BE, EG, EKD, EGL, BG, gn, nch):
        M = lambda name, w=128: T(cx, name, [128, w], F32)
        NAMES = ["S", "qT", "kT", "vT", "qFr", "kFr", "qTr", "kTr", "vTr", "dg", "X1", "dec", "decT", "Lm", "LT",
                 "Pa", "Pb", "PTa", "PTb", "wT", "AT", "qgT", "qgF", "kd", "vn", "osb"]
        WS = [{n_: M(n_ + str(d_)) for n_ in NAMES} for d_ in range(2)]
        for d_ in range(2):
            WS[d_]["B"] = M("B" + str(d_), 256)
        of_, ob_, zc, sz, on, ss, catc = M("of_"), M("ob_"), M("zc"), M("sz"), M("on"), M("ss", 1), M("catc")
        for dr in range(2):
            S = WS[dr]["S"]
            if sidx is None:
                dma("sp", S[:], st_i.ap[l, dr, hh], S.s[0], writes=S.s)
            else:
                op("pool", lambda h: h.memset(S[:], 0.0), writes=S.s)
        for ci in range(nch):
          for dr in range(2):
            W = WS[dr]
            S, qT, kT, vT, qFr, kFr, qTr, kTr, vTr = (W[k_] for k_ in ("S", "qT", "kT", "vT", "qFr", "kFr", "qTr", "kTr", "vTr"))
            dg, X1, dec, decT, Lm, LT, Pa, Pb, PTa, PTb = (W[k_] for k_ in ("dg", "X1", "dec", "decT", "Lm", "LT", "Pa", "Pb", "PTa", "PTb"))
            wT, AT, qgT, qgF, kd, vn, osb, B = (W[k_] for k_ in ("wT", "AT", "qgT", "qgF", "kd", "vn", "osb", "B"))
            gi = dr * 4 + hh
            if True:
                c = ci if dr == 0 else nch - 1 - ci
                cs = slice(c * 128, (c + 1) * 128)
                col = lambda A_: A_[:, c, gi:gi + 1]

                def tr(dst, src_ap, rd):
                    p = ps()
                    op("pe", lambda h: h.transpose(out=p[:, 0:128], in_=src_ap, identity=cst(C_ID)), reads=rd + CN.s, writes=p.s)
                    op("act", lambda h: h.activation(out=dst[:], in_=p[:, 0:128], func=AF.Identity), reads=p.s, writes=dst.s)

                def mm(dst, lhsT, rhs, rd, n=128, eng="act"):
                    p = ps()
                    op("pe", lambda h: h.matmul(p[:, 0:n], lhsT=lhsT, rhs=rhs, start=True, stop=True), reads=rd, writes=p.s)
                    if dst is not None:
                        if eng == "act":
                            op("act", lambda h: h.activation(out=dst[:, 0:n], in_=p[:, 0:n], func=AF.Identity), reads=p.s, writes=dst.s)
                        else:
                            op("dve", lambda h: h.tensor_copy(out=dst[:, 0:n], in_=p[:, 0:n]), reads=p.s, writes=dst.s)
                    return p

                tr(qT, qkv[:, 0, cs], [qkv.s[0]])
                tr(kT, qkv[:, 1, cs], [qkv.s[1]])
                tr(vT, qkv[:, 2, cs], [qkv.s[2]])
                if dr == 0:
                    qF_, kF_, qT_, kT_, vT_ = qkv[:, 0, cs], qkv[:, 1, cs], qT, kT, vT
                    rF = [qkv.s[0], qkv.s[1]]
                else:
                    mm(qFr, qT[:], cst(C_J), qT.s + CN.s)
                    mm(kFr, kT[:], cst(C_J), kT.s + CN.s)
                    mm(qTr, cst(C_J), qT[:], qT.s + CN.s)
                    mm(kTr, cst(C_J), kT[:], kT.s + CN.s)
                    mm(vTr, cst(C_J), vT[:], vT.s + CN.s)
                    qF_, kF_, qT_, kT_, vT_ = qFr[:], kFr[:], qTr, kTr, vTr
                    rF = qFr.s + kFr.s
                op("dve", lambda h: h.tensor_scalar(out=dg[:], in0=cst(C_ID), scalar1=col(GC), scalar2=None, op0=OP.mult),
                   reads=CN.s + GC.s, writes=dg.s)
                prow = ps()
                op("pe", lambda h: h.matmul(prow[:, 0:128], lhsT=cst(C_ONE), rhs=dg[:], start=True, stop=True), reads=dg.s + CN.s, writes=prow.s)
                op("dve", lambda h: h.scalar_tensor_tensor(out=X1[:], in0=prow[:, 0:128], scalar=col(GC), in1=cst(C_MU),
                                                           op0=OP.subtract, op1=OP.add), reads=prow.s + GC.s + CN.s, writes=X1.s)
                op("act", lambda h: h.activation(out=dec[:], in_=X1[:], func=AF.Exp, scale=-1.0), reads=X1.s, writes=dec.s)
                op("dve", lambda h: h.scalar_tensor_tensor(out=X1[:], in0=prow[:, 0:128], scalar=col(GC), in1=cst(C_ML),
                                                           op0=OP.subtract, op1=OP.subtract), reads=prow.s + GC.s + CN.s, writes=X1.s)
                op("act", lambda h: h.activation(out=decT[:], in_=X1[:], func=AF.Exp), reads=X1.s, writes=decT.s)
                pk = ps()
                op("pe", lambda h: h.matmul(pk[:, 0:128], lhsT=kF_, rhs=kF_, start=True, stop=True), reads=rF, writes=pk.s)
                op("dve", lambda h: h.scalar_tensor_tensor(out=Lm[:], in0=pk[:, 0:128], scalar=col(BE), in1=dec[:],
                                                           op0=OP.mult, op1=OP.mult), reads=pk.s + BE.s + dec.s, writes=Lm.s)
                op("pool", lambda h: h.tensor_tensor(out=Lm[:], in0=Lm[:], in1=cst(C_MS), op=OP.mult), reads=Lm.s + CN.s, writes=Lm.s)
                tr(LT, Lm[:], Lm.s)
                op("dve", lambda h: h.tensor_scalar(out=B[:, 0:128], in0=vT_[:], scalar1=col(BE), scalar2=None, op0=OP.mult),
                   reads=vT_.s + BE.s, writes=B.s)
                op("dve", lambda h: h.tensor_scalar(out=B[:, 128:256], in0=kT_[:], scalar1=col(BG), scalar2=None, op0=OP.mult),
                   reads=kT_.s + BG.s, writes=B.s)
                pbb = mm(None, LT[:], B[:], LT.s + B.s, n=256)
                op("dve", lambda h: h.tensor_tensor(out=B[:], in0=B[:], in1=pbb[:, 0:256], op=OP.subtract), reads=B.s + pbb.s, writes=B.s)
                Pc, PTc, Pn, PTn = Lm, LT, Pa, PTa
                for lev in range(6):
                    mm(PTn, Pc[:], PTc[:], Pc.s + PTc.s, eng="dve")
                    if lev < 5:
                        mm(Pn, PTc[:], Pc[:], Pc.s + PTc.s)
                    pbb = mm(None, PTn[:], B[:], PTn.s + B.s, n=256)
                    op("dve", lambda h, pbb=pbb: h.tensor_tensor(out=B[:], in0=B[:], in1=pbb[:, 0:256], op=OP.add), reads=B.s + pbb.s, writes=B.s)
                    Pc, PTc = Pn, PTn
                    Pn, PTn = (Pb, PTb) if Pn is Pa else (Pa, PTa)
                tr(wT, B[:, 128:256], B.s)
                pa_ = ps()
                op("pe", lambda h: h.matmul(pa_[:, 0:128], lhsT=kF_, rhs=qF_, start=True, stop=True), reads=rF, writes=pa_.s)
                op("dve", lambda h: h.tensor_tensor(out=AT[:], in0=pa_[:, 0:128], in1=decT[:], op=OP.mult), reads=pa_.s + decT.s, writes=AT.s)
                op("dve", lambda h: h.tensor_scalar(out=qgT[:], in0=qT_[:], scalar1=col(EG), scalar2=None, op0=OP.mult),
                   reads=qT_.s + EG.s, writes=qgT.s)
                tr(qgF, qgT[:], qgT.s)
                op("pool", lambda h: h.tensor_scalar(out=kd[:], in0=kT_[:], scalar1=col(EKD), scalar2=None, op0=OP.mult),
                   reads=kT_.s + EKD.s, writes=kd.s)
                pw = ps()
                op("pe", lambda h: h.matmul(pw[:, 0:128], lhsT=wT[:], rhs=S[:], start=True, stop=True), reads=wT.s + S.s, writes=pw.s)
                op("dve", lambda h: h.tensor_tensor(out=vn[:], in0=B[:, 0:128], in1=pw[:, 0:128], op=OP.subtract), reads=B.s + pw.s, writes=vn.s)
                po = ps()
                op("pe", lambda h: h.matmul(po[:, 0:128], lhsT=qgF[:], rhs=S[:], start=True, stop=False), reads=qgF.s + S.s, writes=po.s)
                op("pe", lambda h: h.matmul(po[:, 0:128], lhsT=AT[:], rhs=vn[:], start=False, stop=True), reads=AT.s + vn.s, writes=po.s)
                op("act", lambda h: h.activation(out=osb[:], in_=po[:, 0:128], func=AF.Identity), reads=po.s, writes=osb.s)
                pS = ps()
                op("pe", lambda h: h.matmul(pS[:, 0:128], lhsT=kd[:], rhs=vn[:], start=True, stop=True), reads=kd.s + vn.s, writes=pS.s)
                op("dve", lambda h: h.scalar_tensor_tensor(out=S[:], in0=S[:], scalar=col(EGL), in1=pS[:, 0:128],
                                                           op0=OP.mult, op1=OP.add), reads=S.s + EGL.s + pS.s, writes=S.s)
                dst_dr = ofs if dr == 0 else obs
                dma("sp", dst_dr.ap[c * 128:(c + 1) * 128, :], osb[:], osb.s[0], reads=osb.s, writes=dst_dr.s)
        if sidx is not None:
            for dr in range(2):
                S = WS[dr]["S"]
                dma("sp", st_o.ap[sidx, l, dr, hh], S[:], S.s[0], reads=S.s, writes=st_o.s)
        for c in range(nch):
            dma("sp", of_[:], ofs.ap[c * 128:(c + 1) * 128, :], of_.s[0], reads=ofs.s, writes=of_.s)
            dma("sp", ob_[:], obs.ap[c * 128:(c + 1) * 128, :], ob_.s[0], reads=obs.s, writes=ob_.s)
            r0 = O_Z + hh * 128
            dma("sp", zc[:], proj.ap[r0:r0 + 128, t0 + c * 128:t0 + (c + 1) * 128], zc.s[0], reads=tiles_of(t0, L), writes=zc.s)
            pj = ps()
            op("pe", lambda h: h.matmul(pj[:, 0:128], lhsT=cst(C_J), rhs=ob_[:], start=True, stop=True), reads=ob_.s + CN.s, writes=pj.s)
            op("dve", lambda h: h.tensor_tensor(out=of_[:], in0=of_[:], in1=pj[:, 0:128], op=OP.add), reads=of_.s + pj.s, writes=of_.s)
            op("act", lambda h: h.activation(out=on[:], in_=of_[:], func=AF.Square, accum_out=ss[:, 0:1]), reads=of_.s, writes=on.s + ss.s)
            op("act", lambda h: h.activation(out=ss[:], in_=ss[:], func=AF.Sqrt, bias=epsb[:, 0:1], scale=1.0 / 128), reads=ss.s + epsb.s, writes=ss.s)
            op("dve", lambda h: h.reciprocal(out=ss[:], in_=ss[:]), reads=ss.s, writes=ss.s)
            op("dve", lambda h: h.scalar_tensor_tensor(out=on[:], in0=of_[:], scalar=ss[:, 0:1], in1=gn[:], op0=OP.mult, op1=OP.mult),
               reads=of_.s + ss.s + gn.s, writes=on.s)
            op("act", lambda h: h.activation(out=sz[:], in_=zc[:], func=AF.Silu), reads=zc.s, writes=sz.s)
            pt = ps()
            op("pe", lambda h: h.transpose(out=pt[:, 0:128], in_=on[:], identity=cst(C_ID)), reads=on.s + CN.s, writes=pt.s)
            op("dve", lambda h: h.tensor_tensor(out=catc[:], in0=pt[:, 0:128], in1=sz[:], op=OP.mult), reads=pt.s + sz.s, writes=catc.s)
            dma("sp", cat.ap[hh * 128:(hh + 1) * 128, t0 + c * 128:t0 + (c + 1) * 128], catc[:], catc.s[0], reads=catc.s,
                writes=cat_slots(t0, L))

    def attn_seq(l, t0, L, sidx):
        sample = sidx is None
        Lk = L + (PAST if sample else 0)
        nkc = Lk // 128
        QB = min(512, L)
        nsub = QB // 128
        lam_init = 0.8 - 0.6 * float(np.exp(-0.3 * l))
        pr = tiles_of(t0, L)
        with Scope():
            dl = T(cx, "dl", [128, 4, 64], F32)
            dn = T(cx, "dn", [128, 128], F32)
            lam = T(cx, "lam", [128, 4], F32)
            ldump = T(cx, "ldump", [128, 64], F32)
            stg = T(cx, "stg", [128, L], F32)
            stg2 = T(cx, "stg2", [128, 512], F32)
            stg3 = T(cx, "stg3", [128, 512], F32)
            QF = T(cx, "QF", [64, 2, L], BF16, nslots=2)
            KF = T(cx, "KF", [64, 2, Lk], BF16, nslots=2)
            V1 = T(cx, "V1", [128, nkc, 132], BF16)
            PT = T(cx, "PT", [128, 2, 512], BF16, nslots=2)
            ob = T(cx, "ob", [128, 2, 128], F32, nslots=2)
            rs = T(cx, "rs", [128, 2], F32)
            on = T(cx, "onA", [128, 128], F32)
            ss = T(cx, "ssA", [128, 1], F32)
            catc = T(cx, "catcA", [128, 128], F32)
            rp = T(cx, "rp", [64, 2, LS if sample else 2], F32)
            rR = T(cx, "rR", [64, 64], F32)
            dma("sp", dl[:], dlam.ap[l], dl.s[0], writes=dl.s)
            dma("sp", dn[:], gnorm.ap[l, :, 1, :], dn.s[0], writes=dn.s)
            if sample:
                dma("sp", rp[:], ropeS.ap[:, :, :], rp.s[0], writes=rp.s)
                dma("sp", rR[:], ropeR.ap[:, :], rR.s[0], writes=rR.s)
            for i in range(2):
                op("dve", lambda h, i=i: h.tensor_tensor(out=ldump[:], in0=dl[:, 2 * i, :], in1=dl[:, 2 * i + 1, :], op=OP.mult),
                   reads=dl.s, writes=ldump.s)
                op("dve", lambda h, i=i: h.tensor_reduce(out=lam[:, i:i + 1], in_=ldump[:], axis=mybir.AxisListType.X, op=OP.add),
                   reads=ldump.s, writes=lam.s)
            op("act", lambda h: h.activation(out=lam[:, 0:2], in_=lam[:, 0:2], func=AF.Exp), reads=lam.s, writes=lam.s)
            op("dve", lambda h: h.tensor_tensor(out=lam[:, 2:3], in0=lam[:, 1:2], in1=lam[:, 0:1], op=OP.subtract), reads=lam.s, writes=lam.s)
            op("dve", lambda h: h.tensor_scalar(out=lam[:, 2:3], in0=lam[:, 2:3], scalar1=-lam_init, scalar2=None, op0=OP.add), reads=lam.s, writes=lam.s)
            rot["lo"], rot["n"], rot["i"] = 4, 4, 0
            ACC = PS[0:4]
            for hd in range(8):
                for m in range(2):
                    for (which, O_, dst, off) in (("q", O_DQ, QF, 0), ("k", O_DK, KF, Lk - L)):
                        r0 = O_ + hd * 128 + m * 64
                        dma("sp", stg[0:64, :], proj.ap[r0:r0 + 64, t0:t0 + L], stg.s[0], reads=pr, writes=stg.s)
                        if not sample:
                            op("dve", lambda h, dst=dst, m=m, off=off: h.tensor_copy(out=dst[:, m, off:off + L], in_=stg[0:64, :]),
                               reads=stg.s, writes=[dst.s[m]])
                        else:
                            for b0 in range(0, L, 512):
                                p = ps()
                                op("pe", lambda h, p=p, b0=b0: h.matmul(p[0:64, :], lhsT=rR[:], rhs=stg[0:64, b0:b0 + 512], start=True, stop=True),
                                   reads=stg.s + rR.s, writes=p.s)
                                op("pool", lambda h, b0=b0: h.tensor_tensor(out=stg2[0:64, :], in0=stg[0:64, b0:b0 + 512], in1=rp[:, 0, b0:b0 + 512], op=OP.mult),
                                   reads=stg.s + rp.s, writes=stg2.s)
                                op("dve", lambda h, p=p, b0=b0: h.tensor_tensor(out=stg3[0:64, :], in0=p[0:64, :], in1=rp[:, 1, b0:b0 + 512], op=OP.mult),
                                   reads=p.s + rp.s, writes=stg3.s)
                                op("dve", lambda h, dst=dst, m=m, off=off, b0=b0: h.tensor_tensor(
                                    out=dst[:, m, off + b0:off + b0 + 512], in0=stg2[0:64, :], in1=stg3[0:64, :], op=OP.add),
                                   reads=stg2.s + stg3.s, writes=[dst.s[m]])
                    if sample:
                        dma("sp", stg2[0:64, 0:PAST], ckT.ap[l, hd, m], stg2.s[0], writes=stg2.s)
                        op("dve", lambda h, m=m: h.tensor_copy(out=KF[:, m, 0:PAST], in_=stg2[0:64, 0:PAST]), reads=stg2.s, writes=[KF.s[m]])
                r0 = O_DV + hd * 128
                dma("sp", stg[:, :], proj.ap[r0:r0 + 128, t0:t0 + L], stg.s[0], reads=pr, writes=stg.s)
                op("pool", lambda h: h.memset(V1[:, :, 128:132], 1.0), writes=V1.s)
                koff = (PAST // 128) if sample else 0
                if sample:
                    dma("sp", stg2[:, 0:512].rearrange("p (c d) -> p c d", d=128),
                        cvT.ap[l, :, hd, :].rearrange("(c p) d -> p c d", p=128), stg2.s[0], writes=stg2.s)
                    op("dve", lambda h: h.tensor_copy(out=V1[:, 0:4, 0:128], in_=stg2[:, 0:512].rearrange("p (c d) -> p c d", d=128)),
                       reads=stg2.s, writes=V1.s)
                for c in range(L // 128):
                    p = ps()
                    op("pe", lambda h, p=p, c=c: h.transpose(out=p[:, 0:128], in_=stg[:, c * 128:(c + 1) * 128], identity=cst(C_ID)),
                       reads=stg.s + CN.s, writes=p.s)
                    op("act", lambda h, p=p, c=c: h.activation(out=V1[:, koff + c, 0:128], in_=p[:, 0:128], func=AF.Identity), reads=p.s, writes=V1.s)
                for qb in range(L // QB):
                    for s in range(nsub):
                        for m in range(2):
                            pass
                    for m in range(2):
                        for kc in range(nkc):
                            p = ps()
                            sl = kc % 2
                            op("pe", lambda h, p=p, kc=kc, m=m: h.matmul(p[:, 0:QB], lhsT=KF[:, m, kc * 128:(kc + 1) * 128],
                                                                         rhs=QF[:, m, qb * QB:(qb + 1) * QB], start=True, stop=True),
                               reads=[KF.s[m], QF.s[m]], writes=p.s)
                            op("act", lambda h, p=p, sl=sl: h.activation(out=PT[:, sl, 0:QB], in_=p[:, 0:QB], func=AF.Exp,
                                                                         bias=shb[:, 0:1], scale=0.125), reads=p.s + shb.s, writes=[PT.s[sl]])
                            for s in range(nsub):
                                op("pe", lambda h, s=s, sl=sl, kc=kc: h.matmul(ACC[s][:, 0:129], lhsT=PT[:, sl, s * 128:(s + 1) * 128],
                                                                              rhs=V1[:, kc, 0:129], start=(kc == 0), stop=(kc == nkc - 1)),
                                   reads=[PT.s[sl]] + V1.s, writes=ACC[s].s)
                        for s in range(nsub):
                            op("dve", lambda h, s=s, m=m: h.reciprocal(out=rs[:, m:m + 1], in_=ACC[s][:, 128:129]), reads=ACC[s].s, writes=rs.s)
                            if m == 0:
                                op("dve", lambda h, s=s: h.tensor_scalar(out=obq[:, s, :], in0=ACC[s][:, 0:128], scalar1=rs[:, 0:1], scalar2=None, op0=OP.mult),
                                   reads=ACC[s].s + rs.s, writes=[obq.s[s]])
                            else:
                                op("dve", lambda h, s=s: h.tensor_scalar(out=ob[:, 1, :], in0=ACC[s][:, 0:128], scalar1=rs[:, 1:2], scalar2=lam[:, 2:3],
                                                                         op0=OP.mult, op1=OP.mult), reads=ACC[s].s + rs.s + lam.s, writes=[ob.s[1]])
                                op("dve", lambda h, s=s: h.tensor_tensor(out=ob[:, 0, :], in0=obq[:, s, :], in1=ob[:, 1, :], op=OP.add),
                                   reads=[obq.s[s], ob.s[1]], writes=[ob.s[0]])
                                op("act", lambda h: h.activation(out=on[:], in_=ob[:, 0, :], func=AF.Square, accum_out=ss[:, 0:1]), reads=[ob.s[0]], writes=on.s + ss.s)
                                op("act", lambda h: h.activation(out=ss[:], in_=ss[:], func=AF.Sqrt, bias=epsb[:, 0:1], scale=1.0 / 128), reads=ss.s + epsb.s, writes=ss.s)
                                op("dve", lambda h: h.reciprocal(out=ss[:], in_=ss[:]), reads=ss.s, writes=ss.s)
                                op("dve", lambda h: h.scalar_tensor_tensor(out=on[:], in0=ob[:, 0, :], scalar=ss[:, 0:1], in1=dn[:], op0=OP.mult, op1=OP.mult),
                                   reads=[ob.s[0]] + ss.s + dn.s, writes=on.s)
                                pt = ps()
                                op("pe", lambda h, pt=pt: h.transpose(out=pt[:, 0:128], in_=on[:], identity=cst(C_ID)), reads=on.s + CN.s, writes=pt.s)
                                op("act", lambda h, pt=pt: h.activation(out=catc[:], in_=pt[:, 0:128], func=AF.Identity, scale=(1.0 - lam_init)),
                                   reads=pt.s, writes=catc.s)
                                tq = t0 + qb * QB + s * 128
                                dma("sp", cat.ap[512 + hd * 128:512 + (hd + 1) * 128, tq:tq + 128], catc[:], catc.s[0], reads=catc.s,
                                    writes=cat_slots(t0, L))
            rot["lo"], rot["n"], rot["i"] = 0, 8, 0

    obq = None

    def pool_seq(l, t0, L):
        pr = tiles_of(t0, L)
        W = L + 16
        with Scope():
            xp = T(cx, "xp", [128, W], F32)
            wa = T(cx, "wa", [128, W], F32)
            wbf = T(cx, "wbf", [128, W], F32)
            pw = T(cx, "pw", [128, 4, 128], F32)
            psc = T(cx, "psc", [128, 4], F32)
            ped = T(cx, "ped", [128, 4, 16], F32)
            oc_ = T(cx, "oc_", [128, 2, 512], F32, nslots=2)
            dma("sp", pw[:], poolw.ap[l], pw.s[0], writes=pw.s)
            dma("sp", psc[:], pscale.ap[l], psc.s[0], writes=psc.s)
            dma("sp", ped[:], pedge.ap[:, :, :], ped.s[0], writes=ped.s)
            for g in range(4):
                w = 2 << g
                r0 = O_PIN + g * 128
                op("pool", lambda h: h.memset(xp[:], 0.0), writes=xp.s)
                op("pool", lambda h: h.memset(wa[:], 0.0), writes=wa.s)
                op("pool", lambda h: h.memset(wbf[:], 0.0), writes=wbf.s)
                dma("sp", xp[:, 8:8 + L], proj.ap[r0:r0 + 128, t0:t0 + L], xp.s[0], reads=pr, writes=xp.s)
                op("dve", lambda h: h.tensor_tensor(out=wa[:, 1:W], in0=xp[:, 0:W - 1], in1=xp[:, 1:W], op=OP.add), reads=xp.s, writes=wa.s)
                cur, nxt = wa, wbf
                sh = 1
                for lev in range(g):
                    lo, hi = 2 * sh, W - 2 * sh
                    op("dve", lambda h, cur=cur, nxt=nxt, lo=lo, hi=hi, sh=sh: h.tensor_tensor(
                        out=nxt[:, lo:hi], in0=cur[:, lo - sh:hi - sh], in1=cur[:, lo + sh:hi + sh], op=OP.add),
                       reads=cur.s, writes=nxt.s)
                    cur, nxt = nxt, cur
                    sh *= 2
                op("dve", lambda h, cur=cur, g=g: h.tensor_tensor(out=cur[:, 8:16], in0=cur[:, 8:16], in1=ped[:, g, 0:8], op=OP.mult),
                   reads=cur.s + ped.s, writes=cur.s)
                op("dve", lambda h, cur=cur, g=g: h.tensor_tensor(out=cur[:, L:L + 8], in0=cur[:, L:L + 8], in1=ped[:, g, 8:16], op=OP.mult),
                   reads=cur.s + ped.s, writes=cur.s)
                op("dve", lambda h, cur=cur, w=w: h.scalar_tensor_tensor(out=cur[:, 8:8 + L], in0=cur[:, 8:8 + L], scalar=1.0 / w, in1=xp[:, 8:8 + L],
                                                                        op0=OP.mult, op1=OP.subtract), reads=cur.s + xp.s, writes=cur.s)
                for b0 in range(0, L, 512):
                    bw = min(512, L - b0)
                    sl = (b0 // 512) % 2
                    p = ps()
                    op("pe", lambda h, p=p, cur=cur, b0=b0, bw=bw, g=g: h.matmul(p[:, 0:bw], lhsT=pw[:, g, :], rhs=cur[:, 8 + b0:8 + b0 + bw], start=True, stop=True),
                       reads=cur.s + pw.s, writes=p.s)
                    op("act", lambda h, p=p, sl=sl, bw=bw, g=g: h.activation(out=oc_[:, sl, 0:bw], in_=p[:, 0:bw], func=AF.Identity, scale=psc[:, g:g + 1]),
                       reads=p.s + psc.s, writes=[oc_.s[sl]])
                    dma("sp", cat.ap[1536 + g * 128:1536 + (g + 1) * 128, t0 + b0:t0 + b0 + bw], oc_[:, sl, 0:bw], oc_.s[sl], reads=[oc_.s[sl]],
                        writes=cat_slots(t0, L))

    SEQS = [(0, LP, 0), (LP, LP, 1), (NPS * LP, LS, None)]

    def mixer(l):
        nonlocal obq
        fl = dbg.split(":")[1] if mx else "gapPS"
        for (t0, L, sidx) in SEQS:
            if (sidx is None and "S" not in fl) or (sidx is not None and "P" not in fl):
                continue
            if "g" in fl:
                gdn_seq(l, t0, L, sidx)
            if "a" in fl:
                with Scope():
                    obq = T(cx, "obq", [128, 4, 128], F32, nslots=4)
                    attn_seq(l, t0, L, sidx)
            if "p" in fl:
                pool_seq(l, t0, L)

    stage = dbg or "full"
    if mx:
        mixer(0)
        cx.finish(yT.s + ck_o.s + cv_o.s + st_o.s + xs.s + proj.s + cat.s + ofs.s + obs.s + wsc.s)
        return
    ada_layer(0)
    token_phase(0, False, True, True, False)
    if stage != "A0":
        mixer(0)
        if stage == "C0":
            token_phase(1, True, False, False, False)
        elif stage != "M0":
            for l in range(1, DEPTH):
                token_phase(l, True, False, False, False)
                ada_layer(l)
                token_phase(l, False, True, False, False)
                mixer(l)
            token_phase(DEPTH, True, False, False, True)
    cx.finish(yT.s + ck_o.s + cv_o.s + st_o.s + xs.s + proj.s + cat.s + ofs.s + obs.s + wsc.s)


def _consts():
    i = np.arange(128)
    C = np.zeros((NCONST, 128, 128), np.float32)
    C[C_ID] = np.eye(128)
    C[C_ONE] = 1.0
    C[C_U] = (i[:, None] <= i[None, :])
    C[C_UJ] = ((127 - i[:, None]) <= i[None, :])
    C[C_J] = (i[:, None] + i[None, :] == 127)
    C[C_EL] = (i[:, None] == 127)
    C[C_MS] = (i[None, :] < i[:, None])
    C[C_MU] = MASKV * (i[None, :] > i[:, None])
    C[C_ML] = MASKV * (i[None, :] < i[:, None])
    return np.ascontiguousarray(C.transpose(1, 0, 2))


def _rope():
    t = np.arange(LS)
    pos = np.stack([t // 64, t % 64]).astype(np.float32)
    half = 32
    inv = (10000.0 ** (-np.arange(0, half, 2, dtype=np.float32) / half)).astype(np.float32)
    tab = np.zeros((64, 2, LS), np.float32)
    for d in range(64):
        ang = pos[d // 32] * inv[(d % 32) % 16]
        tab[d, 0] = np.cos(ang)
        tab[d, 1] = np.sin(ang)
    R = np.zeros((64, 64), np.float32)
    for base in (0, 32):
        for k in range(16):
            R[base + k, base + k + 16] = -1.0
            R[base + k + 16, base + k] = 1.0
    return tab, np.ascontiguousarray(R.T)


def _pedge():
    E = np.ones((128, 4, 16), np.float32)
    for g, w in enumerate((2, 4, 8, 16)):
        a = w // 2
        b = w - a - 1
        for t in range(8):
            cnt = (t + b + 1) - max(t - a, 0)
            E[:, g, t] = w / cnt
        for r in range(8):
            d = 7 - r
            cnt = min(b, d) + 1 + a
            E[:, g, 8 + r] = w / cnt
    return E


def _fm(v):
    v = np.asarray(v, np.float32)
    n = v.shape[-1] // 128
    r = v.reshape(v.shape[:-1] + (n, 128))
    return np.ascontiguousarray(np.moveaxis(r, -1, 0))


_CACHE = {}
_DBG = {}


def kernel(x_prompt, x_sample, c, cache_k, cache_v, state_gdn, c_ctx, w_ada, b_ada, norm_ffn1, ffn1_in,
           ffn1_out, norm_mix, w_in, gdn_conv, gdn_a_log, gdn_dt_bias, gdn_norm, diff_lam, diff_norm,
           pool_w, pool_scale, w_out, norm_ffn2, ffn2_in, ffn2_out, final_norm, _dbg=None):
    f = lambda a: np.ascontiguousarray(np.asarray(a, np.float32))
    key = _dbg or "full"
    if key not in _CACHE:
        _CACHE[key] = build_program(_dbg)
    nc = _CACHE[key]
    x_prompt, x_sample = f(x_prompt), f(x_sample)
    rope_tab, rope_R = _rope()
    rep = lambda v: np.ascontiguousarray(np.broadcast_to(np.asarray(v, np.float32)[None], (128,) + tuple(np.shape(v))))
    norms = np.stack([_fm(np.stack([norm_ffn1[l], norm_mix[l], norm_ffn2[l]])) for l in range(DEPTH)], 1)
    norms = np.concatenate([norms.reshape(128, DEPTH * 3, KC), _fm(final_norm)[:, None, :]], 1)
    shared = {
        "w_ada": f(w_ada), "b_ada": np.ascontiguousarray(_fm(b_ada).transpose(1, 0, 2)),
        "norms": np.ascontiguousarray(norms),
        "ffn1_in": f(ffn1_in), "ffn2_in": f(ffn2_in), "ffn1_out": f(ffn1_out), "ffn2_out": f(ffn2_out),
        "w_in": f(w_in), "w_out": f(w_out), "consts": _consts(),
        "convw": np.ascontiguousarray(f(gdn_conv).reshape(DEPTH, 4, 12, 128).transpose(0, 3, 2, 1)),
        "gpar": np.ascontiguousarray(np.stack([rep(np.stack([f(gdn_a_log)[l].reshape(8), f(gdn_dt_bias)[l].reshape(8)])) for l in range(DEPTH)])),
        "gnorm": np.ascontiguousarray(np.stack([rep(np.stack([f(gdn_norm)[l], f(diff_norm)[l]])) for l in range(DEPTH)])),
        "dlam": np.ascontiguousarray(np.stack([rep(f(diff_lam)[l]) for l in range(DEPTH)])),
        "poolw": np.ascontiguousarray(f(pool_w).transpose(0, 2, 1, 3)),
        "pscale": np.ascontiguousarray(f(pool_scale).reshape(DEPTH, 4, 128).transpose(0, 2, 1)),
        "pedge": _pedge(), "ropeS": rope_tab, "ropeR": rope_R,
    }
    mx = isinstance(_dbg, str) and _dbg.startswith("MX")
    if mx:
        for k_ in ("w_ada", "ffn1_in", "ffn2_in", "ffn1_out", "ffn2_out", "w_in", "w_out"):
            shared[k_] = np.zeros((1, 1, 1), np.float32)
        shared["proj"] = _DBG["proj"]
    in_maps = []
    for r in range(8):
        b = r // 4
        xcat = np.concatenate([x_prompt[2 * r], x_prompt[2 * r + 1], x_sample[b]], 0)
        m = dict(shared)
        m["xT"] = np.ascontiguousarray(xcat.T)
        m["cT"] = np.ascontiguousarray(np.stack([_fm(c_ctx), _fm(f(c)[b])], -1))
        ck = f(cache_k)[b].reshape(DEPTH, PAST, 8, 2, 64)
        m["ckT"] = np.ascontiguousarray(ck.transpose(0, 2, 3, 4, 1))
        m["cvT"] = f(cache_v)[b]
        m["st_i"] = f(state_gdn)[b]
        in_maps.append(m)
    res = run_bass_kernel_spmd(nc, in_maps, core_ids=list(range(8))).results
    if mx:
        _DBG["cat"] = res[0]["cat"]
    y_prompt = np.zeros((16, LP, D), np.float32)
    y_sample = np.zeros((2, LS, D), np.float32)
    nck = np.zeros((16, DEPTH, LP, 8, 128), np.float32)
    ncv = np.zeros((16, DEPTH, LP, 8, 128), np.float32)
    nst = np.zeros((16, DEPTH, 2, 4, 128, 128), np.float32)
    for r in range(8):
        o = res[r]
        yt = o["yT"]
        for s in range(NPS):
            y_prompt[2 * r + s] = yt[:, s * LP:(s + 1) * LP].T
            nck[2 * r + s] = o["ck_o"][:, :, s * LP:(s + 1) * LP].reshape(DEPTH, 8, 128, LP).transpose(0, 3, 1, 2)
            ncv[2 * r + s] = o["cv_o"][:, :, s * LP:(s + 1) * LP].reshape(DEPTH, 8, 128, LP).transpose(0, 3, 1, 2)
            nst[2 * r + s] = o["st_o"][s]
        if r % 4 == 0:
            y_sample[r // 4] = yt[:, NPS * LP:].T
    return (y_prompt, y_sample, nck, ncv, nst)
```

```python
import numpy as np
from contextlib import ExitStack
import concourse.bass as bass
import concourse.mybir as mybir
from concourse.bass_utils import run_bass_kernel_spmd

F32 = mybir.dt.float32
BF16 = mybir.dt.bfloat16
AF = mybir.ActivationFunctionType
OP = mybir.AluOpType

D = 2048
DEPTH = 2
NPS = 2
LP = 256
LS = 4096
PAST = 512
NT = NPS * LP + LS
TT = 512
NTILE = NT // TT
DFF = 5632
NFC = DFF // 128
INC = 5648
INCP = 45 * 128
KC = 16
EPS = 1e-6
O_QKV, O_Z, O_DQ, O_DK, O_DV, O_PIN, O_A, O_B = 0, 1536, 2048, 3072, 4096, 5120, 5632, 5640
MASKV = 30000.0


class Slot:
    __slots__ = ("w", "r", "dsem", "dcnt", "name")

    def __init__(self, name):
        self.w = None
        self.r = {}
        self.dsem = None
        self.dcnt = 0
        self.name = name


class Eng:
    def __init__(self, name, h):
        self.name = name
        self.h = h
        self.sem = None
        self.n = 0
        self.seen = {}


class Ctx:
    def __init__(self, nc, stack):
        self.nc = nc
        self.stack = stack
        self.semstack = stack
        self.E = {
            "pe": Eng("pe", nc.tensor),
            "act": Eng("act", nc.scalar),
            "dve": Eng("dve", nc.vector),
            "pool": Eng("pool", nc.gpsimd),
            "sp": Eng("sp", nc.sync),
        }
        self.sems = []
        self.ninst = 0
        self.free_dsems = []
        self.scopes = []

    def newsem(self, name):
        h = self.semstack.enter_context(self.nc.semaphore(f"{name}_{len(self.sems)}"))
        self.sems.append(h)
        return len(self.sems) - 1

    def _wait(self, e, deps):
        for si, (val, eng) in deps.items():
            if e.name == "pe" and eng == "pe":
                continue
            if e.seen.get(si, 0) < val:
                e.h.wait_ge(self.sems[si], val)
                e.seen[si] = val

    @staticmethod
    def _add(deps, ev):
        si, val, eng = ev
        if si not in deps or deps[si][0] < val:
            deps[si] = (val, eng)

    def _deps(self, reads, writes):
        deps = {}
        for b in reads:
            if b.w is not None:
                self._add(deps, b.w)
        for b in writes:
            if b.w is not None:
                self._add(deps, b.w)
            for si, (val, eng) in b.r.items():
                self._add(deps, (si, val, eng))
        return deps

    def _record(self, ev, reads, writes):
        si, val, eng = ev
        for b in reads:
            if si not in b.r or b.r[si][0] < val:
                b.r[si] = (val, eng)
        for b in writes:
            b.w = ev
            b.r = {}
        self.ninst += 1

    def op(self, eng, fn, reads=(), writes=()):
        e = self.E[eng]
        if e.sem is None or e.n >= 15000:
            e.sem = self.newsem(eng)
            e.n = 0
        self._wait(e, self._deps(reads, writes))
        inst = fn(e.h)
        e.n += 1
        inst.then_inc(self.sems[e.sem], 1)
        ev = (e.sem, e.n, eng)
        self._record(ev, reads, writes)
        return ev

    def dma(self, q, out, in_, owner, reads=(), writes=(), **kw):
        e = self.E[q]
        if owner.dsem is None:
            if self.free_dsems:
                owner.dsem, owner.dcnt = self.free_dsems.pop()
            else:
                owner.dsem = self.newsem("d")
                owner.dcnt = 0
        deps = self._deps(reads, writes)
        if owner.dcnt > 0:
            self._add(deps, (owner.dsem, 16 * owner.dcnt, "dma"))
        self._wait(e, deps)
        if owner.dcnt >= 900:
            owner.dsem = self.newsem("d")
            owner.dcnt = 0
        inst = e.h.dma_start(out=out, in_=in_, **kw)
        owner.dcnt += 1
        inst.then_inc(self.sems[owner.dsem], 16)
        ev = (owner.dsem, 16 * owner.dcnt, "dma")
        self._record(ev, reads, writes)
        return ev

    def finish(self, slots):
        e = self.E["sp"]
        deps = {}
        for b in slots:
            if b.w is not None:
                self._add(deps, b.w)
            for si, (val, eng) in b.r.items():
                self._add(deps, (si, val, eng))
        self._wait(e, deps)


class T:
    cnt = 0

    def __init__(self, cx, name, shape, dtype, nslots=1, psum=False):
        alloc = cx.nc.psum_tensor if psum else cx.nc.sbuf_tensor
        T.cnt += 1
        self.t = cx.stack.enter_context(alloc(f"{name}_{T.cnt}", list(shape), dtype))
        self.s = [Slot(f"{name}{i}") for i in range(nslots)]
        self.name = name
        if cx.scopes:
            cx.scopes[-1].append(self)

    def __getitem__(self, idx):
        return self.t[idx]


class DR:
    def __init__(self, cx, name, shape, dtype, kind=None, nslots=1):
        if kind is None:
            self.ap = cx.nc.dram_tensor(name, list(shape), dtype).ap()
        else:
            self.ap = cx.nc.dram_tensor(name, list(shape), dtype, kind=kind).ap()
        self.s = [Slot(f"{name}{i}") for i in range(nslots)]


def build_program(dbg=False):
    nc = bass.Bass("TRN2", target_bir_lowering=False)
    stack = ExitStack()
    cx = Ctx(nc, stack)
    with stack:
        _emit(nc, cx, dbg)
    return nc


C_ID, C_ONE, C_U, C_UJ, C_J, C_EL, C_MS, C_MU, C_ML = range(9)
NCONST = 9
SHIFT = 10.0


def _emit(nc, cx, dbg):
    op, dma = cx.op, cx.dma
    IN = lambda name, shape, dt=F32: DR(cx, name, shape, dt, kind="ExternalInput")
    OUT = lambda name, shape, dt=F32: DR(cx, name, shape, dt, kind="ExternalOutput")
    mx = isinstance(dbg, str) and dbg.startswith("MX")
    big = (lambda sh: [1, 1, 1]) if mx else (lambda sh: sh)
    xT = IN("xT", [D, NT])
    cT = IN("cT", [128, KC, 2])
    w_ada = IN("w_ada", big([DEPTH, D, 9 * D]))
    b_ada = IN("b_ada", [DEPTH, 128, 144])
    norms = IN("norms", [128, DEPTH * 3 + 1, KC])
    ffn_in = [IN("ffn1_in", big([DEPTH, D, 2 * DFF])), IN("ffn2_in", big([DEPTH, D, 2 * DFF]))]
    ffn_out = [IN("ffn1_out", big([DEPTH, DFF, D])), IN("ffn2_out", big([DEPTH, DFF, D]))]
    w_in = IN("w_in", big([DEPTH, D, INC]))
    w_out = IN("w_out", big([DEPTH, D, D]))
    consts = IN("consts", [128, NCONST, 128])
    convw = IN("convw", [DEPTH, 128, 12, 4])
    gpar = IN("gpar", [DEPTH, 128, 2, 8])
    gnorm = IN("gnorm", [DEPTH, 128, 2, 128])
    dlam = IN("dlam", [DEPTH, 128, 4, 64])
    poolw = IN("poolw", [DEPTH, 128, 4, 128])
    pscale = IN("pscale", [DEPTH, 128, 4])
    pedge = IN("pedge", [128, 4, 16])
    ropeS = IN("ropeS", [64, 2, LS])
    ropeR = IN("ropeR", [64, 64])
    ckT = IN("ckT", [DEPTH, 8, 2, 64, PAST])
    cvT = IN("cvT", [DEPTH, PAST, 8, 128])
    st_i = IN("st_i", [DEPTH, 2, 4, 128, 128])
    yT = OUT("yT", [D, NT])
    ck_o = OUT("ck_o", [DEPTH, 1024, NPS * LP])
    cv_o = OUT("cv_o", [DEPTH, 1024, NPS * LP])
    st_o = OUT("st_o", [NPS, DEPTH, 2, 4, 128, 128])
    xs = DR(cx, "xs", [D, NT], F32, nslots=NTILE)
    proj = DR(cx, "proj", [INCP, NT], F32, nslots=NTILE, kind=("ExternalInput" if mx else None))
    cat = DR(cx, "cat", [D, NT], F32, nslots=NTILE, kind=("ExternalOutput" if mx else None))
    ofs = DR(cx, "ofs", [LS, 128], F32)
    obs = DR(cx, "obs", [LS, 128], F32)
    wsc = DR(cx, "wsc", [NFC + KC + 45 + KC, 128, NFC * 128], BF16, nslots=NFC + KC + 45 + KC)

    ones_b = T(cx, "ones_b", [128, 128], BF16)
    ada = T(cx, "ada", [128, 144, 2], F32)
    nrm = T(cx, "nrm", [128, DEPTH * 3 + 1, KC], F32)
    cond = T(cx, "cond", [128, KC, 2], F32)
    badat = T(cx, "badat", [128, 144], F32)
    modA = T(cx, "modA", [128, 3, KC, 2], F32)
    gate5 = T(cx, "gate5", [128, 3, KC, 2], F32)
    epsb = T(cx, "epsb", [128, 1], F32)
    oneb = T(cx, "oneb", [128, 1], F32)
    shb = T(cx, "shb", [128, 1], F32)
    CN = T(cx, "CN", [128, NCONST, 128], F32)
    PS = [T(cx, f"ps{i}", [128, 512], F32, psum=True) for i in range(8)]
    rot = {"lo": 0, "n": 8, "i": 0}

    def ps():
        rot["i"] = (rot["i"] + 1) % rot["n"]
        return PS[rot["lo"] + rot["i"]]

    def cst(i):
        return CN[:, i, :]

    op("pool", lambda h: h.memset(ones_b[:], 1.0), writes=ones_b.s)
    op("pool", lambda h: h.memset(epsb[:], EPS), writes=epsb.s)
    op("pool", lambda h: h.memset(oneb[:], 1.0), writes=oneb.s)
    op("pool", lambda h: h.memset(shb[:], -SHIFT), writes=shb.s)
    dma("sp", nrm[:], norms.ap[:, :, :], nrm.s[0], writes=nrm.s)
    dma("sp", cond[:], cT.ap[:, :, :], cond.s[0], writes=cond.s)
    dma("sp", CN[:], consts.ap[:, :, :], CN.s[0], writes=CN.s)
    op("act", lambda h: h.activation(out=cond[:], in_=cond[:], func=AF.Silu), reads=cond.s, writes=cond.s)

    class Scope:
        def __enter__(self):
            self.st = ExitStack()
            self.old = cx.stack
            cx.stack = self.st
            self.st.__enter__()
            cx.scopes.append([])
            return self

        def __exit__(self, *a):
            sc_ = cx.scopes.pop()
            deps_ = {}
            for t_ in sc_:
                for sl_ in t_.s:
                    if sl_.w is not None:
                        cx._add(deps_, sl_.w)
                    for si_, (v_, e_) in sl_.r.items():
                        cx._add(deps_, (si_, v_, e_))
            for en_ in cx.E.values():
                for si_, (v_, e_) in deps_.items():
                    if en_.seen.get(si_, 0) < v_:
                        en_.h.wait_ge(cx.sems[si_], v_)
                        en_.seen[si_] = v_
            for t_ in sc_:
                for sl_ in t_.s:
                    if sl_.dsem is not None:
                        cx.free_dsems.append((sl_.dsem, sl_.dcnt))
            cx.stack = self.old
            return self.st.__exit__(*a)

    def ada_layer(l):
        with Scope():
            wst = T(cx, f"wada{l}", [128, 2, KC, 512], F32, nslots=2)
            dma("sp", badat[:], b_ada.ap[l], badat.s[0], writes=badat.s)
            for blk in range(9 * D // 512):
                sl = blk % 2
                src = w_ada.ap[l, :, blk * 512:(blk + 1) * 512].rearrange("(k p) n -> p k n", p=128)
                dma("sp", wst[:, sl, :, :], src, wst.s[sl], writes=[wst.s[sl]])
                for j in range(4):
                    ch = blk * 4 + j
                    p = ps()
                    for k in range(KC):
                        op("pe", lambda h, k=k, j=j, p=p, sl=sl: h.matmul(
                            p[:, 0:2], lhsT=wst[:, sl, k, j * 128:(j + 1) * 128], rhs=cond[:, k, :],
                            start=(k == 0), stop=(k == KC - 1)),
                           reads=[wst.s[sl], cond.s[0]], writes=p.s)
                    op("dve", lambda h, p=p, ch=ch: h.tensor_scalar(
                        out=ada[:, ch, :], in0=p[:, 0:2], scalar1=badat[:, ch:ch + 1], scalar2=None, op0=OP.add),
                       reads=[p.s[0], badat.s[0]], writes=ada.s)
        for n in range(3):
            for c in range(2):
                op("dve", lambda h, n=n, c=c: h.scalar_tensor_tensor(
                    out=modA[:, n, :, c], in0=ada[:, (3 * n + 1) * KC:(3 * n + 2) * KC, c], scalar=1.0,
                    in1=nrm[:, l * 3 + n, :], op0=OP.add, op1=OP.mult), reads=[ada.s[0], nrm.s[0]], writes=modA.s)
            op("dve", lambda h, n=n: h.tensor_scalar(
                out=gate5[:, n, :, :], in0=ada[:, (3 * n + 2) * KC:(3 * n + 3) * KC, :],
                scalar1=(1.0 if n == 1 else 0.5), scalar2=None, op0=OP.mult),
               reads=[ada.s[0]], writes=gate5.s)

    def modnorm(x, h_, sq, rb, n, c, final=False):
        p = ps()
        for k in range(KC):
            op("act", lambda h, k=k: h.activation(out=sq[:, k % 2, :], in_=x[:, k, :], func=AF.Square),
               reads=x.s, writes=[sq.s[k % 2]])
            op("pe", lambda h, k=k, p=p: h.matmul(p[:, :], lhsT=ones_b[:], rhs=sq[:, k % 2, :],
                                                  start=(k == 0), stop=(k == KC - 1)),
               reads=[sq.s[k % 2], ones_b.s[0]], writes=p.s)
        op("act", lambda h, p=p: h.activation(out=rb[:], in_=p[:, :], func=AF.Sqrt, bias=epsb[:, 0:1], scale=1.0 / D),
           reads=[p.s[0], epsb.s[0]], writes=rb.s)
        op("dve", lambda h: h.reciprocal(out=rb[:], in_=rb[:]), reads=rb.s, writes=rb.s)
        for k in range(KC):
            if not final:
                op("dve", lambda h, k=k: h.scalar_tensor_tensor(
                    out=sq[:, 2 + k % 2, :], in0=x[:, k, :], scalar=modA[:, n, k, c:c + 1], in1=rb[:],
                    op0=OP.mult, op1=OP.mult), reads=[x.s[0], rb.s[0], modA.s[0]], writes=[sq.s[2 + k % 2]])
                op("act", lambda h, k=k: h.activation(
                    out=h_[:, k, :], in_=sq[:, 2 + k % 2, :], func=AF.Identity,
                    bias=ada[:, 3 * n * KC + k, c:c + 1], scale=1.0),
                   reads=[sq.s[2 + k % 2], ada.s[0]], writes=h_.s)
            else:
                op("dve", lambda h, k=k: h.scalar_tensor_tensor(
                    out=x[:, k, :], in0=x[:, k, :], scalar=nrm[:, DEPTH * 3, k:k + 1], in1=rb[:],
                    op0=OP.mult, op1=OP.mult), reads=[x.s[0], rb.s[0], nrm.s[0]], writes=x.s)

    def ffn(l, which, x, h_, act, wstage, wb, sg, c, t):
        wi = ffn_in[which].ap
        wo = ffn_out[which].ap
        gi = 0 if which == 0 else 2
        for j in range(NFC):
            sl = j % 2
            if t == 0:
                for half in range(2):
                    col = half * DFF + j * 128
                    src = wi[l, :, col:col + 128].rearrange("(k p) n -> p k n", p=128)
                    dma("sp", wstage[:, sl, half * 2048:(half + 1) * 2048].rearrange("p (k n) -> p k n", n=128), src,
                        wstage.s[sl], writes=[wstage.s[sl]])
                ce = "pool" if j % 2 == 0 else "dve"
                op(ce, lambda h, sl=sl: h.tensor_copy(out=wb[:, sl, 0:4096], in_=wstage[:, sl, 0:4096]),
                   reads=[wstage.s[sl]], writes=[wb.s[sl]])
                dma("sp", wsc.ap[j, :, 0:4096], wb[:, sl, 0:4096], wb.s[sl], reads=[wb.s[sl]], writes=[wsc.s[j]])
            else:
                dma("sp", wb[:, sl, 0:4096], wsc.ap[j, :, 0:4096], wb.s[sl], reads=[wsc.s[j]], writes=[wb.s[sl]])
            pg, pu = ps(), ps()
            for half, p in ((0, pg), (1, pu)):
                for k in range(KC):
                    o0 = half * 2048 + k * 128
                    op("pe", lambda h, p=p, k=k, o0=o0, sl=sl: h.matmul(
                        p[:, :], lhsT=wb[:, sl, o0:o0 + 128], rhs=h_[:, k, :], start=(k == 0), stop=(k == KC - 1)),
                       reads=[wb.s[sl], h_.s[0]], writes=p.s)
            op("act", lambda h, pg=pg, sl=sl: h.activation(out=sg[:, sl, :], in_=pg[:, :], func=AF.Silu),
               reads=pg.s, writes=[sg.s[sl]])
            op("dve", lambda h, pu=pu, j=j, sl=sl: h.tensor_tensor(out=act[:, j, :], in0=sg[:, sl, :], in1=pu[:, :], op=OP.mult),
               reads=[sg.s[sl], pu.s[0]], writes=act.s)
        for cc in range(KC):
            sl = cc % 2
            if t == 0:
                src = wo[l, :, cc * 128:(cc + 1) * 128].rearrange("(j p) n -> p j n", p=128)
                dma("sp", wstage[:, sl, 0:NFC * 128].rearrange("p (j n) -> p j n", n=128), src, wstage.s[sl],
                    writes=[wstage.s[sl]])
                ce = "pool" if cc % 2 == 0 else "dve"
                op(ce, lambda h, sl=sl: h.tensor_copy(out=wb[:, sl, 0:NFC * 128], in_=wstage[:, sl, 0:NFC * 128]),
                   reads=[wstage.s[sl]], writes=[wb.s[sl]])
                dma("sp", wsc.ap[NFC + cc, :, :], wb[:, sl, 0:NFC * 128], wb.s[sl], reads=[wb.s[sl]], writes=[wsc.s[NFC + cc]])
            else:
                dma("sp", wb[:, sl, 0:NFC * 128], wsc.ap[NFC + cc, :, :], wb.s[sl], reads=[wsc.s[NFC + cc]], writes=[wb.s[sl]])
            p = ps()
            for j in range(NFC):
                op("pe", lambda h, p=p, j=j, sl=sl: h.matmul(
                    p[:, :], lhsT=wb[:, sl, j * 128:(j + 1) * 128], rhs=act[:, j, :], start=(j == 0), stop=(j == NFC - 1)),
                   reads=[wb.s[sl], act.s[0]], writes=p.s)
            op("dve", lambda h, p=p, cc=cc: h.scalar_tensor_tensor(
                out=x[:, cc, :], in0=p[:, :], scalar=gate5[:, gi, cc, c:c + 1], in1=x[:, cc, :], op0=OP.mult, op1=OP.add),
               reads=[p.s[0], gate5.s[0], x.s[0]], writes=x.s)

    def token_phase(l, do_c, do_a, first, last):
        with Scope():
            x = T(cx, "xA", [128, KC, TT], F32)
            h_ = T(cx, "hA", [128, KC, TT], BF16)
            act = T(cx, "actA", [128, NFC, TT], BF16)
            sq = T(cx, "sqA", [128, 4, TT], BF16, nslots=4)
            sg = T(cx, "sgA", [128, 2, TT], F32, nslots=2)
            rb = T(cx, "rbA", [128, TT], F32)
            wstage = T(cx, "wstA", [128, 2, NFC * 128], F32, nslots=2)
            wb = T(cx, "wbA", [128, 2, NFC * 128], BF16, nslots=2)
            ev = T(cx, "evA", [128, 2, TT], F32, nslots=2)
            for t in range(NTILE):
                c = 0 if t == 0 else 1
                cols = slice(t * TT, (t + 1) * TT)
                if first:
                    dma("sp", x[:], xT.ap[:, cols].rearrange("(k p) n -> p k n", p=128), x.s[0], writes=x.s)
                else:
                    dma("sp", x[:], xs.ap[:, cols].rearrange("(k p) n -> p k n", p=128), x.s[0], reads=[xs.s[t]], writes=x.s)
                if do_c:
                    lc = l - 1
                    for k in range(KC):
                        sl = k % 2
                        dma("sp", ev[:, sl, :], cat.ap[k * 128:(k + 1) * 128, cols], ev.s[sl], reads=[cat.s[t]], writes=[ev.s[sl]])
                        op("pool", lambda h, k=k, sl=sl: h.tensor_copy(out=h_[:, k, :], in_=ev[:, sl, :]),
                           reads=[ev.s[sl]], writes=h_.s)
                    for cc in range(KC):
                        sl = cc % 2
                        bi = NFC + KC + 45 + cc
                        if t == 0:
                            src = w_out.ap[lc, :, cc * 128:(cc + 1) * 128].rearrange("(k p) n -> p k n", p=128)
                            dma("sp", wstage[:, sl, 0:KC * 128].rearrange("p (k n) -> p k n", n=128), src, wstage.s[sl],
                                writes=[wstage.s[sl]])
                            op("pool", lambda h, sl=sl: h.tensor_copy(out=wb[:, sl, 0:KC * 128], in_=wstage[:, sl, 0:KC * 128]),
                               reads=[wstage.s[sl]], writes=[wb.s[sl]])
                            dma("sp", wsc.ap[bi, :, 0:KC * 128], wb[:, sl, 0:KC * 128], wb.s[sl], reads=[wb.s[sl]], writes=[wsc.s[bi]])
                        else:
                            dma("sp", wb[:, sl, 0:KC * 128], wsc.ap[bi, :, 0:KC * 128], wb.s[sl], reads=[wsc.s[bi]], writes=[wb.s[sl]])
                        p = ps()
                        for k in range(KC):
                            op("pe", lambda h, p=p, k=k, sl=sl: h.matmul(
                                p[:, :], lhsT=wb[:, sl, k * 128:(k + 1) * 128], rhs=h_[:, k, :],
                                start=(k == 0), stop=(k == KC - 1)), reads=[wb.s[sl], h_.s[0]], writes=p.s)
                        op("dve", lambda h, p=p, cc=cc: h.scalar_tensor_tensor(
                            out=x[:, cc, :], in0=p[:, :], scalar=gate5[:, 1, cc, c:c + 1], in1=x[:, cc, :],
                            op0=OP.mult, op1=OP.add), reads=[p.s[0], gate5.s[0], x.s[0]], writes=x.s)
                    modnorm(x, h_, sq, rb, 2, c)
                    ffn(lc, 1, x, h_, act, wstage, wb, sg, c, t)
                    if last:
                        modnorm(x, None, sq, rb, 0, c, final=True)
                        dma("sp", yT.ap[:, cols].rearrange("(k p) n -> p k n", p=128), x[:], x.s[0], reads=x.s, writes=yT.s)
                    else:
                        dma("sp", xs.ap[:, cols].rearrange("(k p) n -> p k n", p=128), x[:], x.s[0], reads=x.s, writes=[xs.s[t]])
                if do_a:
                    modnorm(x, h_, sq, rb, 0, c)
                    ffn(l, 0, x, h_, act, wstage, wb, sg, c, t)
                    modnorm(x, h_, sq, rb, 1, c)
                    dma("sp", xs.ap[:, cols].rearrange("(k p) n -> p k n", p=128), x[:], x.s[0], reads=x.s, writes=[xs.s[t]])
                    for oc in range(45):
                        ncol = 128 if oc < 44 else 16
                        sl = oc % 2
                        oc0 = oc * 128 if oc < 16 else (2064 + (oc - 16) * 128 if oc < 44 else 2048)
                        bi = NFC + KC + oc
                        if t == 0:
                            src = w_in.ap[l, :, oc0:oc0 + ncol].rearrange("(k p) n -> p k n", p=128)
                            dma("sp", wstage[:, sl, 0:KC * ncol].rearrange("p (k n) -> p k n", n=ncol), src, wstage.s[sl],
                                writes=[wstage.s[sl]])
                            op("pool", lambda h, sl=sl, ncol=ncol: h.tensor_copy(out=wb[:, sl, 0:KC * ncol], in_=wstage[:, sl, 0:KC * ncol]),
                               reads=[wstage.s[sl]], writes=[wb.s[sl]])
                            dma("sp", wsc.ap[bi, :, 0:KC * ncol], wb[:, sl, 0:KC * ncol], wb.s[sl], reads=[wb.s[sl]], writes=[wsc.s[bi]])
                        else:
                            dma("sp", wb[:, sl, 0:KC * ncol], wsc.ap[bi, :, 0:KC * ncol], wb.s[sl], reads=[wsc.s[bi]], writes=[wb.s[sl]])
                        p = ps()
                        for k in range(KC):
                            op("pe", lambda h, p=p, k=k, sl=sl, ncol=ncol: h.matmul(
                                p[0:ncol, :], lhsT=wb[:, sl, k * ncol:(k + 1) * ncol], rhs=h_[:, k, :],
                                start=(k == 0), stop=(k == KC - 1)), reads=[wb.s[sl], h_.s[0]], writes=p.s)
                        op("act", lambda h, p=p, sl=sl, ncol=ncol: h.activation(out=ev[0:ncol, sl, :], in_=p[0:ncol, :], func=AF.Identity),
                           reads=p.s, writes=[ev.s[sl]])
                        dma("sp", proj.ap[oc * 128:oc * 128 + ncol, cols], ev[0:ncol, sl, :], ev.s[sl], reads=[ev.s[sl]],
                            writes=[proj.s[t]])
                        if t == 0:
                            r0 = oc * 128
                            if O_DK <= r0 < O_DK + 1024:
                                dma("sp", ck_o.ap[l, r0 - O_DK:r0 - O_DK + 128, :], ev[:, sl, :], ev.s[sl], reads=[ev.s[sl]],
                                    writes=ck_o.s)
                            if O_DV <= r0 < O_DV + 1024:
                                dma("sp", cv_o.ap[l, r0 - O_DV:r0 - O_DV + 128, :], ev[:, sl, :], ev.s[sl], reads=[ev.s[sl]],
                                    writes=cv_o.s)

    def tiles_of(t0, L):
        return [proj.s[t] for t in range(t0 // TT, (t0 + L + TT - 1) // TT)]

    def cat_slots(t0, L):
        return [cat.s[t] for t in range(t0 // TT, (t0 + L + TT - 1) // TT)]

    def gdn_seq(l, t0, L, sidx):
        nch = L // 128
        NG = nch * 8
        pr = tiles_of(t0, L)
        with Scope():
            abt = T(cx, "abt", [16, L], F32)
            gp = T(cx, "gp", [128, 2, 8], F32)
            gn = T(cx, "gn", [128, 128], F32)
            cw = T(cx, "cw", [128, 12, 4], F32)
            nA = T(cx, "nA", [128, 8], F32)
            tmp = T(cx, "gtmp", [128, nch, 8], F32)
            gcol = T(cx, "gcol", [128, nch, 8], F32)
            bcol = T(cx, "bcol", [128, nch, 8], F32)
            GC = T(cx, "GC", [128, nch, 8], F32)
            BE = T(cx, "BE", [128, nch, 8], F32)
            EG = T(cx, "EG", [128, nch, 8], F32)
            EKD = T(cx, "EKD", [128, nch, 8], F32)
            EGL = T(cx, "EGL", [128, nch, 8], F32)
            BG = T(cx, "BG", [128, nch, 8], F32)
            dma("sp", abt[:], proj.ap[O_A:O_A + 16, t0:t0 + L], abt.s[0], reads=pr, writes=abt.s)
            dma("sp", gp[:], gpar.ap[l], gp.s[0], writes=gp.s)
            dma("sp", gn[:], gnorm.ap[l, :, 0, :], gn.s[0], writes=gn.s)
            dma("sp", cw[:], convw.ap[l], cw.s[0], writes=cw.s)
            op("act", lambda h: h.activation(out=nA[:], in_=gp[:, 0, :], func=AF.Exp), reads=gp.s, writes=nA.s)
            op("dve", lambda h: h.tensor_scalar(out=nA[:], in0=nA[:], scalar1=-1.0, scalar2=None, op0=OP.mult),
               reads=nA.s, writes=nA.s)
            for cp in range(nch // 2):
                c0, c1, n = 2 * cp, 2 * cp + 2, 2
                pa = ps()
                for c in range(c0, c1):
                    op("pe", lambda h, c=c, pa=pa: h.transpose(out=pa[:, (c - c0) * 16:(c - c0 + 1) * 16],
                                                               in_=abt[0:16, c * 128:(c + 1) * 128], identity=CN[0:16, C_ID, 0:16]),
                       reads=[abt.s[0], CN.s[0]], writes=pa.s)
                pav = pa[:, 0:n * 16].rearrange("p (c k) -> p c k", k=16)
                op("dve", lambda h, pav=pav, c0=c0, c1=c1, n=n: h.tensor_tensor(
                    out=tmp[:, c0:c1, :], in0=pav[:, :, 0:8], in1=gp[:, 1:2, :].to_broadcast([128, n, 8]), op=OP.add),
                   reads=[pa.s[0], gp.s[0]], writes=tmp.s)
                op("act", lambda h, c0=c0, c1=c1: h.activation(out=tmp[:, c0:c1, :], in_=tmp[:, c0:c1, :], func=AF.Exp),
                   reads=tmp.s, writes=tmp.s)
                op("act", lambda h, c0=c0, c1=c1: h.activation(out=tmp[:, c0:c1, :], in_=tmp[:, c0:c1, :], func=AF.Ln,
                                                               bias=oneb[:, 0:1], scale=1.0), reads=tmp.s + oneb.s, writes=tmp.s)
                op("dve", lambda h, c0=c0, c1=c1, n=n: h.tensor_tensor(
                    out=gcol[:, c0:c1, :], in0=tmp[:, c0:c1, :], in1=nA[:, None, :].to_broadcast([128, n, 8]), op=OP.mult),
                   reads=tmp.s + nA.s, writes=gcol.s)
                op("act", lambda h, pav=pav, c0=c0, c1=c1: h.activation(out=bcol[:, c0:c1, :], in_=pav[:, :, 8:16], func=AF.Exp, scale=-1.0),
                   reads=pa.s, writes=bcol.s)
                op("dve", lambda h, c0=c0, c1=c1: h.tensor_scalar(out=bcol[:, c0:c1, :], in0=bcol[:, c0:c1, :], scalar1=1.0, scalar2=None, op0=OP.add),
                   reads=bcol.s, writes=bcol.s)
                op("dve", lambda h, c0=c0, c1=c1: h.reciprocal(out=bcol[:, c0:c1, :], in_=bcol[:, c0:c1, :]), reads=bcol.s, writes=bcol.s)
                fl2 = lambda t_: t_[:, c0:c1, :].rearrange("p c k -> p (c k)")
                pF, pB, pJ = ps(), ps(), ps()
                op("pe", lambda h: h.matmul(pF[:, 0:16], lhsT=cst(C_U), rhs=fl2(gcol), start=True, stop=True), reads=gcol.s + CN.s, writes=pF.s)
                op("pe", lambda h: h.matmul(pB[:, 0:16], lhsT=cst(C_UJ), rhs=fl2(gcol), start=True, stop=True), reads=gcol.s + CN.s, writes=pB.s)
                op("pe", lambda h: h.matmul(pJ[:, 0:16], lhsT=cst(C_J), rhs=fl2(bcol), start=True, stop=True), reads=bcol.s + CN.s, writes=pJ.s)
                v3 = lambda p_: p_[:, 0:16].rearrange("p (c k) -> p c k", k=8)
                op("dve", lambda h: h.tensor_copy(out=GC[:, c0:c1, 0:4], in_=v3(pF)[:, :, 0:4]), reads=pF.s, writes=GC.s)
                op("dve", lambda h: h.tensor_copy(out=GC[:, c0:c1, 4:8], in_=v3(pB)[:, :, 4:8]), reads=pB.s, writes=GC.s)
                op("dve", lambda h: h.tensor_copy(out=BE[:, c0:c1, 0:4], in_=bcol[:, c0:c1, 0:4]), reads=bcol.s, writes=BE.s)
                op("dve", lambda h: h.tensor_copy(out=BE[:, c0:c1, 4:8], in_=v3(pJ)[:, :, 4:8]), reads=pJ.s, writes=BE.s)
                pL = ps()
                op("pe", lambda h: h.matmul(pL[:, 0:16], lhsT=cst(C_EL), rhs=fl2(GC), start=True, stop=True), reads=GC.s + CN.s, writes=pL.s)
                op("act", lambda h: h.activation(out=EG[:, c0:c1, :], in_=GC[:, c0:c1, :], func=AF.Exp), reads=GC.s, writes=EG.s)
                op("act", lambda h: h.activation(out=fl2(EGL), in_=pL[:, 0:16], func=AF.Exp), reads=pL.s, writes=EGL.s)
                op("dve", lambda h: h.tensor_tensor(out=fl2(EKD), in0=pL[:, 0:16], in1=fl2(GC), op=OP.subtract), reads=pL.s + GC.s, writes=EKD.s)
                op("act", lambda h: h.activation(out=EKD[:, c0:c1, :], in_=EKD[:, c0:c1, :], func=AF.Exp), reads=EKD.s, writes=EKD.s)
                op("dve", lambda h: h.tensor_tensor(out=BG[:, c0:c1, :], in0=BE[:, c0:c1, :], in1=EG[:, c0:c1, :], op=OP.mult), reads=BE.s + EG.s, writes=BG.s)

            gfl = dbg.split(":")[2] if (isinstance(dbg, str) and dbg.count(":") >= 2) else ""
            if "1" in gfl:
                return
            for hh in range(1 if "3" in gfl or "2" in gfl else 4):
                with Scope():
                    pre = T(cx, "pre", [128, L], F32)
                    qkv = T(cx, "qkv", [128, 3, L], F32, nslots=3)
                    sqb = T(cx, "sqb", [128, 512], F32)
                    rsb = T(cx, "rsb", [128, 512], F32)
                    for i in range(3):
                        ci = i * 4 + hh
                        r0 = O_QKV + i * 512 + hh * 128
                        dma("sp", pre[:], proj.ap[r0:r0 + 128, t0:t0 + L], pre.s[0], reads=pr, writes=pre.s)
                        dst = qkv[:, i, :]
                        op("act", lambda h, dst=dst, ci=ci: h.activation(out=dst, in_=pre[:], func=AF.Identity, scale=cw[:, ci, 2:3]),
                           reads=pre.s + cw.s, writes=[qkv.s[i]])
                        for (j, so, do, n_) in ((0, 0, 2, L - 2), (1, 0, 1, L - 1), (3, 1, 0, L - 1)):
                            op("dve", lambda h, i=i, j=j, so=so, do=do, n_=n_, ci=ci: h.scalar_tensor_tensor(
                                out=qkv[:, i, do:do + n_], in0=pre[:, so:so + n_], scalar=cw[:, ci, j:j + 1],
                                in1=qkv[:, i, do:do + n_], op0=OP.mult, op1=OP.add),
                               reads=pre.s + cw.s + [qkv.s[i]], writes=[qkv.s[i]])
                        op("act", lambda h, dst=dst: h.activation(out=dst, in_=dst, func=AF.Silu), reads=[qkv.s[i]], writes=[qkv.s[i]])
                        if i < 2:
                            for b0 in range(0, L, 512):
                                bw = min(512, L - b0)
                                p = ps()
                                op("act", lambda h, i=i, b0=b0, bw=bw: h.activation(out=sqb[:, 0:bw], in_=qkv[:, i, b0:b0 + bw], func=AF.Square),
                                   reads=[qkv.s[i]], writes=sqb.s)
                                op("pe", lambda h, p=p, bw=bw: h.matmul(p[:, 0:bw], lhsT=cst(C_ONE), rhs=sqb[:, 0:bw], start=True, stop=True),
                                   reads=sqb.s + CN.s, writes=p.s)
                                op("act", lambda h, p=p, bw=bw: h.activation(out=rsb[:, 0:bw], in_=p[:, 0:bw], func=AF.Sqrt, bias=epsb[:, 0:1], scale=1.0),
                                   reads=p.s + epsb.s, writes=rsb.s)
                                op("dve", lambda h, bw=bw: h.reciprocal(out=rsb[:, 0:bw], in_=rsb[:, 0:bw]), reads=rsb.s, writes=rsb.s)
                                op("dve", lambda h, i=i, b0=b0, bw=bw: h.scalar_tensor_tensor(
                                    out=qkv[:, i, b0:b0 + bw], in0=qkv[:, i, b0:b0 + bw], scalar=(128 ** -0.5 if i == 0 else 1.0),
                                    in1=rsb[:, 0:bw], op0=OP.mult, op1=OP.mult), reads=[qkv.s[i]] + rsb.s, writes=[qkv.s[i]])
                    if "2" in gfl:
                        continue
                    gdn_head(l, t0, L, sidx, hh, qkv, GC, BE, EG, EKD, EGL, BG, gn, (2 if "3" in gfl else nch))

    def gdn_head(l, t0, L, sidx, hh, qkv, GC, BE, EG, EKD, EGL, BG, gn, nch):
        M = lambda name, w=128: T(cx, name, [128, w], F32)
        NAMES = ["S", "qT", "kT", "vT", "qFr", "kFr", "qTr", "kTr", "vTr", "dg", "X1", "dec", "decT", "Lm", "LT",
                 "Pa", "Pb", "PTa", "PTb", "wT", "AT", "qgT", "qgF", "kd", "vn", "osb"]
        WS = [{n_: M(n_ + str(d_)) for n_ in NAMES} for d_ in range(2)]
        for d_ in range(2):
            WS[d_]["B"] = M("B" + str(d_), 256)
        of_, ob_, zc, sz, on, ss, catc = M("of_"), M("ob_"), M("zc"), M("sz"), M("on"), M("ss", 1), M("catc")
        for dr in range(2):
            S = WS[dr]["S"]
            if sidx is None:
                dma("sp", S[:], st_i.ap[l, dr, hh], S.s[0], writes=S.s)
            else:
                op("pool", lambda h: h.memset(S[:], 0.0), writes=S.s)
        for ci in range(nch):
          for dr in range(2):
            W = WS[dr]
            S, qT, kT, vT, qFr, kFr, qTr, kTr, vTr = (W[k_] for k_ in ("S", "qT", "kT", "vT", "qFr", "kFr", "qTr", "kTr", "vTr"))
            dg, X1, dec, decT, Lm, LT, Pa, Pb, PTa, PTb = (W[k_] for k_ in ("dg", "X1", "dec", "decT", "Lm", "LT", "Pa", "Pb", "PTa", "PTb"))
            wT, AT, qgT, qgF, kd, vn, osb, B = (W[k_] for k_ in ("wT", "AT", "qgT", "qgF", "kd", "vn", "osb", "B"))
            gi = dr * 4 + hh
            if True:
                c = ci if dr == 0 else nch - 1 - ci
                cs = slice(c * 128, (c + 1) * 128)
                col = lambda A_: A_[:, c, gi:gi + 1]

                def tr(dst, src_ap, rd):
                    p = ps()
                    op("pe", lambda h: h.transpose(out=p[:, 0:128], in_=src_ap, identity=cst(C_ID)), reads=rd + CN.s, writes=p.s)
                    op("act", lambda h: h.activation(out=dst[:], in_=p[:, 0:128], func=AF.Identity), reads=p.s, writes=dst.s)

                def mm(dst, lhsT, rhs, rd, n=128, eng="act"):
                    p = ps()
                    op("pe", lambda h: h.matmul(p[:, 0:n], lhsT=lhsT, rhs=rhs, start=True, stop=True), reads=rd, writes=p.s)
                    if dst is not None:
                        if eng == "act":
                            op("act", lambda h: h.activation(out=dst[:, 0:n], in_=p[:, 0:n], func=AF.Identity), reads=p.s, writes=dst.s)
                        else:
                            op("dve", lambda h: h.tensor_copy(out=dst[:, 0:n], in_=p[:, 0:n]), reads=p.s, writes=dst.s)
                    return p

                tr(qT, qkv[:, 0, cs], [qkv.s[0]])
                tr(kT, qkv[:, 1, cs], [qkv.s[1]])
                tr(vT, qkv[:, 2, cs], [qkv.s[2]])
                if dr == 0:
                    qF_, kF_, qT_, kT_, vT_ = qkv[:, 0, cs], qkv[:, 1, cs], qT, kT, vT
                    rF = [qkv.s[0], qkv.s[1]]
                else:
                    mm(qFr, qT[:], cst(C_J), qT.s + CN.s)
                    mm(kFr, kT[:], cst(C_J), kT.s + CN.s)
                    mm(qTr, cst(C_J), qT[:], qT.s + CN.s)
                    mm(kTr, cst(C_J), kT[:], kT.s + CN.s)
                    mm(vTr, cst(C_J), vT[:], vT.s + CN.s)
                    qF_, kF_, qT_, kT_, vT_ = qFr[:], kFr[:], qTr, kTr, vTr
                    rF = qFr.s + kFr.s
                op("dve", lambda h: h.tensor_scalar(out=dg[:], in0=cst(C_ID), scalar1=col(GC), scalar2=None, op0=OP.mult),
                   reads=CN.s + GC.s, writes=dg.s)
                prow = ps()
                op("pe", lambda h: h.matmul(prow[:, 0:128], lhsT=cst(C_ONE), rhs=dg[:], start=True, stop=True), reads=dg.s + CN.s, writes=prow.s)
                op("dve", lambda h: h.scalar_tensor_tensor(out=X1[:], in0=prow[:, 0:128], scalar=col(GC), in1=cst(C_MU),
                                                           op0=OP.subtract, op1=OP.add), reads=prow.s + GC.s + CN.s, writes=X1.s)
                op("act", lambda h: h.activation(out=dec[:], in_=X1[:], func=AF.Exp, scale=-1.0), reads=X1.s, writes=dec.s)
                op("dve", lambda h: h.scalar_tensor_tensor(out=X1[:], in0=prow[:, 0:128], scalar=col(GC), in1=cst(C_ML),
                                                           op0=OP.subtract, op1=OP.subtract), reads=prow.s + GC.s + CN.s, writes=X1.s)
                op("act", lambda h: h.activation(out=decT[:], in_=X1[:], func=AF.Exp), reads=X1.s, writes=decT.s)
                pk = ps()
                op("pe", lambda h: h.matmul(pk[:, 0:128], lhsT=kF_, rhs=kF_, start=True, stop=True), reads=rF, writes=pk.s)
                op("dve", lambda h: h.scalar_tensor_tensor(out=Lm[:], in0=pk[:, 0:128], scalar=col(BE), in1=dec[:],
                                                           op0=OP.mult, op1=OP.mult), reads=pk.s + BE.s + dec.s, writes=Lm.s)
                op("pool", lambda h: h.tensor_tensor(out=Lm[:], in0=Lm[:], in1=cst(C_MS), op=OP.mult), reads=Lm.s + CN.s, writes=Lm.s)
                tr(LT, Lm[:], Lm.s)
                op("dve", lambda h: h.tensor_scalar(out=B[:, 0:128], in0=vT_[:], scalar1=col(BE), scalar2=None, op0=OP.mult),
                   reads=vT_.s + BE.s, writes=B.s)
                op("dve", lambda h: h.tensor_scalar(out=B[:, 128:256], in0=kT_[:], scalar1=col(BG), scalar2=None, op0=OP.mult),
                   reads=kT_.s + BG.s, writes=B.s)
                pbb = mm(None, LT[:], B[:], LT.s + B.s, n=256)
                op("dve", lambda h: h.tensor_tensor(out=B[:], in0=B[:], in1=pbb[:, 0:256], op=OP.subtract), reads=B.s + pbb.s, writes=B.s)
                Pc, PTc, Pn, PTn = Lm, LT, Pa, PTa
                for lev in range(6):
                    mm(PTn, Pc[:], PTc[:], Pc.s + PTc.s, eng="dve")
                    if lev < 5:
                        mm(Pn, PTc[:], Pc[:], Pc.s + PTc.s)
                    pbb = mm(None, PTn[:], B[:], PTn.s + B.s, n=256)
                    op("dve", lambda h, pbb=pbb: h.tensor_tensor(out=B[:], in0=B[:], in1=pbb[:, 0:256], op=OP.add), reads=B.s + pbb.s, writes=B.s)
                    Pc, PTc = Pn, PTn
                    Pn, PTn = (Pb, PTb) if Pn is Pa else (Pa, PTa)
                tr(wT, B[:, 128:256], B.s)
                pa_ = ps()
                op("pe", lambda h: h.matmul(pa_[:, 0:128], lhsT=kF_, rhs=qF_, start=True, stop=True), reads=rF, writes=pa_.s)
                op("dve", lambda h: h.tensor_tensor(out=AT[:], in0=pa_[:, 0:128], in1=decT[:], op=OP.mult), reads=pa_.s + decT.s, writes=AT.s)
                op("dve", lambda h: h.tensor_scalar(out=qgT[:], in0=qT_[:], scalar1=col(EG), scalar2=None, op0=OP.mult),
                   reads=qT_.s + EG.s, writes=qgT.s)
                tr(qgF, qgT[:], qgT.s)
                op("pool", lambda h: h.tensor_scalar(out=kd[:], in0=kT_[:], scalar1=col(EKD), scalar2=None, op0=OP.mult),
                   reads=kT_.s + EKD.s, writes=kd.s)
                pw = ps()
                op("pe", lambda h: h.matmul(pw[:, 0:128], lhsT=wT[:], rhs=S[:], start=True, stop=True), reads=wT.s + S.s, writes=pw.s)
                op("dve", lambda h: h.tensor_tensor(out=vn[:], in0=B[:, 0:128], in1=pw[:, 0:128], op=OP.subtract), reads=B.s + pw.s, writes=vn.s)
                po = ps()
                op("pe", lambda h: h.matmul(po[:, 0:128], lhsT=qgF[:], rhs=S[:], start=True, stop=False), reads=qgF.s + S.s, writes=po.s)
                op("pe", lambda h: h.matmul(po[:, 0:128], lhsT=AT[:], rhs=vn[:], start=False, stop=True), reads=AT.s + vn.s, writes=po.s)
                op("act", lambda h: h.activation(out=osb[:], in_=po[:, 0:128], func=AF.Identity), reads=po.s, writes=osb.s)
                pS = ps()
                op("pe", lambda h: h.matmul(pS[:, 0:128], lhsT=kd[:], rhs=vn[:], start=True, stop=True), reads=kd.s + vn.s, writes=pS.s)
                op("dve", lambda h: h.scalar_tensor_tensor(out=S[:], in0=S[:], scalar=col(EGL), in1=pS[:, 0:128],
                                                           op0=OP.mult, op1=OP.add), reads=S.s + EGL.s + pS.s, writes=S.s)
                dst_dr = ofs if dr == 0 else obs
                dma("sp", dst_dr.ap[c * 128:(c + 1) * 128, :], osb[:], osb.s[0], reads=osb.s, writes=dst_dr.s)
        if sidx is not None:
            for dr in range(2):
                S = WS[dr]["S"]
                dma("sp", st_o.ap[sidx, l, dr, hh], S[:], S.s[0], reads=S.s, writes=st_o.s)
        for c in range(nch):
            dma("sp", of_[:], ofs.ap[c * 128:(c + 1) * 128, :], of_.s[0], reads=ofs.s, writes=of_.s)
            dma("sp", ob_[:], obs.ap[c * 128:(c + 1) * 128, :], ob_.s[0], reads=obs.s, writes=ob_.s)
            r0 = O_Z + hh * 128
            dma("sp", zc[:], proj.ap[r0:r0 + 128, t0 + c * 128:t0 + (c + 1) * 128], zc.s[0], reads=tiles_of(t0, L), writes=zc.s)
            pj = ps()
            op("pe", lambda h: h.matmul(pj[:, 0:128], lhsT=cst(C_J), rhs=ob_[:], start=True, stop=True), reads=ob_.s + CN.s, writes=pj.s)
            op("dve", lambda h: h.tensor_tensor(out=of_[:], in0=of_[:], in1=pj[:, 0:128], op=OP.add), reads=of_.s + pj.s, writes=of_.s)
            op("act", lambda h: h.activation(out=on[:], in_=of_[:], func=AF.Square, accum_out=ss[:, 0:1]), reads=of_.s, writes=on.s + ss.s)
            op("act", lambda h: h.activation(out=ss[:], in_=ss[:], func=AF.Sqrt, bias=epsb[:, 0:1], scale=1.0 / 128), reads=ss.s + epsb.s, writes=ss.s)
            op("dve", lambda h: h.reciprocal(out=ss[:], in_=ss[:]), reads=ss.s, writes=ss.s)
            op("dve", lambda h: h.scalar_tensor_tensor(out=on[:], in0=of_[:], scalar=ss[:, 0:1], in1=gn[:], op0=OP.mult, op1=OP.mult),
               reads=of_.s + ss.s + gn.s, writes=on.s)
            op("act", lambda h: h.activation(out=sz[:], in_=zc[:], func=AF.Silu), reads=zc.s, writes=sz.s)
            pt = ps()
            op("pe", lambda h: h.transpose(out=pt[:, 0:128], in_=on[:], identity=cst(C_ID)), reads=on.s + CN.s, writes=pt.s)
            op("dve", lambda h: h.tensor_tensor(out=catc[:], in0=pt[:, 0:128], in1=sz[:], op=OP.mult), reads=pt.s + sz.s, writes=catc.s)
            dma("sp", cat.ap[hh * 128:(hh + 1) * 128, t0 + c * 128:t0 + (c + 1) * 128], catc[:], catc.s[0], reads=catc.s,
                writes=cat_slots(t0, L))

    def attn_seq(l, t0, L, sidx):
        sample = sidx is None
        Lk = L + (PAST if sample else 0)
        nkc = Lk // 128
        QB = min(512, L)
        nsub = QB // 128
        lam_init = 0.8 - 0.6 * float(np.exp(-0.3 * l))
        pr = tiles_of(t0, L)
        with Scope():
            dl = T(cx, "dl", [128, 4, 64], F32)
            dn = T(cx, "dn", [128, 128], F32)
            lam = T(cx, "lam", [128, 4], F32)
            ldump = T(cx, "ldump", [128, 64], F32)
            stg = T(cx, "stg", [128, L], F32)
            stg2 = T(cx, "stg2", [128, 512], F32)
            stg3 = T(cx, "stg3", [128, 512], F32)
            QF = T(cx, "QF", [64, 2, L], BF16, nslots=2)
            KF = T(cx, "KF", [64, 2, Lk], BF16, nslots=2)
            V1 = T(cx, "V1", [128, nkc, 132], BF16)
            PT = T(cx, "PT", [128, 2, 512], BF16, nslots=2)
            ob = T(cx, "ob", [128, 2, 128], F32, nslots=2)
            rs = T(cx, "rs", [128, 2], F32)
            on = T(cx, "onA", [128, 128], F32)
            ss = T(cx, "ssA", [128, 1], F32)
            catc = T(cx, "catcA", [128, 128], F32)
            rp = T(cx, "rp", [64, 2, LS if sample else 2], F32)
            rR = T(cx, "rR", [64, 64], F32)
            dma("sp", dl[:], dlam.ap[l], dl.s[0], writes=dl.s)
            dma("sp", dn[:], gnorm.ap[l, :, 1, :], dn.s[0], writes=dn.s)
            if sample:
                dma("sp", rp[:], ropeS.ap[:, :, :], rp.s[0], writes=rp.s)
                dma("sp", rR[:], ropeR.ap[:, :], rR.s[0], writes=rR.s)
            for i in range(2):
                op("dve", lambda h, i=i: h.tensor_tensor(out=ldump[:], in0=dl[:, 2 * i, :], in1=dl[:, 2 * i + 1, :], op=OP.mult),
                   reads=dl.s, writes=ldump.s)
                op("dve", lambda h, i=i: h.tensor_reduce(out=lam[:, i:i + 1], in_=ldump[:], axis=mybir.AxisListType.X, op=OP.add),
                   reads=ldump.s, writes=lam.s)
            op("act", lambda h: h.activation(out=lam[:, 0:2], in_=lam[:, 0:2], func=AF.Exp), reads=lam.s, writes=lam.s)
            op("dve", lambda h: h.tensor_tensor(out=lam[:, 2:3], in0=lam[:, 1:2], in1=lam[:, 0:1], op=OP.subtract), reads=lam.s, writes=lam.s)
            op("dve", lambda h: h.tensor_scalar(out=lam[:, 2:3], in0=lam[:, 2:3], scalar1=-lam_init, scalar2=None, op0=OP.add), reads=lam.s, writes=lam.s)
            rot["lo"], rot["n"], rot["i"] = 4, 4, 0
            ACC = PS[0:4]
            for hd in range(8):
                for m in range(2):
                    for (which, O_, dst, off) in (("q", O_DQ, QF, 0), ("k", O_DK, KF, Lk - L)):
                        r0 = O_ + hd * 128 + m * 64
                        dma("sp", stg[0:64, :], proj.ap[r0:r0 + 64, t0:t0 + L], stg.s[0], reads=pr, writes=stg.s)
                        if not sample:
                            op("dve", lambda h, dst=dst, m=m, off=off: h.tensor_copy(out=dst[:, m, off:off + L], in_=stg[0:64, :]),
                               reads=stg.s, writes=[dst.s[m]])
                        else:
                            for b0 in range(0, L, 512):
                                p = ps()
                                op("pe", lambda h, p=p, b0=b0: h.matmul(p[0:64, :], lhsT=rR[:], rhs=stg[0:64, b0:b0 + 512], start=True, stop=True),
                                   reads=stg.s + rR.s, writes=p.s)
                                op("pool", lambda h, b0=b0: h.tensor_tensor(out=stg2[0:64, :], in0=stg[0:64, b0:b0 + 512], in1=rp[:, 0, b0:b0 + 512], op=OP.mult),
                                   reads=stg.s + rp.s, writes=stg2.s)
                                op("dve", lambda h, p=p, b0=b0: h.tensor_tensor(out=stg3[0:64, :], in0=p[0:64, :], in1=rp[:, 1, b0:b0 + 512], op=OP.mult),
                                   reads=p.s + rp.s, writes=stg3.s)
                                op("dve", lambda h, dst=dst, m=m, off=off, b0=b0: h.tensor_tensor(
                                    out=dst[:, m, off + b0:off + b0 + 512], in0=stg2[0:64, :], in1=stg3[0:64, :], op=OP.add),
                                   reads=stg2.s + stg3.s, writes=[dst.s[m]])
                    if sample:
                        dma("sp", stg2[0:64, 0:PAST], ckT.ap[l, hd, m], stg2.s[0], writes=stg2.s)
                        op("dve", lambda h, m=m: h.tensor_copy(out=KF[:, m, 0:PAST], in_=stg2[0:64, 0:PAST]), reads=stg2.s, writes=[KF.s[m]])
                r0 = O_DV + hd * 128
                dma("sp", stg[:, :], proj.ap[r0:r0 + 128, t0:t0 + L], stg.s[0], reads=pr, writes=stg.s)
                op("pool", lambda h: h.memset(V1[:, :, 128:132], 1.0), writes=V1.s)
                koff = (PAST // 128) if sample else 0
                if sample:
                    dma("sp", stg2[:, 0:512].rearrange("p (c d) -> p c d", d=128),
                        cvT.ap[l, :, hd, :].rearrange("(c p) d -> p c d", p=128), stg2.s[0], writes=stg2.s)
                    op("dve", lambda h: h.tensor_copy(out=V1[:, 0:4, 0:128], in_=stg2[:, 0:512].rearrange("p (c d) -> p c d", d=128)),
                       reads=stg2.s, writes=V1.s)
                for c in range(L // 128):
                    p = ps()
                    op("pe", lambda h, p=p, c=c: h.transpose(out=p[:, 0:128], in_=stg[:, c * 128:(c + 1) * 128], identity=cst(C_ID)),
                       reads=stg.s + CN.s, writes=p.s)
                    op("act", lambda h, p=p, c=c: h.activation(out=V1[:, koff + c, 0:128], in_=p[:, 0:128], func=AF.Identity), reads=p.s, writes=V1.s)
                for qb in range(L // QB):
                    for s in range(nsub):
                        for m in range(2):
                            pass
                    for m in range(2):
                        for kc in range(nkc):
                            p = ps()
                            sl = kc % 2
                            op("pe", lambda h, p=p, kc=kc, m=m: h.matmul(p[:, 0:QB], lhsT=KF[:, m, kc * 128:(kc + 1) * 128],
                                                                         rhs=QF[:, m, qb * QB:(qb + 1) * QB], start=True, stop=True),
                               reads=[KF.s[m], QF.s[m]], writes=p.s)
                            op("act", lambda h, p=p, sl=sl: h.activation(out=PT[:, sl, 0:QB], in_=p[:, 0:QB], func=AF.Exp,
                                                                         bias=shb[:, 0:1], scale=0.125), reads=p.s + shb.s, writes=[PT.s[sl]])
                            for s in range(nsub):
                                op("pe", lambda h, s=s, sl=sl, kc=kc: h.matmul(ACC[s][:, 0:129], lhsT=PT[:, sl, s * 128:(s + 1) * 128],
                                                                              rhs=V1[:, kc, 0:129], start=(kc == 0), stop=(kc == nkc - 1)),
                                   reads=[PT.s[sl]] + V1.s, writes=ACC[s].s)
                        for s in range(nsub):
                            op("dve", lambda h, s=s, m=m: h.reciprocal(out=rs[:, m:m + 1], in_=ACC[s][:, 128:129]), reads=ACC[s].s, writes=rs.s)
                            if m == 0:
                                op("dve", lambda h, s=s: h.tensor_scalar(out=obq[:, s, :], in0=ACC[s][:, 0:128], scalar1=rs[:, 0:1], scalar2=None, op0=OP.mult),
                                   reads=ACC[s].s + rs.s, writes=[obq.s[s]])
                            else:
                                op("dve", lambda h, s=s: h.tensor_scalar(out=ob[:, 1, :], in0=ACC[s][:, 0:128], scalar1=rs[:, 1:2], scalar2=lam[:, 2:3],
                                                                         op0=OP.mult, op1=OP.mult), reads=ACC[s].s + rs.s + lam.s, writes=[ob.s[1]])
                                op("dve", lambda h, s=s: h.tensor_tensor(out=ob[:, 0, :], in0=obq[:, s, :], in1=ob[:, 1, :], op=OP.add),
                                   reads=[obq.s[s], ob.s[1]], writes=[ob.s[0]])
                                op("act", lambda h: h.activation(out=on[:], in_=ob[:, 0, :], func=AF.Square, accum_out=ss[:, 0:1]), reads=[ob.s[0]], writes=on.s + ss.s)
                                op("act", lambda h: h.activation(out=ss[:], in_=ss[:], func=AF.Sqrt, bias=epsb[:, 0:1], scale=1.0 / 128), reads=ss.s + epsb.s, writes=ss.s)
                                op("dve", lambda h: h.reciprocal(out=ss[:], in_=ss[:]), reads=ss.s, writes=ss.s)
                                op("dve", lambda h: h.scalar_tensor_tensor(out=on[:], in0=ob[:, 0, :], scalar=ss[:, 0:1], in1=dn[:], op0=OP.mult, op1=OP.mult),
                                   reads=[ob.s[0]] + ss.s + dn.s, writes=on.s)
                                pt = ps()
                                op("pe", lambda h, pt=pt: h.transpose(out=pt[:, 0:128], in_=on[:], identity=cst(C_ID)), reads=on.s + CN.s, writes=pt.s)
                                op("act", lambda h, pt=pt: h.activation(out=catc[:], in_=pt[:, 0:128], func=AF.Identity, scale=(1.0 - lam_init)),
                                   reads=pt.s, writes=catc.s)
                                tq = t0 + qb * QB + s * 128
                                dma("sp", cat.ap[512 + hd * 128:512 + (hd + 1) * 128, tq:tq + 128], catc[:], catc.s[0], reads=catc.s,
                                    writes=cat_slots(t0, L))
            rot["lo"], rot["n"], rot["i"] = 0, 8, 0

    obq = None

    def pool_seq(l, t0, L):
        pr = tiles_of(t0, L)
        W = L + 16
        with Scope():
            xp = T(cx, "xp", [128, W], F32)
            wa = T(cx, "wa", [128, W], F32)
            wbf = T(cx, "wbf", [128, W], F32)
            pw = T(cx, "pw", [128, 4, 128], F32)
            psc = T(cx, "psc", [128, 4], F32)
            ped = T(cx, "ped", [128, 4, 16], F32)
            oc_ = T(cx, "oc_", [128, 2, 512], F32, nslots=2)
            dma("sp", pw[:], poolw.ap[l], pw.s[0], writes=pw.s)
            dma("sp", psc[:], pscale.ap[l], psc.s[0], writes=psc.s)
            dma("sp", ped[:], pedge.ap[:, :, :], ped.s[0], writes=ped.s)
            for g in range(4):
                w = 2 << g
                r0 = O_PIN + g * 128
                op("pool", lambda h: h.memset(xp[:], 0.0), writes=xp.s)
                op("pool", lambda h: h.memset(wa[:], 0.0), writes=wa.s)
                op("pool", lambda h: h.memset(wbf[:], 0.0), writes=wbf.s)
                dma("sp", xp[:, 8:8 + L], proj.ap[r0:r0 + 128, t0:t0 + L], xp.s[0], reads=pr, writes=xp.s)
                op("dve", lambda h: h.tensor_tensor(out=wa[:, 1:W], in0=xp[:, 0:W - 1], in1=xp[:, 1:W], op=OP.add), reads=xp.s, writes=wa.s)
                cur, nxt = wa, wbf
                sh = 1
                for lev in range(g):
                    lo, hi = 2 * sh, W - 2 * sh
                    op("dve", lambda h, cur=cur, nxt=nxt, lo=lo, hi=hi, sh=sh: h.tensor_tensor(
                        out=nxt[:, lo:hi], in0=cur[:, lo - sh:hi - sh], in1=cur[:, lo + sh:hi + sh], op=OP.add),
                       reads=cur.s, writes=nxt.s)
                    cur, nxt = nxt, cur
                    sh *= 2
                op("dve", lambda h, cur=cur, g=g: h.tensor_tensor(out=cur[:, 8:16], in0=cur[:, 8:16], in1=ped[:, g, 0:8], op=OP.mult),
                   reads=cur.s + ped.s, writes=cur.s)
                op("dve", lambda h, cur=cur, g=g: h.tensor_tensor(out=cur[:, L:L + 8], in0=cur[:, L:L + 8], in1=ped[:, g, 8:16], op=OP.mult),
                   reads=cur.s + ped.s, writes=cur.s)
                op("dve", lambda h, cur=cur, w=w: h.scalar_tensor_tensor(out=cur[:, 8:8 + L], in0=cur[:, 8:8 + L], scalar=1.0 / w, in1=xp[:, 8:8 + L],
                                                                        op0=OP.mult, op1=OP.subtract), reads=cur.s + xp.s, writes=cur.s)
                for b0 in range(0, L, 512):
                    bw = min(512, L - b0)
                    sl = (b0 // 512) % 2
                    p = ps()
                    op("pe", lambda h, p=p, cur=cur, b0=b0, bw=bw, g=g: h.matmul(p[:, 0:bw], lhsT=pw[:, g, :], rhs=cur[:, 8 + b0:8 + b0 + bw], start=True, stop=True),
                       reads=cur.s + pw.s, writes=p.s)
                    op("act", lambda h, p=p, sl=sl, bw=bw, g=g: h.activation(out=oc_[:, sl, 0:bw], in_=p[:, 0:bw], func=AF.Identity, scale=psc[:, g:g + 1]),
                       reads=p.s + psc.s, writes=[oc_.s[sl]])
                    dma("sp", cat.ap[1536 + g * 128:1536 + (g + 1) * 128, t0 + b0:t0 + b0 + bw], oc_[:, sl, 0:bw], oc_.s[sl], reads=[oc_.s[sl]],
                        writes=cat_slots(t0, L))

    SEQS = [(0, LP, 0), (LP, LP, 1), (NPS * LP, LS, None)]

    def mixer(l):
        nonlocal obq
        fl = dbg.split(":")[1] if mx else "gapPS"
        for (t0, L, sidx) in SEQS:
            if (sidx is None and "S" not in fl) or (sidx is not None and "P" not in fl):
                continue
            if "g" in fl:
                gdn_seq(l, t0, L, sidx)
            if "a" in fl:
                with Scope():
                    obq = T(cx, "obq", [128, 4, 128], F32, nslots=4)
                    attn_seq(l, t0, L, sidx)
            if "p" in fl:
                pool_seq(l, t0, L)

    stage = dbg or "full"
    if mx:
        mixer(0)
        cx.finish(yT.s + ck_o.s + cv_o.s + st_o.s + xs.s + proj.s + cat.s + ofs.s + obs.s + wsc.s)
        return
    ada_layer(0)
    token_phase(0, False, True, True, False)
    if stage != "A0":
        mixer(0)
        if stage == "C0":
            token_phase(1, True, False, False, False)
        elif stage != "M0":
            for l in range(1, DEPTH):
                token_phase(l, True, False, False, False)
                ada_layer(l)
                token_phase(l, False, True, False, False)
                mixer(l)
            token_phase(DEPTH, True, False, False, True)
    cx.finish(yT.s + ck_o.s + cv_o.s + st_o.s + xs.s + proj.s + cat.s + ofs.s + obs.s + wsc.s)


def _consts():
    i = np.arange(128)
    C = np.zeros((NCONST, 128, 128), np.float32)
    C[C_ID] = np.eye(128)
    C[C_ONE] = 1.0
    C[C_U] = (i[:, None] <= i[None, :])
    C[C_UJ] = ((127 - i[:, None]) <= i[None, :])
    C[C_J] = (i[:, None] + i[None, :] == 127)
    C[C_EL] = (i[:, None] == 127)
    C[C_MS] = (i[None, :] < i[:, None])
    C[C_MU] = MASKV * (i[None, :] > i[:, None])
    C[C_ML] = MASKV * (i[None, :] < i[:, None])
    return np.ascontiguousarray(C.transpose(1, 0, 2))


def _rope():
    t = np.arange(LS)
    pos = np.stack([t // 64, t % 64]).astype(np.float32)
    half = 32
    inv = (10000.0 ** (-np.arange(0, half, 2, dtype=np.float32) / half)).astype(np.float32)
    tab = np.zeros((64, 2, LS), np.float32)
    for d in range(64):
        ang = pos[d // 32] * inv[(d % 32) % 16]
        tab[d, 0] = np.cos(ang)
        tab[d, 1] = np.sin(ang)
    R = np.zeros((64, 64), np.float32)
    for base in (0, 32):
        for k in range(16):
            R[base + k, base + k + 16] = -1.0
            R[base + k + 16, base + k] = 1.0
    return tab, np.ascontiguousarray(R.T)


def _pedge():
    E = np.ones((128, 4, 16), np.float32)
    for g, w in enumerate((2, 4, 8, 16)):
        a = w // 2
        b = w - a - 1
        for t in range(8):
            cnt = (t + b + 1) - max(t - a, 0)
            E[:, g, t] = w / cnt
        for r in range(8):
            d = 7 - r
            cnt = min(b, d) + 1 + a
            E[:, g, 8 + r] = w / cnt
    return E


def _fm(v):
    v = np.asarray(v, np.float32)
    n = v.shape[-1] // 128
    r = v.reshape(v.shape[:-1] + (n, 128))
    return np.ascontiguousarray(np.moveaxis(r, -1, 0))


_CACHE = {}
_DBG = {}


def kernel(x_prompt, x_sample, c, cache_k, cache_v, state_gdn, c_ctx, w_ada, b_ada, norm_ffn1, ffn1_in,
           ffn1_out, norm_mix, w_in, gdn_conv, gdn_a_log, gdn_dt_bias, gdn_norm, diff_lam, diff_norm,
           pool_w, pool_scale, w_out, norm_ffn2, ffn2_in, ffn2_out, final_norm, _dbg=None):
    f = lambda a: np.ascontiguousarray(np.asarray(a, np.float32))
    key = _dbg or "full"
    if key not in _CACHE:
        _CACHE[key] = build_program(_dbg)
    nc = _CACHE[key]
    x_prompt, x_sample = f(x_prompt), f(x_sample)
    rope_tab, rope_R = _rope()
    rep = lambda v: np.ascontiguousarray(np.broadcast_to(np.asarray(v, np.float32)[None], (128,) + tuple(np.shape(v))))
    norms = np.stack([_fm(np.stack([norm_ffn1[l], norm_mix[l], norm_ffn2[l]])) for l in range(DEPTH)], 1)
    norms = np.concatenate([norms.reshape(128, DEPTH * 3, KC), _fm(final_norm)[:, None, :]], 1)
    shared = {
        "w_ada": f(w_ada), "b_ada": np.ascontiguousarray(_fm(b_ada).transpose(1, 0, 2)),
        "norms": np.ascontiguousarray(norms),
        "ffn1_in": f(ffn1_in), "ffn2_in": f(ffn2_in), "ffn1_out": f(ffn1_out), "ffn2_out": f(ffn2_out),
        "w_in": f(w_in), "w_out": f(w_out), "consts": _consts(),
        "convw": np.ascontiguousarray(f(gdn_conv).reshape(DEPTH, 4, 12, 128).transpose(0, 3, 2, 1)),
        "gpar": np.ascontiguousarray(np.stack([rep(np.stack([f(gdn_a_log)[l].reshape(8), f(gdn_dt_bias)[l].reshape(8)])) for l in range(DEPTH)])),
        "gnorm": np.ascontiguousarray(np.stack([rep(np.stack([f(gdn_norm)[l], f(diff_norm)[l]])) for l in range(DEPTH)])),
        "dlam": np.ascontiguousarray(np.stack([rep(f(diff_lam)[l]) for l in range(DEPTH)])),
        "poolw": np.ascontiguousarray(f(pool_w).transpose(0, 2, 1, 3)),
        "pscale": np.ascontiguousarray(f(pool_scale).reshape(DEPTH, 4, 128).transpose(0, 2, 1)),
        "pedge": _pedge(), "ropeS": rope_tab, "ropeR": rope_R,
    }
    mx = isinstance(_dbg, str) and _dbg.startswith("MX")
    if mx:
        for k_ in ("w_ada", "ffn1_in", "ffn2_in", "ffn1_out", "ffn2_out", "w_in", "w_out"):
            shared[k_] = np.zeros((1, 1, 1), np.float32)
        shared["proj"] = _DBG["proj"]
    in_maps = []
    for r in range(8):
        b = r // 4
        xcat = np.concatenate([x_prompt[2 * r], x_prompt[2 * r + 1], x_sample[b]], 0)
        m = dict(shared)
        m["xT"] = np.ascontiguousarray(xcat.T)
        m["cT"] = np.ascontiguousarray(np.stack([_fm(c_ctx), _fm(f(c)[b])], -1))
        ck = f(cache_k)[b].reshape(DEPTH, PAST, 8, 2, 64)
        m["ckT"] = np.ascontiguousarray(ck.transpose(0, 2, 3, 4, 1))
        m["cvT"] = f(cache_v)[b]
        m["st_i"] = f(state_gdn)[b]
        in_maps.append(m)
    res = run_bass_kernel_spmd(nc, in_maps, core_ids=list(range(8))).results
    if mx:
        _DBG["cat"] = res[0]["cat"]
    y_prompt = np.zeros((16, LP, D), np.float32)
    y_sample = np.zeros((2, LS, D), np.float32)
    nck = np.zeros((16, DEPTH, LP, 8, 128), np.float32)
    ncv = np.zeros((16, DEPTH, LP, 8, 128), np.float32)
    nst = np.zeros((16, DEPTH, 2, 4, 128, 128), np.float32)
    for r in range(8):
        o = res[r]
        yt = o["yT"]
        for s in range(NPS):
            y_prompt[2 * r + s] = yt[:, s * LP:(s + 1) * LP].T
            nck[2 * r + s] = o["ck_o"][:, :, s * LP:(s + 1) * LP].reshape(DEPTH, 8, 128, LP).transpose(0, 3, 1, 2)
            ncv[2 * r + s] = o["cv_o"][:, :, s * LP:(s + 1) * LP].reshape(DEPTH, 8, 128, LP).transpose(0, 3, 1, 2)
            nst[2 * r + s] = o["st_o"][s]
        if r % 4 == 0:
            y_sample[r // 4] = yt[:, NPS * LP:].T
    return (y_prompt, y_sample, nck, ncv, nst)
```
